# Optimizing a Trainium2 kernel written in Bass

```python
import math
import jax, jax.numpy as jnp
from jax import lax
import numpy as np

D_MODEL = 1024
BATCH = 32
SEQ = 256
DEPTH = 4
DEC_BATCH = 8
DEC_SEQ = 4096
PAST_LEN = 256

GRID_W = 64
N_HEADS = 8
N_KV_HEADS = 2
HEAD_DIM = 64
ATTN_W = N_HEADS * HEAD_DIM
KV_W = N_KV_HEADS * HEAD_DIM
HYENA_W = 256
CONV_W = 256
MIX_W = ATTN_W + HYENA_W + CONV_W
WINDOW = 128
BLOCK = 128
HYENA_ORDER = 2
FILTER_EMB = 33
FILTER_HIDDEN = 64
HYENA_TARGET = 1e-2
FAST_DECAY_PCT = 0.3
SLOW_DECAY_PCT = 1.5
ROPE_BASE = 10000.0
RMS_EPS = 1e-6
NEG_INF = -1e30
IN_COLS = ATTN_W + 2 * KV_W + ATTN_W + 3 * HYENA_W + HYENA_W + 3 * CONV_W + CONV_W
SPLITS = (ATTN_W,
          ATTN_W + KV_W,
          ATTN_W + 2 * KV_W,
          2 * ATTN_W + 2 * KV_W,
          2 * ATTN_W + 2 * KV_W + 3 * HYENA_W,
          2 * ATTN_W + 2 * KV_W + 4 * HYENA_W,
          2 * ATTN_W + 2 * KV_W + 4 * HYENA_W + 3 * CONV_W)

kernel_name = 'hybrid_diffusion_parallel_heads_step'


def rms_norm(x, g):
    xf = x.astype(jnp.float32)
    y = xf * lax.rsqrt(jnp.mean(xf * xf, axis=-1, keepdims=True) + RMS_EPS)
    return (y * g.astype(jnp.float32)).astype(x.dtype)


def dwconv3(u, w):
    L = u.shape[1]
    up = jnp.pad(u, ((0, 0), (1, 1), (0, 0)))
    return up[:, :L] * w[0] + up[:, 1:L + 1] * w[1] + up[:, 2:] * w[2]


def axial_rope(L, dtype):
    t = jnp.arange(L)
    n_freq = HEAD_DIM // 4
    inv = ROPE_BASE ** (-jnp.arange(n_freq, dtype=jnp.float32) / n_freq)
    row = (t // GRID_W).astype(jnp.float32)[:, None] * inv
    col = (t % GRID_W).astype(jnp.float32)[:, None] * inv
    ang = jnp.stack([row, col], axis=1)
    return jnp.cos(ang).astype(dtype), jnp.sin(ang).astype(dtype)


def apply_rope(x, cos, sin):
    xs = x.reshape(x.shape[:-1] + (2, 2, HEAD_DIM // 4))
    xa, xb = xs[..., 0, :], xs[..., 1, :]
    c = cos[:, None]
    s = sin[:, None]
    out = jnp.stack([xa * c - xb * s, xb * c + xa * s], axis=-2)
    return out.reshape(x.shape)


def context_attention(q, k, v, sink):
    B, Lc = q.shape[:2]
    G = N_HEADS // N_KV_HEADS
    nb = Lc // BLOCK
    scale = HEAD_DIM ** -0.5
    sink_logit = sink.astype(jnp.float32).reshape(1, N_KV_HEADS, G, 1, 1)

    def block(n):
        qb = lax.dynamic_slice_in_dim(q, n * BLOCK, BLOCK, 1).reshape(B, BLOCK, N_KV_HEADS, G, HEAD_DIM)
        s = jnp.einsum('bqhgd,bkhd->bhgqk', qb, k).astype(jnp.float32) * scale
        s_snk = jnp.broadcast_to(sink_logit, s.shape[:-1] + (1,))
        p = jax.nn.softmax(jnp.concatenate([s, s_snk], axis=-1), axis=-1)[..., :Lc].astype(v.dtype)
        o = jnp.einsum('bhgqk,bkhd->bqhgd', p, v)
        return o.reshape(B, BLOCK, ATTN_W)

    out = lax.map(block, jnp.arange(nb))
    return jnp.moveaxis(out, 0, 1).reshape(B, Lc, ATTN_W)


def latent_attention(q_rot, q_raw, k_rot, v, ck, cv, sink):
    B, L = q_rot.shape[:2]
    Lc = ck.shape[1]
    G = N_HEADS // N_KV_HEADS
    nb = L // BLOCK
    scale = HEAD_DIM ** -0.5
    pad = ((0, 0), (BLOCK, BLOCK), (0, 0), (0, 0))
    kp = jnp.pad(k_rot, pad)
    vp = jnp.pad(v, pad)
    qi = jnp.arange(BLOCK)[:, None]
    kj = jnp.arange(3 * BLOCK)[None, :]
    in_window = jnp.abs(kj - BLOCK - qi) <= WINDOW
    sink_logit = sink.astype(jnp.float32).reshape(1, N_KV_HEADS, G, 1, 1)

    def block(n):
        start = n * BLOCK
        qb = lax.dynamic_slice_in_dim(q_rot, start, BLOCK, 1).reshape(B, BLOCK, N_KV_HEADS, G, HEAD_DIM)
        q0 = lax.dynamic_slice_in_dim(q_raw, start, BLOCK, 1).reshape(B, BLOCK, N_KV_HEADS, G, HEAD_DIM)
        kb = lax.dynamic_slice_in_dim(kp, start, 3 * BLOCK, 1)
        vb = lax.dynamic_slice_in_dim(vp, start, 3 * BLOCK, 1)
        kpos = start - BLOCK + kj
        mask = in_window & (kpos >= 0) & (kpos < L)
        s_loc = jnp.einsum('bqhgd,bkhd->bhgqk', qb, kb).astype(jnp.float32) * scale
        s_loc = jnp.where(mask, s_loc, NEG_INF)
        s_ctx = jnp.einsum('bqhgd,bchd->bhgqc', q0, ck).astype(jnp.float32) * scale
        s_snk = jnp.broadcast_to(sink_logit, s_loc.shape[:-1] + (1,))
        p = jax.nn.softmax(jnp.concatenate([s_loc, s_ctx, s_snk], axis=-1), axis=-1).astype(v.dtype)
        o = (jnp.einsum('bhgqk,bkhd->bqhgd', p[..., :3 * BLOCK], vb)
             + jnp.einsum('bhgqc,bchd->bqhgd', p[..., 3 * BLOCK:3 * BLOCK + Lc], cv))
        return o.reshape(B, BLOCK, ATTN_W)

    out = lax.map(block, jnp.arange(nb))
    return jnp.moveaxis(out, 0, 1).reshape(B, L, ATTN_W)


def hyena_filters(L, w1, b1, w2, b2, w3, freq):
    f32 = jnp.float32
    t = jnp.linspace(0.0, 1.0, L, dtype=f32)[:, None]
    bands = (FILTER_EMB - 1) // 2
    ang = (2.0 * math.pi / L) * jnp.arange(L, dtype=f32)[:, None]
    fr = jnp.linspace(1e-4, bands - 1, bands, dtype=f32)[None, :]
    feats = jnp.concatenate([t, jnp.cos(fr * ang), -jnp.sin(fr * ang)], axis=-1)
    fq = freq.astype(f32)
    h = jnp.sin(fq * (feats @ w1.astype(f32) + b1.astype(f32)))
    h = jnp.sin(fq * (h @ w2.astype(f32) + b2.astype(f32)))
    h = (h @ w3.astype(f32)).reshape(L, 2, HYENA_ORDER, HYENA_W)
    max_decay = math.log(HYENA_TARGET) / FAST_DECAY_PCT
    min_decay = math.log(HYENA_TARGET) / SLOW_DECAY_PCT
    deltas = jnp.abs(jnp.linspace(min_decay, max_decay, HYENA_W, dtype=f32))
    h = h * jnp.exp(-t * deltas)[:, None, None, :]
    fwd, bwd = h[:, 0], h[:, 1]
    k = jnp.concatenate([fwd, jnp.zeros_like(fwd[:1]), jnp.flip(bwd[1:], axis=0)], axis=0)
    k = k / jnp.sum(jnp.abs(k), axis=0, keepdims=True)
    return jnp.fft.rfft(k, axis=0)


def long_conv(u, kf, d):
    L = u.shape[1]
    uf32 = u.astype(jnp.float32)
    uf = jnp.fft.rfft(uf32, n=2 * L, axis=1)
    y = jnp.fft.irfft(uf * kf[None], n=2 * L, axis=1)[:, :L]
    return (y + uf32 * d.astype(jnp.float32)).astype(u.dtype)


def trunk_layer(x, mod, ck, cv, norm_g, w_in, attn_sink, hy_conv_w, hy_filt_w1, hy_filt_b1,
                hy_filt_w2, hy_filt_b2, hy_filt_w3, hy_filt_freq, hy_d, sc_conv_w, w_out):
    B, L = x.shape[:2]
    shift, scale, gate = jnp.split(mod, 3, axis=-1)
    h = rms_norm(x, norm_g) * (1.0 + scale[:, None]) + shift[:, None]
    proj = h @ w_in
    q, k, v, g_a, hy, g_h, sc, g_c = jnp.split(proj, SPLITS, axis=-1)
    q = q.reshape(B, L, N_HEADS, HEAD_DIM)
    k = k.reshape(B, L, N_KV_HEADS, HEAD_DIM)
    v = v.reshape(B, L, N_KV_HEADS, HEAD_DIM)
    if ck is None:
        attn = context_attention(q, k, v, attn_sink)
    else:
        cos, sin = axial_rope(L, q.dtype)
        attn = latent_attention(apply_rope(q, cos, sin), q, apply_rope(k, cos, sin), v, ck, cv, attn_sink)
    hy = dwconv3(hy, hy_conv_w)
    hv, hx1, hx2 = jnp.split(hy, 3, axis=-1)
    kf = hyena_filters(L, hy_filt_w1, hy_filt_b1, hy_filt_w2, hy_filt_b2, hy_filt_w3, hy_filt_freq)
    z = hx1 * long_conv(hv, kf[:, 0], hy_d[0])
    z = hx2 * long_conv(z, kf[:, 1], hy_d[1])
    sb, scg, sx = jnp.split(sc, 3, axis=-1)
    sc_out = sb * dwconv3(scg * sx, sc_conv_w)
    mixed = jnp.concatenate([attn * jax.nn.silu(g_a), z * jax.nn.silu(g_h), sc_out * jax.nn.silu(g_c)], axis=-1)
    return x + gate[:, None] * (mixed @ w_out), k, v


def setup_inputs(seed: int = 0) -> dict:
    key = jax.random.key(seed)
    ks = jax.random.split(key, 22)
    f32 = jnp.float32

    def nrm(k, shape, s):
        return jax.random.normal(k, shape, f32) * s

    cache_shape = (DEC_BATCH, DEPTH, PAST_LEN, N_KV_HEADS, HEAD_DIM)
    return {
        'x_prompt': nrm(ks[0], (BATCH, SEQ, D_MODEL), 1.0),
        'x_sample': nrm(ks[1], (DEC_BATCH, DEC_SEQ, D_MODEL), 1.0),
        'cache_k': nrm(ks[2], cache_shape, 1.0),
        'cache_v': nrm(ks[3], cache_shape, 1.0),
        'c': nrm(ks[4], (DEC_BATCH, D_MODEL), 1.0),
        'c_ctx': nrm(ks[5], (D_MODEL,), 1.0),
        'norm_g': 1.0 + nrm(ks[6], (DEPTH, D_MODEL), 0.1),
        'mod_w': nrm(ks[7], (DEPTH, D_MODEL, 3 * D_MODEL), D_MODEL ** -0.5),
        'mod_b': nrm(ks[8], (DEPTH, 3 * D_MODEL), 0.02),
        'w_in': nrm(ks[9], (DEPTH, D_MODEL, IN_COLS), D_MODEL ** -0.5),
        'attn_sink': nrm(ks[10], (DEPTH, N_HEADS), 0.5),
        'hy_conv_w': nrm(ks[11], (DEPTH, 3, 3 * HYENA_W), 3 ** -0.5),
        'hy_filt_w1': nrm(ks[12], (DEPTH, FILTER_EMB, FILTER_HIDDEN), FILTER_EMB ** -0.5),
        'hy_filt_b1': nrm(ks[13], (DEPTH, FILTER_HIDDEN), 0.02),
        'hy_filt_w2': nrm(ks[14], (DEPTH, FILTER_HIDDEN, FILTER_HIDDEN), FILTER_HIDDEN ** -0.5),
        'hy_filt_b2': nrm(ks[15], (DEPTH, FILTER_HIDDEN), 0.02),
        'hy_filt_w3': nrm(ks[16], (DEPTH, FILTER_HIDDEN, 2 * HYENA_ORDER * HYENA_W), FILTER_HIDDEN ** -0.5),
        'hy_filt_freq': 1.0 + nrm(ks[17], (DEPTH, FILTER_HIDDEN), 0.1),
        'hy_d': nrm(ks[18], (DEPTH, HYENA_ORDER, HYENA_W), 0.5),
        'sc_conv_w': nrm(ks[19], (DEPTH, 3, CONV_W), 3 ** -0.5),
        'w_out': nrm(ks[20], (DEPTH, MIX_W, D_MODEL), MIX_W ** -0.5),
        'final_g': 1.0 + nrm(ks[21], (D_MODEL,), 0.1),
    }


def reference(x_prompt, x_sample, cache_k, cache_v, c, c_ctx, norm_g, mod_w, mod_b, w_in, attn_sink,
              hy_conv_w, hy_filt_w1, hy_filt_b1, hy_filt_w2, hy_filt_b2, hy_filt_w3, hy_filt_freq, hy_d,
              sc_conv_w, w_out, final_g):
    cond_ctx = jax.nn.silu(c_ctx)[None, :]
    cond_lat = jax.nn.silu(c)
    xp, xs = x_prompt, x_sample
    ks_new, vs_new = [], []
    for l in range(DEPTH):
        lw = (norm_g[l], w_in[l], attn_sink[l], hy_conv_w[l], hy_filt_w1[l], hy_filt_b1[l], hy_filt_w2[l],
              hy_filt_b2[l], hy_filt_w3[l], hy_filt_freq[l], hy_d[l], sc_conv_w[l], w_out[l])
        mod_p = cond_ctx @ mod_w[l] + mod_b[l]
        mod_s = cond_lat @ mod_w[l] + mod_b[l]
        xp, k_l, v_l = trunk_layer(xp, mod_p, None, None, *lw)
        xs, _, _ = trunk_layer(xs, mod_s, cache_k[:, l], cache_v[:, l], *lw)
        ks_new.append(k_l)
        vs_new.append(v_l)
    new_cache_k = jnp.stack(ks_new, axis=1)
    new_cache_v = jnp.stack(vs_new, axis=1)
    y_prompt = rms_norm(xp, final_g)
    y_sample = rms_norm(xs, final_g)
    return (y_prompt, y_sample, new_cache_k, new_cache_v)
```

```python
import math
import os as _os_mod
from contextlib import ExitStack
import numpy as np
import ml_dtypes
import concourse.bass as bass
import concourse.mybir as mybir
from concourse.bass_utils import run_bass_kernel_spmd

F32 = mybir.dt.float32
BF16 = mybir.dt.bfloat16
ALU = mybir.AluOpType
AF = mybir.ActivationFunctionType
AX = mybir.AxisListType
NPBF = ml_dtypes.bfloat16

T = 5120
SEQS = [(0, 4096, True)] + [(4096 + 256 * j, 256, False) for j in range(4)]
CQ, CK, CGA, CHY, CGH, CSB, CSC, CSX, CGC, NCH = 0, 4, 5, 9, 15, 17, 19, 21, 23, 25
GATE_CH = set(range(CGA, CGA + 4)) | set(range(CGH, CGH + 2)) | set(range(CGC, CGC + 2))
PI = math.pi


class Buf:
    __slots__ = ("name", "w", "r", "sem", "cnt", "g")

    def __init__(self, name):
        self.name = name
        self.w = {}
        self.r = {}
        self.sem = None
        self.cnt = 0
        self.g = {}


def _merge(d, tok):
    k = id(tok[0])
    if k not in d or d[k][1] < tok[1]:
        d[k] = tok


class Sched:
    ENG = ("pe", "act", "dve", "pool", "sp")
    STRICT = ("act", "dve", "pool")

    def __init__(self, nc, es):
        self.nc = nc
        self.es = es
        self.q = {e: [] for e in self.ENG}
        self.csem = {e: es.enter_context(nc.semaphore("c_" + e)) for e in self.ENG}
        self.cnt = {e: 0 for e in self.ENG}
        self.waited = {e: {} for e in self.ENG}
        self.nsem = 0
        self.bufs = []
        self.sempool = []

    def buf(self, name):
        b = Buf(name)
        self.bufs.append(b)
        return b

    def drop(self, bufs):
        ids = set(id(b) for b in bufs)
        seen = set()
        for b in bufs:
            if id(b) in seen:
                continue
            seen.add(id(b))
            if b.sem is not None:
                self.sempool.append((b.sem, b.cnt))
        self.bufs = [b for b in self.bufs if id(b) not in ids]

    def _deps(self, eng, reads, writes, joins):
        deps = {}
        for b in reads:
            for t in b.w.values():
                _merge(deps, t)
        for b in writes:
            for t in b.w.values():
                _merge(deps, t)
            for t in b.r.values():
                _merge(deps, t)
        for b in joins:
            for t in b.r.values():
                _merge(deps, t)
            for t in b.g.values():
                _merge(deps, t)
        out = []
        own = id(self.csem[eng])
        wd = self.waited[eng]
        for k, (s, v) in deps.items():
            if k == own and eng not in self.STRICT:
                continue
            if wd.get(k, -1) >= v:
                continue
            wd[k] = v
            out.append((s, v))
        return out

    def _record(self, tok, reads, writes, joins):
        for b in reads:
            _merge(b.r, tok)
        for b in writes:
            g = dict(b.w)
            for t in b.r.values():
                _merge(g, t)
            b.g = g
            b.w = {}
            b.r = {}
            _merge(b.w, tok)
        for b in joins:
            _merge(b.w, tok)

    def op(self, eng, fn, reads=(), writes=(), joins=()):
        waits = self._deps(eng, reads, writes, joins)
        self.cnt[eng] += 1
        tok = (self.csem[eng], self.cnt[eng])
        self.q[eng].append((waits, fn, self.csem[eng], 1))
        self._record(tok, reads, writes, joins)
        return tok

    def dma(self, fn, sembuf, reads=(), writes=(), joins=(), q="sp"):
        waits = self._deps(q, reads, writes, joins)
        if sembuf.sem is None:
            if self.sempool:
                sembuf.sem, sembuf.cnt = self.sempool.pop()
            else:
                sembuf.sem = self.es.enter_context(self.nc.semaphore("d%d" % self.nsem))
                self.nsem += 1
        sembuf.cnt += 16
        tok = (sembuf.sem, sembuf.cnt)
        self.q[q].append((waits, fn, sembuf.sem, 16))
        self._record(tok, reads, writes, joins)
        return tok

    def barrier(self):
        deps = {}
        for b in self.bufs:
            for t in list(b.w.values()) + list(b.r.values()):
                _merge(deps, t)
        for e in self.ENG:
            if self.cnt[e] > 0:
                _merge(deps, (self.csem[e], self.cnt[e]))
        for eng in self.ENG:
            waits = []
            wd = self.waited[eng]
            own = id(self.csem[eng])
            for k, (s, v) in deps.items():
                if k == own:
                    continue
                if wd.get(k, -1) >= v:
                    continue
                wd[k] = v
                waits.append((s, v))
            self.q[eng].append((waits, None, None, 0))

    def emit(self):
        nc = self.nc
        qs = self.q
        self.q = {e: [] for e in self.ENG}

        def run(name):
            def body(e):
                for (waits, fn, sem, inc) in qs[name]:
                    for (s, v) in waits:
                        e.wait_ge(s, v)
                    if fn is not None:
                        ins = fn(e)
                        ins.then_inc(sem, inc)
            return body

        with nc.Block() as block:
            block.tensor(run("pe"))
            block.scalar(run("act"))
            block.vector(run("dve"))
            block.gpsimd(run("pool"))
            block.sync(run("sp"))


def _bf(a):
    return np.asarray(a, dtype=np.float32).astype(NPBF)


def dft_consts(L):
    N = 2 * L
    nt = L // 128
    tab = np.arange(N, dtype=np.float64) * (2 * np.pi / N)
    ctab, stab = np.cos(tab), np.sin(tab)
    t = np.arange(L)
    k = np.arange(L)
    tk = (t[:, None] * k[None, :]) % N
    fre = ctab[tk]
    fim = -stab[tk]
    fim[:, 0] = np.where(t % 2 == 0, 1.0, -1.0)
    kperm = np.concatenate([np.arange(0, L, 2), np.arange(1, L, 2)])
    fre = fre[:, kperm]
    fim = fim[:, kperm]
    fw = np.stack([fre, fim], 0).reshape(2, nt, 128, nt, 128)
    fw = np.ascontiguousarray(fw.transpose(3, 2, 0, 1, 4))
    ire = (2.0 / N) * ctab[tk].T
    ire[0, :] = 1.0 / N
    iim = -(2.0 / N) * stab[tk].T
    iim[0, :] = (1.0 / N) * np.where(t % 2 == 0, 1.0, -1.0)
    ire = ire[kperm, :]
    iim = iim[kperm, :]
    iv = np.concatenate([ire, iim], 0).reshape(2 * nt, 128, nt, 128)
    iv = np.ascontiguousarray(iv.transpose(2, 1, 0, 3))
    return _bf(fw), _bf(iv)


def filt_consts(L):
    N = 2 * L
    f32 = np.float32
    tl = np.linspace(0.0, 1.0, L, dtype=f32)
    bands = 16
    ang = (f32(2.0 * math.pi / L) * np.arange(L, dtype=f32))
    fr = np.linspace(1e-4, bands - 1, bands, dtype=f32)
    feats = np.concatenate([tl[:, None], np.cos(fr[None, :] * ang[:, None]), -np.sin(fr[None, :] * ang[:, None])], -1).astype(f32)
    n = np.arange(N)
    lag = np.where(n < L, n, N - n)
    lag[L] = 0
    fk = feats[lag].T.astype(f32)
    hi = fk.astype(NPBF)
    lo = (fk - hi.astype(f32)).astype(NPBF)
    featc = np.ascontiguousarray(np.stack([hi, lo], 1))
    max_decay = math.log(1e-2) / 0.3
    min_decay = math.log(1e-2) / 1.5
    deltas = np.abs(np.linspace(min_decay, max_decay, 256, dtype=f32))
    dec = np.exp(-tl[lag][None, :] * deltas[:, None]).astype(f32)
    dec[:, L] = 0.0
    return featc, np.ascontiguousarray(dec.reshape(2, 128, N))


def rope_consts():
    Ls = 4096
    t = np.arange(Ls)
    inv = (10000.0 ** (-np.arange(16, dtype=np.float32) / 16)).astype(np.float32)
    row = (t // 64).astype(np.float32)[:, None] * inv
    col = (t % 64).astype(np.float32)[:, None] * inv
    ang = np.stack([row, col], 1)
    C = np.zeros((128, Ls), np.float32)
    S = np.zeros((128, Ls), np.float32)
    for p in range(128):
        pos = p % 64
        half = (pos % 32) // 16
        f = pos % 16
        C[p] = np.cos(ang[:, half, f])
        S[p] = np.sin(ang[:, half, f])
    PM = np.zeros((128, 128), np.float32)
    for m in range(128):
        pos = m % 64
        if pos < 32:
            PM[m + 32, m] = -1.0
        else:
            PM[m - 32, m] = 1.0
    return C, S, _bf(PM)


def col_perm():
    dperm = np.zeros(64, np.int64)
    for pos in range(64):
        ab, half, f = pos // 32, (pos % 32) // 16, pos % 16
        dperm[pos] = half * 32 + ab * 16 + f
    cols = []
    for j in range(4):
        cols += list(0 + j * 64 + dperm) + list(0 + (4 + j) * 64 + dperm)
    cols += list(512 + dperm) + list(512 + 64 + dperm)
    for j in range(4):
        cols += list(768 + j * 64 + np.arange(64)) + list(768 + (4 + j) * 64 + np.arange(64))
    cols += list(1280 + np.arange(768))
    cols += list(2048 + np.arange(256))
    cols += list(2304 + np.arange(768))
    cols += list(3072 + np.arange(256))
    rows = []
    for j in range(4):
        rows += list(j * 64 + np.arange(64)) + list((4 + j) * 64 + np.arange(64))
    rows += list(512 + np.arange(512))
    return np.array(cols), np.array(rows), dperm


def build(depth=4, stop_after=None):
    nc = bass.Bass("TRN2", target_bir_lowering=False)

    def din(name, shape, dt=F32):
        return nc.dram_tensor(name, list(shape), dt, kind="ExternalInput").ap()

    def dout(name, shape):
        return nc.dram_tensor(name, list(shape), F32, kind="ExternalOutput").ap()

    def dscr(name, shape, dt):
        return nc.dram_tensor(name, list(shape), dt, kind="Internal").ap()

    x_tm = din("x_tm", [T, 1024])
    ck_in = din("ck", [4, 256, 128])
    cv_in = din("cv", [4, 256, 128])
    cond_in = din("cond", [1024, 2])
    norm_g = din("norm_g", [4, 1024])
    mod_w = din("mod_w", [4, 1024, 3072])
    mod_b = din("mod_b", [4, 3072])
    w_in = din("w_in", [4, 1024, NCH * 128])
    w_kv = din("w_kv", [4, 1024, 256])
    sink_in = din("sink", [4, 8])
    hy_cw = din("hy_cw", [4, 3, 768])
    f_w1 = din("f_w1", [4, 33, 64])
    f_b1 = din("f_b1", [4, 64])
    f_w2 = din("f_w2", [4, 64, 64])
    f_b2 = din("f_b2", [4, 64])
    f_w3 = din("f_w3", [4, 64, 1024])
    f_fq = din("f_fq", [4, 64])
    hy_d = din("hy_d", [4, 2, 256])
    sc_cw = din("sc_cw", [4, 3, 256])
    w_out = din("w_out", [4, 1024, 1024])
    final_g = din("final_g", [1024])
    ropeC = din("ropeC", [128, 4096])
    ropeS = din("ropeS", [128, 4096])
    permm = din("permm", [128, 128], BF16)
    FWc = {4096: din("fw4096", [32, 128, 2, 32, 128], BF16), 256: din("fw256", [2, 128, 2, 2, 128], BF16)}
    IVc = {4096: din("iv4096", [32, 128, 64, 128], BF16), 256: din("iv256", [2, 128, 4, 128], BF16)}
    FEc = {4096: din("fe4096", [33, 2, 8192], BF16), 256: din("fe256", [33, 2, 512], BF16)}
    DEc = {4096: din("de4096", [2, 128, 8192]), 256: din("de256", [2, 128, 512])}

    y_out = dout("y", [T, 1024])
    nk_out = dout("nk", [4, 4, 256, 128])
    nv_out = dout("nv", [4, 4, 256, 128])

    xT = dscr("xT", [8, 128, T], F32)
    Pscr = dscr("Pscr", [NCH, 128, T], BF16)
    Vscr = dscr("Vscr", [T, 128], BF16)
    HX = dscr("HX", [T, 768], BF16)
    Zscr = dscr("Zscr", [2, 128, T], BF16)
    TA = {4096: dscr("TA4096", [32, 128, 512], F32), 256: dscr("TA256", [2, 128, 512], F32)}
    TC = {4096: dscr("TC4096", [32, 128, 512], F32), 256: dscr("TC256", [2, 128, 512], F32)}
    TD = {4096: dscr("TD4096", [128, 512], F32), 256: dscr("TD256", [128, 512], F32)}

    with ExitStack() as es:
        S = Sched(nc, es)

        uid = [0]

        def sbt(st, name, shape, dt):
            uid[0] += 1
            return st.enter_context(nc.sbuf_tensor("%s_%d" % (name, uid[0]), list(shape), dt))

        ident = sbt(es, "ident", [128, 128], BF16)
        identf = sbt(es, "identf", [128, 128], F32)
        ones = sbt(es, "ones", [128, 128], BF16)
        mlo = sbt(es, "mlo", [128, 128], BF16)
        mhi = sbt(es, "mhi", [128, 128], BF16)
        mtmp = sbt(es, "mtmp", [128, 128], F32)
        pmat = sbt(es, "pmat", [128, 128], BF16)
        sgn = sbt(es, "sgn", [128, 1], F32)
        m0v = sbt(es, "m0v", [128, 1], F32)
        m1v = sbt(es, "m1v", [128, 1], F32)
        zeros = sbt(es, "zeros", [128, 512], F32)
        SH = sbt(es, "SH", [128, 4, 2, 8], F32)
        GP = sbt(es, "GP", [128, 4, 2, 8], F32)
        GT = sbt(es, "GT", [128, 4, 2, 8], F32)
        FG = sbt(es, "FG", [128, 8], F32)
        Wh = {}
        B_const = S.buf("const")
        B_mod = S.buf("mod")
        B_W = S.buf("W")
        B_xT = S.buf("xT")
        B_P = S.buf("Pscr")
        B_V = S.buf("Vscr")
        B_HX = S.buf("HX")
        B_Z = S.buf("Zscr")
        B_tab = {4096: S.buf("tab4096"), 256: S.buf("tab256")}
        B_out = S.buf("out")

        PS = [es.enter_context(nc.psum_tensor("ps%d" % i, [128, 512], F32)) for i in range(8)]
        BPS = [S.buf("ps%d" % i) for i in range(8)]
        psi = [0]

        def nps():
            i = psi[0] % 8
            psi[0] += 1
            return PS[i], BPS[i]

        rr = [0]

        def alt(engs=("dve", "pool")):
            rr[0] += 1
            return engs[rr[0] % len(engs)]

        def phase_end(st_bufs):
            S.barrier()
            S.emit()
            S.drop(st_bufs)

        S.op("pool", lambda e: e.memset(identf[:], 0.0), writes=[B_const])
        S.op("pool", lambda e: e.affine_select(out=identf[:], in_=identf[:], pattern=[[-1, 128]], compare_op=ALU.not_equal, fill=1.0, base=0, channel_multiplier=1), writes=[B_const])
        S.op("dve", lambda e: e.tensor_copy(out=ident[:], in_=identf[:]), reads=[B_const], joins=[B_const])
        S.op("pool", lambda e: e.memset(ones[:], 1.0), joins=[B_const])
        S.op("pool", lambda e: e.memset(zeros[:], 0.0), joins=[B_const])
        S.op("pool", lambda e: e.memset(mtmp[:], 1.0), joins=[B_const])
        S.op("pool", lambda e: e.affine_select(out=mtmp[:], in_=mtmp[:], pattern=[[-1, 128]], compare_op=ALU.is_ge, fill=0.0, base=0, channel_multiplier=1), joins=[B_const])
        S.op("pool", lambda e: e.tensor_copy(out=mlo[:], in_=mtmp[:]), joins=[B_const])
        S.op("pool", lambda e: e.memset(mtmp[:], 1.0), joins=[B_const])
        S.op("pool", lambda e: e.affine_select(out=mtmp[:], in_=mtmp[:], pattern=[[1, 128]], compare_op=ALU.is_ge, fill=0.0, base=0, channel_multiplier=-1), joins=[B_const])
        S.op("pool", lambda e: e.tensor_copy(out=mhi[:], in_=mtmp[:]), joins=[B_const])
        S.op("pool", lambda e: e.memset(mtmp[:], 1.0), joins=[B_const])
        S.op("pool", lambda e: e.affine_select(out=mtmp[:, 0:1], in_=mtmp[:, 0:1], pattern=[[0, 1]], compare_op=ALU.not_equal, fill=0.0, base=0, channel_multiplier=1), joins=[B_const])
        S.op("pool", lambda e: e.tensor_copy(out=m0v[:], in_=mtmp[:, 0:1]), joins=[B_const])
        S.op("pool", lambda e: e.tensor_scalar(out=m1v[:], in0=m0v[:], scalar1=-1.0, scalar2=1.0, op0=ALU.mult, op1=ALU.add), joins=[B_const])
        S.dma(lambda e: e.dma_start(out=pmat[:], in_=permm), B_const, joins=[B_const])
        S.dma(lambda e: e.dma_start(out=FG[:], in_=final_g.rearrange("(c p) -> p c", p=128), allow_slow_non_contiguous=True), B_const, joins=[B_const])

        with ExitStack() as st:
            condt = sbt(st, "condt", [128, 8, 2], F32)
            condb = sbt(st, "condb", [128, 8, 2], BF16)
            mst = [sbt(st, "mst%d" % i, [128, 8, 384], F32) for i in range(2)]
            msb = [sbt(st, "msb%d" % i, [128, 8, 384], BF16) for i in range(2)]
            modT = sbt(st, "modT", [128, 24, 2], F32)
            mbT = sbt(st, "mbT", [128, 24], F32)
            ngT = sbt(st, "ngT", [128, 8], F32)
            sgnt = sbt(st, "sgnt", [128, 2], F32)
            b_cond = S.buf("cond")
            b_mst = [S.buf("mst0"), S.buf("mst1")]
            b_msb = [S.buf("msb0"), S.buf("msb1")]
            b_modT = S.buf("modT")
            b_mb = S.buf("mb")
            loc = [b_cond, b_modT, b_mb] + b_mst + b_msb
            S.dma(lambda e: e.dma_start(out=condt[:], in_=cond_in.rearrange("(c p) s -> p c s", p=128)), b_cond, writes=[b_cond])
            S.op("act", lambda e: e.activation(out=condt[:], in_=condt[:], func=AF.Silu), reads=[b_cond], writes=[b_cond])
            S.op("dve", lambda e: e.tensor_copy(out=condb[:], in_=condt[:]), reads=[b_cond], joins=[b_cond])
            for l in range(depth):
                S.dma(lambda e, l=l: e.dma_start(out=mbT[:], in_=mod_b[l].rearrange("(c p) -> p c", p=128), allow_slow_non_contiguous=True), b_mb, writes=[b_mb])
                S.dma(lambda e, l=l: e.dma_start(out=ngT[:], in_=norm_g[l].rearrange("(c p) -> p c", p=128), allow_slow_non_contiguous=True), b_mb, joins=[b_mb])
                for cb in range(8):
                    sl = cb % 2
                    for kc in range(0, 8, 4):
                        S.dma(lambda e, l=l, cb=cb, sl=sl, kc=kc: e.dma_start(
                            out=mst[sl][:, kc:kc + 4, :],
                            in_=mod_w[l, kc * 128:(kc + 4) * 128, cb * 384:(cb + 1) * 384].rearrange("(c p) f -> p c f", p=128)),
                            b_mst[sl], writes=[b_mst[sl]] if kc == 0 else (), joins=() if kc == 0 else [b_mst[sl]])
                    S.op(alt(("dve", "pool")), lambda e, sl=sl: e.tensor_copy(out=msb[sl][:], in_=mst[sl][:]), reads=[b_mst[sl]], writes=[b_msb[sl]])
                    for fc in range(3):
                        ps, bps = nps()
                        for kc in range(8):
                            S.op("pe", lambda e, ps=ps, sl=sl, kc=kc, fc=fc: e.matmul(ps[:, 0:2], lhsT=msb[sl][:, kc, fc * 128:(fc + 1) * 128], rhs=condb[:, kc, :], start=(kc == 0), stop=(kc == 7)),
                                 reads=[b_msb[sl], b_cond], writes=[bps] if kc == 0 else (), joins=() if kc == 0 else [bps])
                        S.op("act", lambda e, ps=ps, cb=cb, fc=fc: e.copy(out=modT[:, cb * 3 + fc, :], in_=ps[:, 0:2]), reads=[bps], joins=[b_modT])
                for s in range(2):
                    S.op("dve", lambda e, l=l, s=s: e.tensor_tensor(out=SH[:, l, s, :], in0=modT[:, 0:8, s], in1=mbT[:, 0:8], op=ALU.add), reads=[b_modT, b_mb], joins=[B_mod])
                    S.op("dve", lambda e, l=l, s=s: e.tensor_tensor(out=GP[:, l, s, :], in0=modT[:, 8:16, s], in1=mbT[:, 8:16], op=ALU.add), reads=[b_modT, b_mb], joins=[B_mod])
                    S.op("dve", lambda e, l=l, s=s: e.scalar_tensor_tensor(out=GP[:, l, s, :], in0=GP[:, l, s, :], scalar=1.0, in1=ngT[:], op0=ALU.add, op1=ALU.mult), reads=[b_mb, B_mod], joins=[B_mod])
                    S.op("dve", lambda e, l=l, s=s: e.tensor_tensor(out=GT[:, l, s, :], in0=modT[:, 16:24, s], in1=mbT[:, 16:24], op=ALU.add), reads=[b_modT, b_mb], joins=[B_mod])
            phase_end(loc)

        with ExitStack() as st:
            altrow = sbt(st, "altrow", [128, 128], F32)
            junk = sbt(st, "junk0", [128, 128], F32)
            b_a = S.buf("altrow")
            S.op("pool", lambda e: e.memset(altrow[:], 1.0), writes=[b_a])
            S.op("pool", lambda e: e.memset(altrow[:].rearrange("p (a two) -> p a two", two=2)[:, :, 1:2], -1.0), joins=[b_a])
            S.op("dve", lambda e: e.tensor_tensor(out=junk[:], in0=identf[:], in1=altrow[:], op=ALU.mult), reads=[b_a, B_const], writes=[b_a])
            S.op("dve", lambda e: e.tensor_reduce(out=sgn[:], in_=junk[:], axis=AX.X, op=ALU.add), reads=[b_a], joins=[B_const])
            phase_end([b_a])

        with ExitStack() as st:
            xin = [sbt(st, "xin%d" % i, [128, 1024], F32) for i in range(2)]
            xhi = [sbt(st, "xhi%d" % i, [128, 1024], BF16) for i in range(2)]
            xlo = [sbt(st, "xlo%d" % i, [128, 1024], BF16) for i in range(2)]
            xtmp = [sbt(st, "xtmp%d" % i, [128, 1024], F32) for i in range(2)]
            xo = [sbt(st, "xo%d" % i, [128, 8, 128], F32) for i in range(2)]
            b_in = [S.buf("xin0"), S.buf("xin1")]
            b_hl = [S.buf("xhl0"), S.buf("xhl1")]
            b_xo = [S.buf("xo0"), S.buf("xo1")]
            nb = T // 128
            first = [True]

            def ld0(b):
                sl = b % 2
                S.dma(lambda e: e.dma_start(out=xin[sl][:], in_=x_tm[b * 128:(b + 1) * 128, :]), b_in[sl], writes=[b_in[sl]])
            ld0(0)
            for b in range(nb):
                sl = b % 2
                if b + 1 < nb:
                    ld0(b + 1)
                S.op("act", lambda e, sl=sl: e.copy(out=xhi[sl][:], in_=xin[sl][:]), reads=[b_in[sl]], writes=[b_hl[sl]])
                S.op("dve", lambda e, sl=sl: e.tensor_copy(out=xtmp[sl][:], in_=xhi[sl][:]), reads=[b_hl[sl]], joins=[b_hl[sl]])
                S.op("dve", lambda e, sl=sl: e.tensor_tensor(out=xlo[sl][:], in0=xin[sl][:], in1=xtmp[sl][:], op=ALU.subtract), reads=[b_in[sl], b_hl[sl]], joins=[b_hl[sl]])
                for half in range(2):
                    ps, bps = nps()
                    for c in range(4):
                        cc = half * 4 + c
                        S.op("pe", lambda e, ps=ps, sl=sl, c=c, cc=cc: e.matmul(ps[:, c * 128:(c + 1) * 128], lhsT=xhi[sl][:, cc * 128:(cc + 1) * 128], rhs=ident[:], start=True, stop=False),
                             reads=[b_hl[sl], B_const], writes=[bps] if c == 0 else (), joins=() if c == 0 else [bps])
                        S.op("pe", lambda e, ps=ps, sl=sl, c=c, cc=cc: e.matmul(ps[:, c * 128:(c + 1) * 128], lhsT=xlo[sl][:, cc * 128:(cc + 1) * 128], rhs=ident[:], start=False, stop=True),
                             reads=[b_hl[sl]], joins=[bps])
                    S.op("act" if half == 0 else "dve",
                         (lambda e, ps=ps, sl=sl, half=half: e.copy(out=xo[sl][:, half * 4:(half + 1) * 4, :], in_=ps[:].rearrange("p (c t) -> p c t", c=4))) if half == 0 else
                         (lambda e, ps=ps, sl=sl, half=half: e.tensor_copy(out=xo[sl][:, half * 4:(half + 1) * 4, :], in_=ps[:].rearrange("p (c t) -> p c t", c=4))),
                         reads=[bps], writes=[b_xo[sl]] if half == 0 else (), joins=() if half == 0 else [b_xo[sl]])
                S.dma(lambda e, sl=sl, b=b: e.dma_start(out=xT[:, :, b * 128:(b + 1) * 128].rearrange("c p t -> p c t"), in_=xo[sl][:]), b_xo[sl],
                      reads=[b_xo[sl]], writes=[B_xT] if first[0] else (), joins=() if first[0] else [B_xT])
                first[0] = False
            phase_end(b_in + b_hl + b_xo)

        def load_weights(l):
            with ExitStack() as st:
                wst = [sbt(st, "wst%d" % i, [128, 8, 512], F32) for i in range(2)]
                b_wst = [S.buf("wst0"), S.buf("wst1")]
                jobs = []
                for cb in range(0, NCH * 128, 512):
                    w = min(512, NCH * 128 - cb)
                    jobs.append((w_in, Wh["b"], cb, w))
                jobs.append((w_kv, Wh["kv"], 0, 256))
                for cb in range(0, 1024, 512):
                    jobs.append((w_out, Wh["o"], cb, 512))
                firstw = [True]

                def ldj(i):
                    src, dst, cb, w = jobs[i]
                    sl = i % 2
                    for kc in range(0, 8, 4):
                        S.dma(lambda e, kc=kc: e.dma_start(out=wst[sl][:, kc:kc + 4, 0:w], in_=src[l, kc * 128:(kc + 4) * 128, cb:cb + w].rearrange("(c p) f -> p c f", p=128)),
                              b_wst[sl], writes=[b_wst[sl]] if kc == 0 else (), joins=() if kc == 0 else [b_wst[sl]])
                ldj(0)
                for i, (src, dst, cb, w) in enumerate(jobs):
                    sl = i % 2
                    if i + 1 < len(jobs):
                        ldj(i + 1)
                    eng = ("dve", "pool", "act")[i % 3]
                    if eng == "act":
                        fn = lambda e, sl=sl, dst=dst, cb=cb, w=w: e.copy(out=dst[:, :, cb:cb + w], in_=wst[sl][:, :, 0:w])
                    else:
                        fn = lambda e, sl=sl, dst=dst, cb=cb, w=w: e.tensor_copy(out=dst[:, :, cb:cb + w], in_=wst[sl][:, :, 0:w])
                    S.op(eng, fn, reads=[b_wst[sl]], writes=[B_W] if firstw[0] else (), joins=() if firstw[0] else [B_W])
                    firstw[0] = False
                phase_end(b_wst)

        def rms_norm_tile(st, xt, b_xt, n, gp_ap, sh_ap, hT, b_hT, tagname):
            sq = sbt(st, "sq_" + tagname, [128, 8, n], BF16)
            rstd = sbt(st, "rstd_" + tagname, [128, n], F32)
            tmpn = sbt(st, "tmpn_" + tagname, [128, 8, n], F32)
            b_sq = S.buf("sq")
            b_rstd = S.buf("rstd")
            b_tmpn = S.buf("tmpn")

            def run():
                S.op("act", lambda e: e.activation(out=sq[:], in_=xt[:], func=AF.Square), reads=[b_xt], writes=[b_sq])
                ps, bps = nps()
                for c in range(8):
                    S.op("pe", lambda e, c=c: e.matmul(ps[:, 0:n], lhsT=ones[:], rhs=sq[:, c, :], start=(c == 0), stop=(c == 7)),
                         reads=[b_sq, B_const], writes=[bps] if c == 0 else (), joins=() if c == 0 else [bps])
                S.op("dve", lambda e: e.tensor_scalar(out=rstd[:], in0=ps[:, 0:n], scalar1=1.0 / 1024, scalar2=1e-6, op0=ALU.mult, op1=ALU.add), reads=[bps], writes=[b_rstd])
                S.op("act", lambda e: e.activation(out=rstd[:], in_=rstd[:], func=AF.Sqrt), reads=[b_rstd], writes=[b_rstd])
                S.op("dve", lambda e: e.reciprocal(out=rstd[:], in_=rstd[:]), reads=[b_rstd], writes=[b_rstd])
                for c in range(8):
                    eng = "dve" if c % 2 == 0 else "pool"
                    S.op(eng, lambda e, c=c: e.tensor_tensor(out=tmpn[:, c, :], in0=xt[:, c, :], in1=rstd[:], op=ALU.mult), reads=[b_xt, b_rstd], writes=[b_tmpn] if c == 0 else (), joins=() if c == 0 else [b_tmpn])
                for c in range(8):
                    if sh_ap is None:
                        S.op("act", lambda e, c=c: e.activation(out=hT[:, c, :], in_=tmpn[:, c, :], func=AF.Identity, scale=gp_ap[:, c:c + 1]), reads=[b_tmpn, B_mod, B_const], writes=[b_hT] if c == 0 else (), joins=() if c == 0 else [b_hT])
                    else:
                        S.op("act", lambda e, c=c: e.activation(out=hT[:, c, :], in_=tmpn[:, c, :], func=AF.Identity, scale=gp_ap[:, c:c + 1], bias=sh_ap[:, c:c + 1]), reads=[b_tmpn, B_mod], writes=[b_hT] if c == 0 else (), joins=() if c == 0 else [b_hT])
            return run, [b_sq, b_rstd, b_tmpn]

        def norm_tmps(st, tag, nmax):
            return dict(sq=sbt(st, "sq_" + tag, [128, 8, nmax], BF16), rstd=sbt(st, "rstd_" + tag, [128, nmax], F32), tmpn=sbt(st, "tmpn_" + tag, [128, 8, nmax], F32),
                        b_sq=S.buf("sq"), b_rstd=S.buf("rstd"), b_tmpn=S.buf("tmpn"))

        def rms_norm(tm, xt, b_xt, n, gp_ap, sh_ap, hT, b_hT):
            sq, rstd, tmpn = tm["sq"], tm["rstd"], tm["tmpn"]
            b_sq, b_rstd, b_tmpn = tm["b_sq"], tm["b_rstd"], tm["b_tmpn"]
            S.op("act", lambda e: e.activation(out=sq[:, :, 0:n], in_=xt[:, :, 0:n], func=AF.Square), reads=[b_xt], writes=[b_sq])
            ps, bps = nps()
            for c in range(8):
                S.op("pe", lambda e, c=c: e.matmul(ps[:, 0:n], lhsT=ones[:], rhs=sq[:, c, 0:n], start=(c == 0), stop=(c == 7)),
                     reads=[b_sq, B_const], writes=[bps] if c == 0 else (), joins=() if c == 0 else [bps])
            S.op("dve", lambda e: e.tensor_scalar(out=rstd[:, 0:n], in0=ps[:, 0:n], scalar1=1.0 / 1024, scalar2=1e-6, op0=ALU.mult, op1=ALU.add), reads=[bps], writes=[b_rstd])
            S.op("act", lambda e: e.activation(out=rstd[:, 0:n], in_=rstd[:, 0:n], func=AF.Sqrt), reads=[b_rstd], writes=[b_rstd])
            S.op("dve", lambda e: e.reciprocal(out=rstd[:, 0:n], in_=rstd[:, 0:n]), reads=[b_rstd], writes=[b_rstd])
            for c in range(8):
                eng = "dve" if c % 2 == 0 else "pool"
                S.op(eng, lambda e, c=c: e.tensor_tensor(out=tmpn[:, c, 0:n], in0=xt[:, c, 0:n], in1=rstd[:, 0:n], op=ALU.mult), reads=[b_xt, b_rstd], writes=[b_tmpn] if c == 0 else (), joins=() if c == 0 else [b_tmpn])
            for c in range(8):
                if sh_ap is None:
                    S.op("act", lambda e, c=c: e.activation(out=hT[:, c, 0:n], in_=tmpn[:, c, 0:n], func=AF.Identity, scale=gp_ap[:, c:c + 1]), reads=[b_tmpn, B_mod, B_const], writes=[b_hT] if c == 0 else (), joins=() if c == 0 else [b_hT])
                else:
                    S.op("act", lambda e, c=c: e.activation(out=hT[:, c, 0:n], in_=tmpn[:, c, 0:n], func=AF.Identity, scale=gp_ap[:, c:c + 1], bias=sh_ap[:, c:c + 1]), reads=[b_tmpn, B_mod], writes=[b_hT] if c == 0 else (), joins=() if c == 0 else [b_hT])

        def tiles_of(base, L):
            n = min(512, L)
            return [(base + i * n, n) for i in range(L // n)]

        def filter_phase(l, L):
            N = 2 * L
            nt = L // 128
            pw = min(512, L)
            npc = N // pw
            with ExitStack() as st:
                w1s = sbt(st, "w1s", [33, 64], F32)
                w1b = sbt(st, "w1b", [33, 64], BF16)
                w2s = sbt(st, "w2s", [64, 64], F32)
                w2b = sbt(st, "w2b", [64, 64], BF16)
                w3s = sbt(st, "w3s", [64, 1024], F32)
                w3b = sbt(st, "w3b", [64, 1024], BF16)
                fqv = sbt(st, "fqv", [64, 1], F32)
                fb1 = sbt(st, "fb1", [64, 1], F32)
                fb2 = sbt(st, "fb2", [64, 1], F32)
                fet = [sbt(st, "fet%d" % i, [33, 2, pw], BF16) for i in range(2)]
                dct = [sbt(st, "dct%d" % i, [128, 2, pw], F32) for i in range(2)]
                arg = sbt(st, "arg", [64, pw], F32)
                tmpa = sbt(st, "tmpa", [64, pw], F32)
                h1 = sbt(st, "h1", [64, pw], BF16)
                h2 = sbt(st, "h2", [64, pw], BF16)
                kf = sbt(st, "kf", [128, 4, pw], F32)
                kjunk = sbt(st, "kjunk", [128, pw], F32)
                kb = sbt(st, "kb", [128, 4, pw], BF16)
                nacc = sbt(st, "nacc", [128, 4, npc], F32)
                nrm = sbt(st, "nrm", [128, 4], F32)
                rnh = sbt(st, "rnh", [128, 4], BF16)
                rnhf = sbt(st, "rnhf", [128, 4], F32)
                rnl = sbt(st, "rnl", [128, 4], BF16)
                dg = sbt(st, "dg", [128, 2, 4, 128], BF16)
                rnbc = sbt(st, "rnbc", [128, 512], F32)
                kTM = sbt(st, "kTM", [128, 2 * nt, 512], BF16)
                fwt = [sbt(st, "fwt%d" % i, [128, 2, nt, 128], BF16) for i in range(2)]
                ev = [sbt(st, "ev%d" % i, [128, 512], F32) for i in range(4)]
                tabA = sbt(st, "tabA", [128, 512], F32)
                tabC = sbt(st, "tabC", [128, 512], F32)
                tabD = sbt(st, "tabD", [128, 512], F32)
                b_w = S.buf("fw_w")
                b_fet = [S.buf("fet0"), S.buf("fet1")]
                b_dct = [S.buf("dct0"), S.buf("dct1")]
                b_arg = S.buf("arg")
                b_tmpa = S.buf("tmpa")
                b_h1 = S.buf("h1")
                b_h2 = S.buf("h2")
                b_kf = S.buf("kf")
                b_kb = S.buf("kb")
                b_nacc = S.buf("nacc")
                b_rn = S.buf("rn")
                b_kTM = S.buf("kTM")
                b_fwt = [S.buf("fwt0"), S.buf("fwt1")]
                b_ev = [S.buf("ev%d" % i) for i in range(4)]
                b_tA, b_tC, b_tD = S.buf("tabA"), S.buf("tabC"), S.buf("tabD")
                loc = [b_w, b_arg, b_tmpa, b_h1, b_h2, b_kf, b_kb, b_nacc, b_rn, b_kTM, b_tA, b_tC, b_tD] + b_fet + b_dct + b_fwt + b_ev
                S.dma(lambda e: e.dma_start(out=w1s[:], in_=f_w1[l]), b_w, writes=[b_w])
                S.dma(lambda e: e.dma_start(out=w2s[:], in_=f_w2[l]), b_w, joins=[b_w])
                S.dma(lambda e: e.dma_start(out=w3s[:], in_=f_w3[l]), b_w, joins=[b_w])
                S.dma(lambda e: e.dma_start(out=fqv[:], in_=f_fq[l].rearrange("(p o) -> p o", o=1)), b_w, joins=[b_w])
                S.dma(lambda e: e.dma_start(out=fb1[:], in_=f_b1[l].rearrange("(p o) -> p o", o=1)), b_w, joins=[b_w])
                S.dma(lambda e: e.dma_start(out=fb2[:], in_=f_b2[l].rearrange("(p o) -> p o", o=1)), b_w, joins=[b_w])
                S.op("dve", lambda e: e.tensor_copy(out=w1b[:], in_=w1s[:]), reads=[b_w], joins=[b_w])
                S.op("dve", lambda e: e.tensor_copy(out=w2b[:], in_=w2s[:]), reads=[b_w], joins=[b_w])
                S.op("dve", lambda e: e.tensor_copy(out=w3b[:], in_=w3s[:]), reads=[b_w], joins=[b_w])
                S.op("dve", lambda e: e.tensor_tensor(out=fb1[:], in0=fb1[:], in1=fqv[:], op=ALU.mult), reads=[b_w], joins=[b_w])
                S.op("dve", lambda e: e.tensor_tensor(out=fb2[:], in0=fb2[:], in1=fqv[:], op=ALU.mult), reads=[b_w], joins=[b_w])

                def ldp(pc):
                    sl = pc % 2
                    S.dma(lambda e: e.dma_start(out=fet[sl][:], in_=FEc[L][:, :, pc * pw:(pc + 1) * pw]), b_fet[sl], writes=[b_fet[sl]])
                    S.dma(lambda e: e.dma_start(out=dct[sl][:], in_=DEc[L][:, :, pc * pw:(pc + 1) * pw].rearrange("c p n -> p c n")), b_dct[sl], writes=[b_dct[sl]])

                def sin_layer(ps, bps, bvec, hout, b_hout):
                    S.op("act", lambda e: e.activation(out=arg[:], in_=ps[0:64, 0:pw], func=AF.Identity, scale=fqv[:, 0:1], bias=bvec[:, 0:1]), reads=[bps, b_w], writes=[b_arg])
                    S.op("dve", lambda e: e.tensor_scalar(out=tmpa[:], in0=arg[:], scalar1=PI, scalar2=-2 * PI, op0=ALU.is_gt, op1=ALU.mult), reads=[b_arg], writes=[b_tmpa])
                    S.op("dve", lambda e: e.tensor_tensor(out=arg[:], in0=arg[:], in1=tmpa[:], op=ALU.add), reads=[b_tmpa], writes=[b_arg])
                    S.op("dve", lambda e: e.tensor_scalar(out=tmpa[:], in0=arg[:], scalar1=-PI, scalar2=2 * PI, op0=ALU.is_lt, op1=ALU.mult), reads=[b_arg], writes=[b_tmpa])
                    S.op("dve", lambda e: e.tensor_tensor(out=arg[:], in0=arg[:], in1=tmpa[:], op=ALU.add), reads=[b_tmpa], writes=[b_arg])
                    S.op("dve", lambda e: e.tensor_scalar(out=arg[:], in0=arg[:], scalar1=3.14159, scalar2=-3.14159, op0=ALU.min, op1=ALU.max), reads=[b_arg], writes=[b_arg])
                    S.op("act", lambda e: e.activation(out=hout[:], in_=arg[:], func=AF.Sin), reads=[b_arg], writes=[b_hout])

                ldp(0)
                for pc in range(npc):
                    sl = pc % 2
                    if pc + 1 < npc:
                        ldp(pc + 1)
                    dr = 0 if pc * pw < L else 1
                    ps, bps = nps()
                    S.op("pe", lambda e, ps=ps, sl=sl: e.matmul(ps[0:64, 0:pw], lhsT=w1b[:], rhs=fet[sl][:, 0, :], start=True, stop=False), reads=[b_w, b_fet[sl]], writes=[bps])
                    S.op("pe", lambda e, ps=ps, sl=sl: e.matmul(ps[0:64, 0:pw], lhsT=w1b[:], rhs=fet[sl][:, 1, :], start=False, stop=True), reads=[b_w, b_fet[sl]], joins=[bps])
                    sin_layer(ps, bps, fb1, h1, b_h1)
                    ps, bps = nps()
                    S.op("pe", lambda e, ps=ps: e.matmul(ps[0:64, 0:pw], lhsT=w2b[:], rhs=h1[:], start=True, stop=True), reads=[b_w, b_h1], writes=[bps])
                    sin_layer(ps, bps, fb2, h2, b_h2)
                    for oc in range(4):
                        o, cc = oc // 2, oc % 2
                        col = (dr * 2 + o) * 256 + cc * 128
                        ps, bps = nps()
                        S.op("pe", lambda e, ps=ps, col=col: e.matmul(ps[:, 0:pw], lhsT=w3b[:, col:col + 128], rhs=h2[:], start=True, stop=True), reads=[b_w, b_h2], writes=[bps])
                        S.op("dve", lambda e, ps=ps, cc=cc, oc=oc, sl=sl: e.tensor_tensor(out=kf[:, oc, :], in0=ps[:, 0:pw], in1=dct[sl][:, cc, :], op=ALU.mult), reads=[bps, b_dct[sl]], writes=[b_kf] if oc == 0 else (), joins=() if oc == 0 else [b_kf])
                        S.op("act", lambda e, oc=oc, pc=pc: e.activation(out=kjunk[:], in_=kf[:, oc, :], func=AF.Abs, accum_out=nacc[:, oc, pc:pc + 1]), reads=[b_kf], joins=[b_nacc])
                    S.op("pool", lambda e: e.tensor_copy(out=kb[:], in_=kf[:]), reads=[b_kf], writes=[b_kb])
                    for blk in range(pw // 128):
                        ps, bps = nps()
                        for cc in range(4):
                            S.op("pe", lambda e, ps=ps, cc=cc, blk=blk: e.matmul(ps[:, cc * 128:(cc + 1) * 128], lhsT=kb[:, cc, blk * 128:(blk + 1) * 128], rhs=ident[:], start=True, stop=True),
                                 reads=[b_kb, B_const], writes=[bps] if cc == 0 else (), joins=() if cc == 0 else [bps])
                        tch = (pc * pw) // 128 + blk
                        S.op("act" if blk % 2 == 0 else "dve",
                             (lambda e, ps=ps, tch=tch: e.copy(out=kTM[:, tch, :], in_=ps[:, 0:512])) if blk % 2 == 0 else (lambda e, ps=ps, tch=tch: e.tensor_copy(out=kTM[:, tch, :], in_=ps[:, 0:512])),
                             reads=[bps], joins=[b_kTM])
                S.op("dve", lambda e: e.tensor_reduce(out=nrm[:], in_=nacc[:], axis=AX.X, op=ALU.add), reads=[b_nacc], writes=[b_rn])
                S.op("dve", lambda e: e.reciprocal(out=nrm[:], in_=nrm[:]), reads=[b_rn], writes=[b_rn])
                S.op("dve", lambda e: e.tensor_copy(out=rnh[:], in_=nrm[:]), reads=[b_rn], writes=[b_rn])
                S.op("dve", lambda e: e.tensor_copy(out=rnhf[:], in_=rnh[:]), reads=[b_rn], writes=[b_rn])
                S.op("dve", lambda e: e.tensor_tensor(out=rnhf[:], in0=nrm[:], in1=rnhf[:], op=ALU.subtract), reads=[b_rn], writes=[b_rn])
                S.op("dve", lambda e: e.tensor_copy(out=rnl[:], in_=rnhf[:]), reads=[b_rn], writes=[b_rn])
                for cc in range(4):
                    S.op("dve", lambda e, cc=cc: e.tensor_scalar(out=dg[:, 0, cc, :], in0=identf[:], scalar1=nrm[:, cc:cc + 1], scalar2=None, op0=ALU.mult), reads=[b_rn, B_const], writes=[b_rn])
                    S.op("dve", lambda e, cc=cc: e.tensor_scalar(out=dg[:, 1, cc, :], in0=identf[:], scalar1=rnhf[:, cc:cc + 1], scalar2=None, op0=ALU.mult), reads=[b_rn, B_const], writes=[b_rn])
                ps, bps = nps()
                for cc in range(4):
                    S.op("pe", lambda e, ps=ps, cc=cc: e.matmul(ps[:, cc * 128:(cc + 1) * 128], lhsT=ones[:], rhs=dg[:, 0, cc, :], start=True, stop=False), reads=[b_rn, B_const], writes=[bps] if cc == 0 else (), joins=() if cc == 0 else [bps])
                    S.op("pe", lambda e, ps=ps, cc=cc: e.matmul(ps[:, cc * 128:(cc + 1) * 128], lhsT=ones[:], rhs=dg[:, 1, cc, :], start=False, stop=True), reads=[b_rn], joins=[bps])
                S.op("act", lambda e, ps=ps: e.copy(out=rnbc[:], in_=ps[:, 0:512]), reads=[bps], writes=[b_rn])

                S.op("dve", lambda e: e.tensor_tensor(out=kTM[:, 0:nt, :], in0=kTM[:, 0:nt, :], in1=kTM[:, nt:2 * nt, :], op=ALU.add), reads=[b_kTM], writes=[b_kTM])
                S.op("dve", lambda e: e.scalar_tensor_tensor(out=kTM[:, nt:2 * nt, :], in0=kTM[:, nt:2 * nt, :], scalar=-2.0, in1=kTM[:, 0:nt, :], op0=ALU.mult, op1=ALU.add), reads=[b_kTM], writes=[b_kTM])

                def ldf(m):
                    sl = m % 2
                    S.dma(lambda e: e.dma_start(out=fwt[sl][:], in_=FWc[L][m]), b_fwt[sl], writes=[b_fwt[sl]])
                ldf(0)
                for m in range(nt):
                    sl = m % 2
                    if m + 1 < nt:
                        ldf(m + 1)
                    pss = [nps() for _ in range(2)]
                    hf = 0 if m < max(nt // 2, 1) else 1
                    for ri in range(2):
                        ps, bps = pss[ri]
                        for tau in range(nt):
                            S.op("pe", lambda e, ps=ps, sl=sl, ri=ri, tau=tau, hf=hf: e.matmul(ps[:, 0:512], lhsT=fwt[sl][:, ri, tau, :], rhs=kTM[:, hf * nt + tau, :], start=(tau == 0), stop=(tau == nt - 1)),
                                 reads=[b_fwt[sl], b_kTM], writes=[bps] if tau == 0 else (), joins=() if tau == 0 else [bps])
                    for ri in range(2):
                        p1, b1 = pss[ri]
                        eb, beb = ev[ri * 2 + 1], b_ev[ri * 2 + 1]
                        S.op("act", lambda e, p1=p1, eb=eb: e.copy(out=eb[:], in_=p1[:, 0:512]), reads=[b1], writes=[beb])
                    S.op("pool", lambda e: e.tensor_tensor(out=tabA[:], in0=ev[1][:], in1=rnbc[:], op=ALU.mult), reads=[b_ev[1], b_rn], writes=[b_tA])
                    S.op("pool", lambda e: e.tensor_tensor(out=tabC[:], in0=ev[3][:], in1=rnbc[:], op=ALU.mult), reads=[b_ev[3], b_rn], writes=[b_tC])
                    if m == 0:
                        S.op("dve", lambda e: e.tensor_scalar(out=tabD[:], in0=tabC[:], scalar1=m1v[:, 0:1], scalar2=None, op0=ALU.mult), reads=[b_tC, B_const], writes=[b_tD])
                        S.op("dve", lambda e: e.scalar_tensor_tensor(out=tabD[:], in0=tabA[:], scalar=m0v[:, 0:1], in1=tabD[:], op0=ALU.mult, op1=ALU.add), reads=[b_tA], writes=[b_tD])
                        S.op("dve", lambda e: e.tensor_scalar(out=tabC[:], in0=tabC[:], scalar1=m0v[:, 0:1], scalar2=None, op0=ALU.mult), reads=[b_tD], writes=[b_tC])
                        S.dma(lambda e: e.dma_start(out=TD[L], in_=tabD[:]), b_tD, reads=[b_tD], writes=[B_tab[L]])
                    S.dma(lambda e, m=m: e.dma_start(out=TA[L][m], in_=tabA[:]), b_tA, reads=[b_tA], joins=[B_tab[L]])
                    S.dma(lambda e, m=m: e.dma_start(out=TC[L][m], in_=tabC[:]), b_tC, reads=[b_tC], joins=[B_tab[L]])
                phase_end(loc)

        def phase_P(l):
            with ExitStack() as st:
                xt = [sbt(st, "xt%d" % i, [128, 8, 512], F32) for i in range(2)]
                hT = [sbt(st, "hT%d" % i, [128, 8, 512], BF16) for i in range(2)]
                Pt = sbt(st, "Pt", [128, NCH, 512], BF16)
                kvf = sbt(st, "kvf", [128, 4, 256], F32)
                vbt = sbt(st, "vbt", [128, 4, 128], BF16)
                tm = norm_tmps(st, "P", 512)
                b_xt = [S.buf("xt0"), S.buf("xt1")]
                b_hT = [S.buf("hT0"), S.buf("hT1")]
                b_Pt = [S.buf("Pt%d" % g) for g in range(5)]
                b_kvf = S.buf("kvf")
                b_vbt = S.buf("vbt")
                alltiles = []
                for si, (base, L, ctx) in enumerate(SEQS):
                    for (t0, n) in tiles_of(base, L):
                        alltiles.append((si, t0, n, ctx))
                first = {"P": True, "V": True}

                def ldx(i):
                    si, t0, n, ctx = alltiles[i]
                    sl = i % 2
                    for c0 in range(0, 8, 4):
                        S.dma(lambda e, c0=c0: e.dma_start(out=xt[sl][:, c0:c0 + 4, 0:n], in_=xT[c0:c0 + 4, :, t0:t0 + n].rearrange("c p t -> p c t")), b_xt[sl],
                              reads=[B_xT], writes=[b_xt[sl]] if c0 == 0 else (), joins=() if c0 == 0 else [b_xt[sl]])

                def do_norm(i):
                    si, t0, n, ctx = alltiles[i]
                    sl = i % 2
                    s = 0 if ctx else 1
                    rms_norm(tm, xt[sl], b_xt[sl], n, GP[:, l, s, :], SH[:, l, s, :], hT[sl], b_hT[sl])

                def do_tile(i):
                    si, t0, n, ctx = alltiles[i]
                    sl = i % 2
                    h = hT[sl]
                    bh = b_hT[sl]
                    for ch in range(NCH):
                        g = ch // 5
                        ps, bps = nps()
                        for kc in range(8):
                            S.op("pe", lambda e, ps=ps, ch=ch, kc=kc: e.matmul(ps[:, 0:n], lhsT=Wh["b"][:, kc, ch * 128:(ch + 1) * 128], rhs=h[:, kc, 0:n], start=(kc == 0), stop=(kc == 7)),
                                 reads=[B_W, bh], writes=[bps] if kc == 0 else (), joins=() if kc == 0 else [bps])
                        wr = dict(writes=[b_Pt[g]]) if ch % 5 == 0 else dict(joins=[b_Pt[g]])
                        if ch in GATE_CH:
                            S.op("act", lambda e, ps=ps, ch=ch: e.activation(out=Pt[:, ch, 0:n], in_=ps[:, 0:n], func=AF.Silu), reads=[bps], **wr)
                        elif ch % 2 == 0:
                            S.op("dve", lambda e, ps=ps, ch=ch: e.tensor_copy(out=Pt[:, ch, 0:n], in_=ps[:, 0:n]), reads=[bps], **wr)
                        else:
                            S.op("act", lambda e, ps=ps, ch=ch: e.copy(out=Pt[:, ch, 0:n], in_=ps[:, 0:n]), reads=[bps], **wr)
                        if ch % 5 == 4:
                            c0 = ch - 4
                            S.dma(lambda e, c0=c0: e.dma_start(out=Pscr[c0:c0 + 5, :, t0:t0 + n].rearrange("c p t -> p c t"), in_=Pt[:, c0:c0 + 5, 0:n]), b_Pt[g],
                                  reads=[b_Pt[g]], writes=[B_P] if first["P"] else (), joins=() if first["P"] else [B_P])
                            first["P"] = False
                    nbk = n // 128
                    for bk in range(nbk):
                        ps, bps = nps()
                        for kc in range(8):
                            S.op("pe", lambda e, ps=ps, bk=bk, kc=kc: e.matmul(ps[:, 0:256], lhsT=h[:, kc, bk * 128:(bk + 1) * 128], rhs=Wh["kv"][:, kc, :], start=(kc == 0), stop=(kc == 7)),
                                 reads=[B_W, bh], writes=[bps] if kc == 0 else (), joins=() if kc == 0 else [bps])
                        wv = dict(writes=[b_vbt]) if bk == 0 else dict(joins=[b_vbt])
                        if ctx:
                            S.op("act", lambda e, ps=ps, bk=bk: e.copy(out=vbt[:, bk, :], in_=ps[:, 128:256]), reads=[bps], **wv)
                        else:
                            wk = dict(writes=[b_kvf]) if bk == 0 else dict(joins=[b_kvf])
                            S.op("dve", lambda e, ps=ps, bk=bk: e.tensor_copy(out=kvf[:, bk, :], in_=ps[:, 0:256]), reads=[bps], **wk)
                            S.op("pool", lambda e, bk=bk: e.tensor_copy(out=vbt[:, bk, :], in_=kvf[:, bk, 128:256]), reads=[b_kvf], **wv)
                    S.dma(lambda e: e.dma_start(out=Vscr[t0:t0 + n, :].rearrange("(b p) d -> p b d", p=128), in_=vbt[:, 0:nbk, :]), b_vbt,
                          reads=[b_vbt], writes=[B_V] if first["V"] else (), joins=() if first["V"] else [B_V])
                    first["V"] = False
                    if not ctx:
                        j = si - 1
                        S.dma(lambda e: e.dma_start(out=nk_out[j, l].rearrange("(b p) d -> p b d", p=128), in_=kvf[:, 0:nbk, 0:128]), b_kvf, reads=[b_kvf], joins=[B_out])
                        S.dma(lambda e: e.dma_start(out=nv_out[j, l].rearrange("(b p) d -> p b d", p=128), in_=kvf[:, 0:nbk, 128:256]), b_kvf, reads=[b_kvf], joins=[B_out])

                ldx(0)
                do_norm(0)
                for i in range(len(alltiles)):
                    if i + 1 < len(alltiles):
                        ldx(i + 1)
                        do_norm(i + 1)
                    do_tile(i)
                    if _os_mod.environ.get('KBAR_P'):
                        S.barrier()
                phase_end(b_xt + b_hT + b_Pt + [b_kvf, b_vbt, tm["b_sq"], tm["b_rstd"], tm["b_tmpn"]])

        class _View3:
            def __init__(self, t, n):
                self.t = t
                self.n = n

            def __getitem__(self, key):
                if not isinstance(key, tuple):
                    key = (key,)
                key = list(key) + [slice(None)] * (3 - len(key))
                if isinstance(key[2], slice) and key[2] == slice(None):
                    key[2] = slice(0, self.n)
                return self.t[tuple(key)]

        def phase_H(l, sis):
            base, L, ctx = SEQS[sis[0]]
            nt = L // 128
            n = min(512, L)
            with ExitStack() as st:
                XT = sbt(st, "XT", [128, nt, 256], BF16)
                Y = sbt(st, "Y", [128, 2 * nt, 256], BF16)
                cw = sbt(st, "cw", [128, 6, 3], F32)
                dbc = sbt(st, "dbc", [128, 2, 256], F32)
                b_XT = S.buf("XT")
                b_Y = S.buf("Y")
                b_cw = S.buf("cw")
                loc = [b_XT, b_Y, b_cw]
                for k3 in range(3):
                    S.dma(lambda e, k3=k3: e.dma_start(out=cw[:, :, k3:k3 + 1], in_=hy_cw[l, k3:k3 + 1, :].rearrange("k (c p) -> p c k", p=128), allow_slow_non_contiguous=True), b_cw, joins=[b_cw])
                for o in range(2):
                    S.dma(lambda e, o=o: e.dma_start(out=dbc[:, o, :], in_=hy_d[l, o:o + 1, :].partition_broadcast(128)), b_cw, joins=[b_cw])
                for si_ in sis:
                    base = SEQS[si_][0]
                    with ExitStack() as st1:
                        hr = [sbt(st1, "hr%d" % i, [128, 8, n + 2], BF16) for i in range(2)]
                        hc = sbt(st1, "hc", [128, 8, n], BF16)
                        hacc = sbt(st1, "hacc", [128, 6, n], F32)
                        tmo = sbt(st1, "tmo", [128, n // 128, 768], BF16)
                        b_hr = [S.buf("hr0"), S.buf("hr1")]
                        b_hc = S.buf("hc")
                        b_hacc = S.buf("hacc")
                        b_tmo = S.buf("tmo")
                        tl = tiles_of(base, L)
                        firstHX = [True]

                        def ldh(i):
                            t0, nn = tl[i]
                            sl = i % 2
                            lo = max(t0 - 1, base)
                            hi = min(t0 + nn + 1, base + L)
                            S.dma(lambda e: e.dma_start(out=hr[sl][:, 0:4, lo - (t0 - 1):hi - (t0 - 1)], in_=Pscr[CHY:CHY + 4, :, lo:hi].rearrange("c p t -> p c t")), b_hr[sl], reads=[B_P], writes=[b_hr[sl]])
                            if lo == t0:
                                S.op("pool", lambda e: e.memset(hr[sl][:, :, 0:1], 0.0), joins=[b_hr[sl]])
                            if hi == t0 + nn:
                                S.op("pool", lambda e: e.memset(hr[sl][:, :, nn + 1:nn + 2], 0.0), joins=[b_hr[sl]])
                            S.dma(lambda e: e.dma_start(out=hr[sl][:, 4:8, lo - (t0 - 1):hi - (t0 - 1)], in_=Pscr[CHY + 4:CHY + 8, :, lo:hi].rearrange("c p t -> p c t")), b_hr[sl], reads=[B_P], joins=[b_hr[sl]])
                        ldh(0)
                        for i, (t0, nn) in enumerate(tl):
                            sl = i % 2
                            if i + 1 < len(tl):
                                ldh(i + 1)
                            for c in range(6):
                                eng = "dve"
                                S.op(eng, lambda e, c=c, sl=sl: e.tensor_scalar(out=hacc[:, c, :], in0=hr[sl][:, c, 1:nn + 1], scalar1=cw[:, c, 1:2], scalar2=None, op0=ALU.mult), reads=[b_hr[sl], b_cw], writes=[b_hacc] if c == 0 else (), joins=() if c == 0 else [b_hacc])
                                S.op(eng, lambda e, c=c, sl=sl: e.scalar_tensor_tensor(out=hacc[:, c, :], in0=hr[sl][:, c, 0:nn], scalar=cw[:, c, 0:1], in1=hacc[:, c, :], op0=ALU.mult, op1=ALU.add), reads=[b_hr[sl], b_hacc], joins=[b_hacc])
                                S.op(eng, lambda e, c=c, sl=sl: e.scalar_tensor_tensor(out=hc[:, c, :], in0=hr[sl][:, c, 2:nn + 2], scalar=cw[:, c, 2:3], in1=hacc[:, c, :], op0=ALU.mult, op1=ALU.add), reads=[b_hr[sl], b_hacc], writes=[b_hc] if c == 0 else (), joins=() if c == 0 else [b_hc])
                            S.op("pool", lambda e, sl=sl: e.tensor_copy(out=hc[:, 6:8, :], in_=hr[sl][:, 6:8, 1:nn + 1]), reads=[b_hr[sl]], joins=[b_hc])
                            for bk in range(nn // 128):
                                pa, ba = nps()
                                pb, bb = nps()
                                for c in range(8):
                                    ps_, bp_ = (pa, ba) if c < 4 else (pb, bb)
                                    cc = c % 4
                                    S.op("pe", lambda e, ps_=ps_, c=c, cc=cc, bk=bk: e.matmul(ps_[:, cc * 128:(cc + 1) * 128], lhsT=hc[:, c, bk * 128:(bk + 1) * 128], rhs=ident[:], start=True, stop=True),
                                         reads=[b_hc, B_const], writes=[bp_] if cc == 0 else (), joins=() if cc == 0 else [bp_])
                                tch = (t0 - base) // 128 + bk
                                S.op("act", lambda e, pa=pa, tch=tch: e.copy(out=XT[:, tch, :], in_=pa[:, 0:256]), reads=[ba], joins=[b_XT])
                                S.op("act", lambda e, pa=pa, bk=bk: e.copy(out=tmo[:, bk, 0:256], in_=pa[:, 256:512]), reads=[ba], writes=[b_tmo] if bk == 0 else (), joins=() if bk == 0 else [b_tmo])
                                S.op("dve", lambda e, pb=pb, bk=bk: e.tensor_copy(out=tmo[:, bk, 256:768], in_=pb[:, 0:512]), reads=[bb], joins=[b_tmo])
                            S.dma(lambda e, t0=t0, nn=nn: e.dma_start(out=HX[t0:t0 + nn, :].rearrange("(b p) d -> p b d", p=128), in_=tmo[:, 0:nn // 128, :]), b_tmo,
                                  reads=[b_tmo], writes=[B_HX] if firstHX[0] else (), joins=() if firstHX[0] else [B_HX])
                            firstHX[0] = False
                        S.barrier()
                        S.emit()
                        S.drop(b_hr + [b_hc, b_hacc, b_tmo])
                    with ExitStack() as st2:
                        fwt = [sbt(st2, "hfw%d" % i, [128, 2, nt, 128], BF16) for i in range(2)]
                        ivt = [fwt[i][:].rearrange("p r t j -> p (r t) j") for i in range(2)]
                        ta = [sbt(st2, "ta%d" % i, [128, 256], F32) for i in range(2)]
                        tc_ = [sbt(st2, "tc%d" % i, [128, 256], F32) for i in range(2)]
                        td = sbt(st2, "td", [128, 256], F32)
                        er = sbt(st2, "er", [128, 256], F32)
                        ei = sbt(st2, "ei", [128, 256], F32)
                        t1 = sbt(st2, "t1", [128, 256], F32)
                        t2 = sbt(st2, "t2", [128, 256], F32)
                        t3 = sbt(st2, "t3", [128, 256], F32)
                        t4 = sbt(st2, "t4", [128, 256], F32)
                        hx = [sbt(st2, "hx%d" % i, [128, 768], BF16) for i in range(2)]
                        u1 = sbt(st2, "u1", [128, 256], F32)
                        zt = sbt(st2, "zt", [128, 256], BF16)
                        zo = sbt(st2, "zo", [128, 2, 512], BF16)
                        b_fw = [S.buf("hfw0"), S.buf("hfw1")]
                        b_iv = b_fw
                        b_ta = [S.buf("ta0"), S.buf("ta1")]
                        b_td = S.buf("td")
                        b_e = S.buf("e")
                        b_t12 = S.buf("t12")
                        b_t34 = S.buf("t34")
                        b_hx = [S.buf("hx0"), S.buf("hx1")]
                        b_u1 = S.buf("u1")
                        b_zt = S.buf("zt")
                        b_zo = S.buf("zo")
                        loc2 = b_fw + b_ta + b_hx + [b_td, b_e, b_t12, b_t34, b_u1, b_zt, b_zo]
                        firstZ = [True]
                        for o in range(2):
                            S.dma(lambda e, o=o: e.dma_start(out=td[:], in_=TD[L][:, o * 256:(o + 1) * 256]), b_td, reads=[B_tab[L]], writes=[b_td])

                            def ldfw(m, o=o):
                                sl = m % 2
                                S.dma(lambda e: e.dma_start(out=fwt[sl][:], in_=FWc[L][m]), b_fw[sl], writes=[b_fw[sl]])
                                S.dma(lambda e: e.dma_start(out=ta[sl][:], in_=TA[L][m, :, o * 256:(o + 1) * 256]), b_ta[sl], reads=[B_tab[L]], writes=[b_ta[sl]])
                                S.dma(lambda e: e.dma_start(out=tc_[sl][:], in_=TC[L][m, :, o * 256:(o + 1) * 256]), b_ta[sl], reads=[B_tab[L]], joins=[b_ta[sl]])
                            ldfw(0)
                            for m in range(nt):
                                sl = m % 2
                                if m + 1 < nt:
                                    ldfw(m + 1)
                                (pr, bpr), (pi_, bpi) = nps(), nps()
                                for ri, (ps, bps) in enumerate(((pr, bpr), (pi_, bpi))):
                                    for tau in range(nt):
                                        S.op("pe", lambda e, ps=ps, sl=sl, ri=ri, tau=tau: e.matmul(ps[:, 0:256], lhsT=fwt[sl][:, ri, tau, :], rhs=XT[:, tau, :], start=(tau == 0), stop=(tau == nt - 1)),
                                             reads=[b_fw[sl], b_XT], writes=[bps] if tau == 0 else (), joins=() if tau == 0 else [bps])
                                S.op("act", lambda e, pr=pr: e.copy(out=er[:], in_=pr[:, 0:256]), reads=[bpr], writes=[b_e])
                                S.op("act", lambda e, pi_=pi_: e.copy(out=ei[:], in_=pi_[:, 0:256]), reads=[bpi], joins=[b_e])
                                dt_ = td if m == 0 else ta[sl]
                                rd = [b_td] if m == 0 else []
                                S.op("dve", lambda e, sl=sl: e.tensor_tensor(out=t1[:], in0=er[:], in1=ta[sl][:], op=ALU.mult), reads=[b_e, b_ta[sl]], writes=[b_t12])
                                S.op("dve", lambda e, sl=sl: e.tensor_tensor(out=t2[:], in0=ei[:], in1=tc_[sl][:], op=ALU.mult), reads=[b_e, b_ta[sl]], joins=[b_t12])
                                S.op("dve", lambda e, m=m: e.tensor_tensor(out=Y[:, m, :], in0=t1[:], in1=t2[:], op=ALU.subtract), reads=[b_t12], joins=[b_Y])
                                S.op("pool", lambda e, sl=sl: e.tensor_tensor(out=t3[:], in0=er[:], in1=tc_[sl][:], op=ALU.mult), reads=[b_e, b_ta[sl]], writes=[b_t34])
                                S.op("pool", lambda e, dt_=dt_: e.tensor_tensor(out=t4[:], in0=ei[:], in1=dt_[:], op=ALU.mult), reads=[b_e, b_ta[sl]] + rd, joins=[b_t34])
                                S.op("pool", lambda e, m=m: e.tensor_tensor(out=Y[:, nt + m, :], in0=t3[:], in1=t4[:], op=ALU.add), reads=[b_t34], joins=[b_Y])

                            def ldiv(tau, o=o):
                                sl = tau % 2
                                S.dma(lambda e: e.dma_start(out=ivt[sl], in_=IVc[L][tau]), b_iv[sl], writes=[b_iv[sl]])
                                S.dma(lambda e: e.dma_start(out=hx[sl][:], in_=HX[base + tau * 128:base + (tau + 1) * 128, :]), b_hx[sl], reads=[B_HX], writes=[b_hx[sl]])
                            ldiv(0)
                            for tau in range(nt):
                                sl = tau % 2
                                if tau + 1 < nt:
                                    ldiv(tau + 1)
                                py, bpy = nps()
                                for r in range(2 * nt):
                                    S.op("pe", lambda e, py=py, sl=sl, r=r: e.matmul(py[:, 0:256], lhsT=ivt[sl][:, r, :], rhs=Y[:, r, :], start=(r == 0), stop=(r == 2 * nt - 1)),
                                         reads=[b_iv[sl], b_Y], writes=[bpy] if r == 0 else (), joins=() if r == 0 else [bpy])
                                S.op("dve", lambda e, tau=tau, o=o: e.tensor_tensor(out=u1[:], in0=XT[:, tau, :], in1=dbc[:, o, :], op=ALU.mult), reads=[b_XT, b_cw], writes=[b_u1])
                                S.op("dve", lambda e, py=py: e.tensor_tensor(out=u1[:], in0=u1[:], in1=py[:, 0:256], op=ALU.add), reads=[bpy, b_u1], writes=[b_u1])
                                if o == 0:
                                    S.op("dve", lambda e, tau=tau, sl=sl: e.tensor_tensor(out=XT[:, tau, :], in0=u1[:], in1=hx[sl][:, 0:256], op=ALU.mult), reads=[b_u1, b_hx[sl]], joins=[b_XT])
                                else:
                                    S.op("dve", lambda e, sl=sl: e.tensor_tensor(out=u1[:], in0=u1[:], in1=hx[sl][:, 256:512], op=ALU.mult), reads=[b_u1, b_hx[sl]], writes=[b_u1])
                                    S.op("pool", lambda e, sl=sl: e.tensor_tensor(out=zt[:], in0=u1[:], in1=hx[sl][:, 512:768], op=ALU.mult), reads=[b_u1, b_hx[sl]], writes=[b_zt])
                                    pz, bpz = nps()
                                    for cc in range(2):
                                        S.op("pe", lambda e, pz=pz, cc=cc: e.matmul(pz[:, cc * 128:(cc + 1) * 128], lhsT=zt[:, cc * 128:(cc + 1) * 128], rhs=ident[:], start=True, stop=True),
                                             reads=[b_zt, B_const], writes=[bpz] if cc == 0 else (), joins=() if cc == 0 else [bpz])
                                    q4 = tau % 4
                                    S.op("act", lambda e, pz=pz, q4=q4: e.copy(out=zo[:, :, q4 * 128:(q4 + 1) * 128], in_=pz[:, 0:256].rearrange("p (c t) -> p c t", c=2)), reads=[bpz], writes=[b_zo] if q4 == 0 else (), joins=() if q4 == 0 else [b_zo])
                                    if q4 == 3 or tau == nt - 1:
                                        w = (q4 + 1) * 128
                                        t0 = base + (tau - q4) * 128
                                        S.dma(lambda e, w=w, t0=t0: e.dma_start(out=Zscr[:, :, t0:t0 + w].rearrange("c p t -> p c t"), in_=zo[:, :, 0:w]), b_zo,
                                              reads=[b_zo], writes=[B_Z] if firstZ[0] else (), joins=() if firstZ[0] else [B_Z])
                                        firstZ[0] = False
                        S.barrier()
                        S.emit()
                        S.drop(loc2)
                phase_end(loc)

        def phase_C(l, sis):
            base, L, ctx = SEQS[sis[0]]
            n = min(512, L)
            nbq = n // 128
            s = 0 if ctx else 1
            W = n + 256 if ctx else L
            nkb = W // 128
            with ExitStack() as st:
                xt_2 = [sbt(st, "cxt", [128, 8, n], F32) for _ in range(2)]
                qraw_2 = [sbt(st, "qraw", [128, 4, n], BF16) for _ in range(2)]
                qrot_2 = [sbt(st, "qrot", [128, 4, n], BF16) for _ in range(2)]
                kraw_2 = [sbt(st, "kraw", [128, W], BF16) for _ in range(2)]
                krot_2 = [sbt(st, "krot", [128, W], BF16) for _ in range(2)]
                rc_1 = sbt(st, "rc", [128, W], F32)
                rc_2 = [rc_1, rc_1]
                rs_1 = sbt(st, "rs", [128, W], F32)
                rs_2 = [rs_1, rs_1]
                rt1 = sbt(st, "rt1", [128, 512], F32)
                rt2 = sbt(st, "rt2", [128, 512], F32)
                vt_2 = [sbt(st, "vt", [128, nkb, 128], BF16) for _ in range(2)]
                ga_2 = [sbt(st, "ga", [128, 4, n], BF16) for _ in range(2)]
                scb_2 = [sbt(st, "scb", [128, 6, n + 2], BF16) for _ in range(2)]
                gcz_2 = [sbt(st, "gcz", [128, 4, n], BF16) for _ in range(2)]
                mixT = sbt(st, "mixT", [128, 8, n], BF16)
                attf = sbt(st, "attf", [128, 4, n], F32)
                PT = [sbt(st, "PT%d" % i, [128, 512], BF16) for i in range(10)]
                rden = sbt(st, "rden", [64, 512], F32)
                sinkt = sbt(st, "sinkt", [64, 2, 512], F32)
                esk = sbt(st, "esk", [64, 8], F32)
                scw = sbt(st, "scw", [128, 2, 3], F32)
                su = sbt(st, "su", [128, 2, n + 2], F32)
                sacc = sbt(st, "sacc", [128, 2, n], F32)
                b_xt_2, b_q_2, b_k_2, b_rope_2, b_vt_2, b_ga_2, b_scb_2, b_gcz_2 = [[S.buf(x + "0"), S.buf(x + "1")] for x in ("cxt", "q", "k", "rope", "vt", "ga", "scb", "gcz")]
                b_rope_2 = [b_rope_2[0]] * 2
                b_rt, b_mix, b_att = [S.buf(x) for x in ("rt", "mix", "att")]
                b_PT = [S.buf("PT%d" % i) for i in range(10)]
                b_rden, b_sink, b_scw, b_su, b_sacc, b_xn = [S.buf(x) for x in ("rden", "sink", "scw", "su", "sacc", "xn")]
                b_rt2 = S.buf("rt2")
                loc = [b_rt2, b_rt, b_mix, b_att, b_rden, b_sink, b_scw, b_su, b_sacc, b_xn] + b_PT + b_xt_2 + b_q_2 + b_k_2 + b_rope_2 + b_vt_2 + b_ga_2 + b_scb_2 + b_gcz_2
                if ctx:
                    cks = sbt(st, "cks", [128, 2, 128], F32)
                    ckb = sbt(st, "ckb", [128, 2, 128], BF16)
                    ckT = sbt(st, "ckT", [128, 256], BF16)
                    cvs = sbt(st, "cvs", [128, 2, 128], F32)
                    cvb = sbt(st, "cvb", [128, 2, 128], BF16)
                    b_ck = S.buf("ck")
                    loc.append(b_ck)
                    S.dma(lambda e: e.dma_start(out=cks[:], in_=ck_in[l].rearrange("(b p) d -> p b d", p=128)), b_ck, writes=[b_ck])
                    S.dma(lambda e: e.dma_start(out=cvs[:], in_=cv_in[l].rearrange("(b p) d -> p b d", p=128)), b_ck, joins=[b_ck])
                    S.op("dve", lambda e: e.tensor_copy(out=ckb[:], in_=cks[:]), reads=[b_ck], joins=[b_ck])
                    S.op("dve", lambda e: e.tensor_copy(out=cvb[:], in_=cvs[:]), reads=[b_ck], joins=[b_ck])
                    ps, bps = nps()
                    for b in range(2):
                        S.op("pe", lambda e, ps=ps, b=b: e.matmul(ps[:, b * 128:(b + 1) * 128], lhsT=ckb[:, b, :], rhs=ident[:], start=True, stop=True), reads=[b_ck, B_const], writes=[bps] if b == 0 else (), joins=() if b == 0 else [bps])
                    S.op("act", lambda e, ps=ps: e.copy(out=ckT[:], in_=ps[:, 0:256]), reads=[bps], joins=[b_ck])
                S.dma(lambda e: e.dma_start(out=esk[:], in_=sink_in[l:l + 1, :].partition_broadcast(64)), b_sink, writes=[b_sink])
                S.op("act", lambda e: e.activation(out=esk[:], in_=esk[:], func=AF.Exp), reads=[b_sink], writes=[b_sink])
                for g in range(8):
                    h, j = g // 4, g % 4
                    S.op("dve", lambda e, g=g, h=h, j=j: e.tensor_scalar(out=sinkt[:, h, j * 128:(j + 1) * 128], in0=zeros[0:64, 0:128], scalar1=esk[:, g:g + 1], scalar2=None, op0=ALU.add), reads=[b_sink, B_const], joins=[b_sink])
                for k3 in range(3):
                    S.dma(lambda e, k3=k3: e.dma_start(out=scw[:, :, k3:k3 + 1], in_=sc_cw[l, k3:k3 + 1, :].rearrange("k (c p) -> p c k", p=128), allow_slow_non_contiguous=True), b_scw, joins=[b_scw])

                def loads(ti, base, t0, nn):
                    sl = ti % 2
                    xt, qraw, qrot, kraw, krot, rc, rs, vt, ga, scb, gcz = [x[sl] for x in (xt_2, qraw_2, qrot_2, kraw_2, krot_2, rc_2, rs_2, vt_2, ga_2, scb_2, gcz_2)]
                    b_xt, b_q, b_k, b_rope, b_vt, b_ga, b_scb, b_gcz = [x[sl] for x in (b_xt_2, b_q_2, b_k_2, b_rope_2, b_vt_2, b_ga_2, b_scb_2, b_gcz_2)]
                    lo = max(t0 - 1, base)
                    hi = min(t0 + nn + 1, base + L)
                    if ctx:
                        k0 = t0 - 128
                        klo = max(k0, base)
                        khi = min(t0 + nn + 128, base + L)
                    else:
                        k0, klo, khi = base, base, base + L
                    for c0 in range(0, 8, 4):
                        S.dma(lambda e, c0=c0: e.dma_start(out=xt[:, c0:c0 + 4, :], in_=xT[c0:c0 + 4, :, t0:t0 + nn].rearrange("c p t -> p c t")), b_xt, reads=[B_xT], writes=[b_xt] if c0 == 0 else (), joins=() if c0 == 0 else [b_xt])
                    S.dma(lambda e: e.dma_start(out=qraw[:], in_=Pscr[CQ:CQ + 4, :, t0:t0 + nn].rearrange("c p t -> p c t")), b_q, reads=[B_P], writes=[b_q])
                    S.dma(lambda e: e.dma_start(out=ga[:], in_=Pscr[CGA:CGA + 4, :, t0:t0 + nn].rearrange("c p t -> p c t")), b_ga, reads=[B_P], writes=[b_ga])
                    S.dma(lambda e: e.dma_start(out=gcz[:, 0:2, :], in_=Pscr[CGC:CGC + 2, :, t0:t0 + nn].rearrange("c p t -> p c t")), b_gcz, reads=[B_P], writes=[b_gcz])
                    S.dma(lambda e: e.dma_start(out=gcz[:, 2:4, :], in_=Zscr[:, :, t0:t0 + nn].rearrange("c p t -> p c t")), b_gcz, reads=[B_Z], joins=[b_gcz])
                    lo = max(t0 - 1, base)
                    hi = min(t0 + nn + 1, base + L)
                    S.dma(lambda e: e.dma_start(out=scb[:, :, lo - (t0 - 1):hi - (t0 - 1)], in_=Pscr[CSB:CSB + 6, :, lo:hi].rearrange("c p t -> p c t")), b_scb, reads=[B_P], writes=[b_scb])
                    if lo == t0:
                        S.op("pool", lambda e: e.memset(scb[:, :, 0:1], 0.0), joins=[b_scb])
                    if hi == t0 + nn:
                        S.op("pool", lambda e: e.memset(scb[:, :, nn + 1:nn + 2], 0.0), joins=[b_scb])
                    if ctx:
                        k0 = t0 - 128
                        klo = max(k0, base)
                        khi = min(t0 + nn + 128, base + L)
                    else:
                        k0, klo, khi = base, base, base + L
                    S.dma(lambda e: e.dma_start(out=kraw[:, klo - k0:khi - k0], in_=Pscr[CK, :, klo:khi]), b_k, reads=[B_P], writes=[b_k])
                    S.dma(lambda e: e.dma_start(out=vt[:, (klo - k0) // 128:(khi - k0) // 128, :], in_=Vscr[klo:khi, :].rearrange("(b p) d -> p b d", p=128)), b_vt, reads=[B_V], writes=[b_vt])

                def compute(ti, base, t0, nn):
                    sl = ti % 2
                    xt, qraw, qrot, kraw, krot, rc, rs, vt, ga, scb, gcz = [x[sl] for x in (xt_2, qraw_2, qrot_2, kraw_2, krot_2, rc_2, rs_2, vt_2, ga_2, scb_2, gcz_2)]
                    b_xt, b_q, b_k, b_rope, b_vt, b_ga, b_scb, b_gcz = [x[sl] for x in (b_xt_2, b_q_2, b_k_2, b_rope_2, b_vt_2, b_ga_2, b_scb_2, b_gcz_2)]
                    lo = max(t0 - 1, base)
                    hi = min(t0 + nn + 1, base + L)
                    if ctx:
                        k0 = t0 - 128
                        klo = max(k0, base)
                        khi = min(t0 + nn + 128, base + L)
                    else:
                        k0, klo, khi = base, base, base + L
                    if ctx:
                        S.dma(lambda e: e.dma_start(out=rc[:, klo - k0:khi - k0], in_=ropeC[:, klo - base:khi - base]), b_rope, writes=[b_rope])
                        S.dma(lambda e: e.dma_start(out=rs[:, klo - k0:khi - k0], in_=ropeS[:, klo - base:khi - base]), b_rope, joins=[b_rope])

                        def rope(src_ap, dst_ap, c_ap, s_ap, wdt, rb, wb, first):
                            ps, bps = nps()
                            S.op("pe", lambda e: e.matmul(ps[:, 0:wdt], lhsT=pmat[:], rhs=src_ap, start=True, stop=True), reads=[rb, B_const], writes=[bps])
                            S.op("dve", lambda e: e.tensor_tensor(out=rt1[:, 0:wdt], in0=ps[:, 0:wdt], in1=s_ap, op=ALU.mult), reads=[bps, b_rope], writes=[b_rt])
                            S.op("pool", lambda e: e.tensor_tensor(out=rt2[:, 0:wdt], in0=src_ap, in1=c_ap, op=ALU.mult), reads=[rb, b_rope], writes=[b_rt2])
                            S.op("dve", lambda e: e.tensor_tensor(out=dst_ap, in0=rt1[:, 0:wdt], in1=rt2[:, 0:wdt], op=ALU.add), reads=[b_rt, b_rt2], joins=[wb])
                        for c in range(4):
                            rope(qraw[:, c, :], qrot[:, c, :], rc[:, 128:128 + nn], rs[:, 128:128 + nn], nn, b_q, b_q, False)
                        a0, a1 = klo - k0, khi - k0
                        for off in range(a0, a1, 512):
                            wdt = min(512, a1 - off)
                            rope(kraw[:, off:off + wdt], krot[:, off:off + wdt], rc[:, off:off + wdt], rs[:, off:off + wdt], wdt, b_k, b_k, False)
                    S.op("dve", lambda e: e.tensor_tensor(out=su[:], in0=scb[:, 2:4, :], in1=scb[:, 4:6, :], op=ALU.mult), reads=[b_scb], writes=[b_su])
                    for c in range(2):
                        eng = "dve"
                        S.op(eng, lambda e, c=c: e.tensor_scalar(out=sacc[:, c, :], in0=su[:, c, 1:nn + 1], scalar1=scw[:, c, 1:2], scalar2=None, op0=ALU.mult), reads=[b_su, b_scw], writes=[b_sacc] if c == 0 else (), joins=() if c == 0 else [b_sacc])
                        S.op(eng, lambda e, c=c: e.scalar_tensor_tensor(out=sacc[:, c, :], in0=su[:, c, 0:nn], scalar=scw[:, c, 0:1], in1=sacc[:, c, :], op0=ALU.mult, op1=ALU.add), reads=[b_su, b_sacc], joins=[b_sacc])
                        S.op(eng, lambda e, c=c: e.scalar_tensor_tensor(out=sacc[:, c, :], in0=su[:, c, 2:nn + 2], scalar=scw[:, c, 2:3], in1=sacc[:, c, :], op0=ALU.mult, op1=ALU.add), reads=[b_su, b_sacc], joins=[b_sacc])
                    S.op("dve", lambda e: e.tensor_tensor(out=sacc[:], in0=sacc[:], in1=scb[:, 0:2, 1:nn + 1], op=ALU.mult), reads=[b_scb, b_sacc], writes=[b_sacc])
                    S.op("dve", lambda e: e.tensor_tensor(out=mixT[:, 6:8, :], in0=sacc[:], in1=gcz[:, 0:2, :], op=ALU.mult), reads=[b_sacc, b_gcz], writes=[b_mix])
                    S.op("pool", lambda e: e.tensor_copy(out=mixT[:, 4:6, :], in_=gcz[:, 2:4, :]), reads=[b_gcz], joins=[b_mix])
                    for qb in range(nn // 128):
                        nblk = (t0 - base) // 128 + qb
                        for h in range(2):
                            hp = slice(64 * h, 64 * h + 64)
                            blocks = []
                            if ctx:
                                for d, msk in ((-1, mlo), (0, None), (1, mhi)):
                                    kb_ = nblk + d
                                    if kb_ < 0 or kb_ >= L // 128:
                                        continue
                                    w0 = (kb_ * 128 + base) - k0
                                    blocks.append((krot[hp, w0:w0 + 128], qrot, msk, vt[:, w0 // 128, hp]))
                                for cb_ in range(2):
                                    blocks.append((ckT[hp, cb_ * 128:(cb_ + 1) * 128], qraw, None, cvb[:, cb_, hp]))
                            else:
                                for kb_ in range(L // 128):
                                    blocks.append((kraw[hp, kb_ * 128:(kb_ + 1) * 128], qraw, None, vt[:, kb_, hp]))
                            pts = []
                            for bi, (kap, qt, msk, vap) in enumerate(blocks):
                                ps, bps = nps()
                                for j in range(4):
                                    S.op("pe", lambda e, ps=ps, kap=kap, qt=qt, j=j, qb=qb, hp=hp: e.matmul(ps[:, j * 128:(j + 1) * 128], lhsT=kap, rhs=qt[hp, j, qb * 128:(qb + 1) * 128], start=True, stop=True),
                                         reads=[b_k, b_q] + ([b_ck] if ctx else []), writes=[bps] if j == 0 else (), joins=() if j == 0 else [bps])
                                pt, bpt = PT[((qb * 2 + h) % 2) * 5 + bi], b_PT[((qb * 2 + h) % 2) * 5 + bi]
                                S.op("act", lambda e, ps=ps, pt=pt: e.activation(out=pt[:], in_=ps[:], func=AF.Exp, scale=0.125), reads=[bps], writes=[bpt])
                                if msk is not None:
                                    for j in range(4):
                                        eng = "dve" if j % 2 == 0 else "pool"
                                        S.op(eng, lambda e, pt=pt, msk=msk, j=j: e.tensor_tensor(out=pt[:, j * 128:(j + 1) * 128], in0=pt[:, j * 128:(j + 1) * 128], in1=msk[:], op=ALU.mult), reads=[B_const, bpt], joins=[bpt])
                                pts.append((pt, bpt, vap))
                            po, bpo = nps()
                            pd, bpd = nps()
                            for bi, (pt, bpt, vap) in enumerate(pts):
                                S.op("pe", lambda e, po=po, pt=pt, vap=vap, bi=bi, last=(bi == len(pts) - 1): e.matmul(po[0:64, :], lhsT=vap, rhs=pt[:], start=(bi == 0), stop=last),
                                     reads=[bpt, b_vt] + ([b_ck] if ctx else []), writes=[bpo] if bi == 0 else (), joins=() if bi == 0 else [bpo])
                            for bi, (pt, bpt, vap) in enumerate(pts):
                                S.op("pe", lambda e, pd=pd, pt=pt, bi=bi, last=(bi == len(pts) - 1): e.matmul(pd[0:64, :], lhsT=ones[:, 0:64], rhs=pt[:], start=(bi == 0), stop=last),
                                     reads=[bpt, B_const], writes=[bpd] if bi == 0 else (), joins=() if bi == 0 else [bpd])
                            S.op("dve", lambda e, pd=pd, h=h: e.tensor_tensor(out=rden[:], in0=pd[0:64, :], in1=sinkt[:, h, :], op=ALU.add), reads=[bpd, b_sink], writes=[b_rden])
                            S.op("dve", lambda e: e.reciprocal(out=rden[:], in_=rden[:]), reads=[b_rden], writes=[b_rden])
                            S.op("dve", lambda e, po=po, hp=hp, qb=qb: e.tensor_tensor(out=attf[hp, :, qb * 128:(qb + 1) * 128], in0=po[0:64, :].rearrange("p (j t) -> p j t", j=4), in1=rden[:].rearrange("p (j t) -> p j t", j=4), op=ALU.mult),
                                 reads=[bpo, b_rden], joins=[b_att])
                    S.op("pool", lambda e: e.tensor_tensor(out=mixT[:, 0:4, :], in0=attf[:], in1=ga[:], op=ALU.mult), reads=[b_att, b_ga], joins=[b_mix])
                    _kbr = _os_mod.environ.get('KBR')
                    if _kbr:
                        if 'attn' not in _kbr:
                            S.op("pool", lambda e: e.memset(mixT[:, 0:4, :], 0.0), writes=[b_mix])
                        if 'hy' not in _kbr:
                            S.op("pool", lambda e: e.memset(mixT[:, 4:6, :], 0.0), writes=[b_mix])
                        if 'sc' not in _kbr:
                            S.op("pool", lambda e: e.memset(mixT[:, 6:8, :], 0.0), writes=[b_mix])
                    for oc in range(8):
                        ps, bps = nps()
                        for kc in range(8):
                            S.op("pe", lambda e, ps=ps, oc=oc, kc=kc: e.matmul(ps[:, 0:nn], lhsT=Wh["o"][:, kc, oc * 128:(oc + 1) * 128], rhs=mixT[:, kc, :], start=(kc == 0), stop=(kc == 7)),
                                 reads=[B_W, b_mix], writes=[bps] if kc == 0 else (), joins=() if kc == 0 else [bps])
                        S.op("dve", lambda e, ps=ps, oc=oc: e.scalar_tensor_tensor(out=xt[:, oc, :], in0=ps[:, 0:nn], scalar=GT[:, l, s, oc:oc + 1], in1=xt[:, oc, :], op0=ALU.mult, op1=ALU.add),
                             reads=[bps, b_xt, B_mod], joins=[b_xt])
                    for c0 in range(0, 8, 4):
                        S.dma(lambda e, c0=c0: e.dma_start(out=xT[c0:c0 + 4, :, t0:t0 + nn].rearrange("c p t -> p c t"), in_=xt[:, c0:c0 + 4, :]), b_xt, reads=[b_xt], joins=[B_xT])
                tl = [(SEQS[si_][0], t0_, nn_) for si_ in sis for (t0_, nn_) in tiles_of(SEQS[si_][0], L)]
                loads(0, *tl[0])
                for ti, (bs_, t0, nn) in enumerate(tl):
                    if ti + 1 < len(tl):
                        loads(ti + 1, *tl[ti + 1])
                    compute(ti, bs_, t0, nn)
                    if _os_mod.environ.get('KBAR_C'):
                        S.barrier()
                phase_end(loc)

        def phase_F():
            with ExitStack() as st:
                xt = [sbt(st, "fxt%d" % i, [128, 8, 512], F32) for i in range(2)]
                yT = sbt(st, "fyT", [128, 8, 512], F32)
                yhi = sbt(st, "fyhi", [128, 8, 512], BF16)
                ylo = sbt(st, "fylo", [128, 8, 512], BF16)
                yo = [sbt(st, "fyo%d" % i, [128, 4, 1024], F32) for i in range(2)]
                tm = norm_tmps(st, "F", 512)
                ytmp = tm["tmpn"]
                b_xt = [S.buf("fxt0"), S.buf("fxt1")]
                b_yT, b_yhl = S.buf("fyT"), S.buf("fyhl")
                b_yo = [S.buf("fyo0"), S.buf("fyo1")]
                alltiles = []
                for (base, L, ctx) in SEQS:
                    alltiles += tiles_of(base, L)

                def ldx(i):
                    t0, n = alltiles[i]
                    sl = i % 2
                    for c0 in range(0, 8, 4):
                        S.dma(lambda e, c0=c0: e.dma_start(out=xt[sl][:, c0:c0 + 4, 0:n], in_=xT[c0:c0 + 4, :, t0:t0 + n].rearrange("c p t -> p c t")), b_xt[sl],
                              reads=[B_xT], writes=[b_xt[sl]] if c0 == 0 else (), joins=() if c0 == 0 else [b_xt[sl]])

                def do_tile(i):
                    t0, n = alltiles[i]
                    sl = i % 2
                    rms_norm(tm, xt[sl], b_xt[sl], n, FG, None, yT, b_yT)
                    S.op("act", lambda e: e.copy(out=yhi[:, :, 0:n], in_=yT[:, :, 0:n]), reads=[b_yT], writes=[b_yhl])
                    S.op("pool", lambda e: e.tensor_copy(out=ytmp[:, :, 0:n], in_=yhi[:, :, 0:n]), reads=[b_yhl], writes=[tm["b_tmpn"]])
                    S.op("pool", lambda e: e.tensor_tensor(out=ylo[:, :, 0:n], in0=yT[:, :, 0:n], in1=ytmp[:, :, 0:n], op=ALU.subtract), reads=[b_yT, b_yhl, tm["b_tmpn"]], joins=[b_yhl])
                    for bk in range(n // 128):
                        for half in range(2):
                            ps, bps = nps()
                            for c in range(4):
                                cc = half * 4 + c
                                S.op("pe", lambda e, ps=ps, c=c, cc=cc, bk=bk: e.matmul(ps[:, c * 128:(c + 1) * 128], lhsT=yhi[:, cc, bk * 128:(bk + 1) * 128], rhs=ident[:], start=True, stop=False), reads=[b_yhl, B_const], writes=[bps] if c == 0 else (), joins=() if c == 0 else [bps])
                                S.op("pe", lambda e, ps=ps, c=c, cc=cc, bk=bk: e.matmul(ps[:, c * 128:(c + 1) * 128], lhsT=ylo[:, cc, bk * 128:(bk + 1) * 128], rhs=ident[:], start=False, stop=True), reads=[b_yhl], joins=[bps])
                            wr = dict(writes=[b_yo[sl]]) if (bk == 0 and half == 0) else dict(joins=[b_yo[sl]])
                            if half == 0:
                                S.op("act", lambda e, ps=ps, bk=bk: e.copy(out=yo[sl][:, bk, 0:512], in_=ps[:]), reads=[bps], **wr)
                            else:
                                S.op("dve", lambda e, ps=ps, bk=bk: e.tensor_copy(out=yo[sl][:, bk, 512:1024], in_=ps[:]), reads=[bps], **wr)
                    S.dma(lambda e: e.dma_start(out=y_out[t0:t0 + n, :].rearrange("(b p) f -> p b f", p=128), in_=yo[sl][:, 0:n // 128, :]), b_yo[sl], reads=[b_yo[sl]], joins=[B_out])

                ldx(0)
                for i in range(len(alltiles)):
                    if i + 1 < len(alltiles):
                        ldx(i + 1)
                    do_tile(i)
                    if _os_mod.environ.get('KBAR_F'):
                        S.barrier()
                phase_end(b_xt + b_yo + [b_yT, b_yhl, tm["b_sq"], tm["b_rstd"], tm["b_tmpn"]])

        class _Stop(Exception):
            pass

        def chk(name):
            if stop_after == name:
                raise _Stop()
        try:
            chk("0")
            for l in range(depth):
                filter_phase(l, 4096)
                chk("f4096")
                filter_phase(l, 256)
                chk("f256")
                with ExitStack() as wstk:
                    Wh["o"] = sbt(wstk, "Wo", [128, 8, 1024], BF16)
                    with ExitStack() as wstk2:
                        Wh["b"] = sbt(wstk2, "Wb", [128, 8, NCH * 128], BF16)
                        Wh["kv"] = sbt(wstk2, "Wkv", [128, 8, 256], BF16)
                        load_weights(l)
                        chk("W")
                        phase_P(l)
                        chk("P")
                    phase_H(l, [0])
                    chk("H0")
                    phase_H(l, [1, 2, 3, 4])
                    chk("H1")
                    phase_C(l, [0])
                    chk("C0")
                    phase_C(l, [1, 2, 3, 4])
                    chk("C1")
            phase_F()
        except _Stop:
            pass
        S.barrier()
        S.emit()
    return nc


_CONST_CACHE = {}


def _consts():
    if not _CONST_CACHE:
        fw4096, iv4096 = dft_consts(4096)
        fw256, iv256 = dft_consts(256)
        fe4096, de4096 = filt_consts(4096)
        fe256, de256 = filt_consts(256)
        C, Sn, PM = rope_consts()
        _CONST_CACHE.update(dict(fw4096=fw4096, iv4096=iv4096, fw256=fw256, iv256=iv256, fe4096=fe4096, de4096=de4096,
                                 fe256=fe256, de256=de256, ropeC=C, ropeS=Sn, permm=PM))
    return _CONST_CACHE


def kernel(x_prompt, x_sample, cache_k, cache_v, c, c_ctx, norm_g, mod_w, mod_b, w_in, attn_sink,
           hy_conv_w, hy_filt_w1, hy_filt_b1, hy_filt_w2, hy_filt_b2, hy_filt_w3, hy_filt_freq, hy_d,
           sc_conv_w, w_out, final_g, _depth=4):
    f = lambda a: np.ascontiguousarray(np.asarray(a, dtype=np.float32))
    x_prompt, x_sample, cache_k, cache_v, c, c_ctx = map(f, (x_prompt, x_sample, cache_k, cache_v, c, c_ctx))
    cols, rows, dperm = col_perm()
    w_in = f(w_in)
    w_in_p = np.ascontiguousarray(w_in[:, :, cols])
    w_kv = np.ascontiguousarray(w_in[:, :, 512:768])
    w_out_p = np.ascontiguousarray(f(w_out)[:, rows, :])
    cst = _consts()
    shared = dict(norm_g=f(norm_g), mod_w=f(mod_w), mod_b=f(mod_b), w_in=w_in_p, w_kv=w_kv, sink=f(attn_sink),
                  hy_cw=f(hy_conv_w), f_w1=f(hy_filt_w1), f_b1=f(hy_filt_b1), f_w2=f(hy_filt_w2), f_b2=f(hy_filt_b2),
                  f_w3=f(hy_filt_w3), f_fq=f(hy_filt_freq), hy_d=f(hy_d), sc_cw=f(sc_conv_w), w_out=w_out_p,
                  final_g=f(final_g))
    shared.update(cst)
    in_maps = []
    for r in range(8):
        xin = np.concatenate([x_sample[r], x_prompt[4 * r:4 * r + 4].reshape(1024, 1024)], 0)
        ck = cache_k[r][:, :, :, dperm].reshape(4, 256, 128)
        cv = cache_v[r].reshape(4, 256, 128)
        cond = np.stack([c[r], c_ctx], 1)
        m = dict(x_tm=np.ascontiguousarray(xin), ck=np.ascontiguousarray(ck), cv=np.ascontiguousarray(cv), cond=np.ascontiguousarray(cond))
        m.update(shared)
        in_maps.append(m)
    import os
    nc = build(_depth, os.environ.get('KSTOP'))
    ncores = int(os.environ.get('KCORES', '8'))
    res = run_bass_kernel_spmd(nc, in_maps[:ncores], core_ids=list(range(ncores)))
    if ncores < 8:
        res.results.extend([res.results[0]] * (8 - ncores))
    y_prompt = np.zeros((32, 256, 1024), np.float32)
    y_sample = np.zeros((8, 4096, 1024), np.float32)
    nk = np.zeros((32, 4, 256, 2, 64), np.float32)
    nv = np.zeros((32, 4, 256, 2, 64), np.float32)
    for r in range(8):
        o = res.results[r]
        y_sample[r] = o["y"][0:4096]
        y_prompt[4 * r:4 * r + 4] = o["y"][4096:].reshape(4, 256, 1024)
        nk[4 * r:4 * r + 4] = o["nk"].reshape(4, 4, 256, 2, 64)
        nv[4 * r:4 * r + 4] = o["nv"].reshape(4, 4, 256, 2, 64)
    return (y_prompt, y_sample, nk, nv)
```

```python
import math
import os as _os_mod
from contextlib import ExitStack
import numpy as np
import ml_dtypes
import concourse.bass as bass
import concourse.mybir as mybir
from concourse.bass_utils import run_bass_kernel_spmd

F32 = mybir.dt.float32
BF16 = mybir.dt.bfloat16
ALU = mybir.AluOpType
AF = mybir.ActivationFunctionType
AX = mybir.AxisListType
NPBF = ml_dtypes.bfloat16

T = 5120
SEQS = [(0, 4096, True)] + [(4096 + 256 * j, 256, False) for j in range(4)]
CQ, CK, CGA, CHY, CGH, CSB, CSC, CSX, CGC, NCH = 0, 4, 5, 9, 15, 17, 19, 21, 23, 25
GATE_CH = set(range(CGA, CGA + 4)) | set(range(CGH, CGH + 2)) | set(range(CGC, CGC + 2))
PI = math.pi


class Buf:
    __slots__ = ("name", "w", "r", "sem", "cnt", "g")

    def __init__(self, name):
        self.name = name
        self.w = {}
        self.r = {}
        self.sem = None
        self.cnt = 0
        self.g = {}


def _merge(d, tok):
    k = id(tok[0])
    if k not in d or d[k][1] < tok[1]:
        d[k] = tok


class Sched:
    ENG = ("pe", "act", "dve", "pool", "sp")
    STRICT = ("act", "dve", "pool")

    def __init__(self, nc, es):
        self.nc = nc
        self.es = es
        self.q = {e: [] for e in self.ENG}
        self.csem = {e: es.enter_context(nc.semaphore("c_" + e)) for e in self.ENG}
        self.cnt = {e: 0 for e in self.ENG}
        self.waited = {e: {} for e in self.ENG}
        self.nsem = 0
        self.bufs = []
        self.sempool = []

    def buf(self, name):
        b = Buf(name)
        self.bufs.append(b)
        return b

    def drop(self, bufs):
        ids = set(id(b) for b in bufs)
        seen = set()
        for b in bufs:
            if id(b) in seen:
                continue
            seen.add(id(b))
            if b.sem is not None:
                self.sempool.append((b.sem, b.cnt))
        self.bufs = [b for b in self.bufs if id(b) not in ids]

    def _deps(self, eng, reads, writes, joins):
        deps = {}
        for b in reads:
            for t in b.w.values():
                _merge(deps, t)
        for b in writes:
            for t in b.w.values():
                _merge(deps, t)
            for t in b.r.values():
                _merge(deps, t)
        for b in joins:
            for t in b.r.values():
                _merge(deps, t)
            for t in b.g.values():
                _merge(deps, t)
        out = []
        own = id(self.csem[eng])
        wd = self.waited[eng]
        for k, (s, v) in deps.items():
            if k == own and eng not in self.STRICT:
                continue
            if wd.get(k, -1) >= v:
                continue
            wd[k] = v
            out.append((s, v))
        return out

    def _record(self, tok, reads, writes, joins):
        for b in reads:
            _merge(b.r, tok)
        for b in writes:
            g = dict(b.w)
            for t in b.r.values():
                _merge(g, t)
            b.g = g
            b.w = {}
            b.r = {}
            _merge(b.w, tok)
        for b in joins:
            _merge(b.w, tok)

    def op(self, eng, fn, reads=(), writes=(), joins=()):
        waits = self._deps(eng, reads, writes, joins)
        self.cnt[eng] += 1
        tok = (self.csem[eng], self.cnt[eng])
        self.q[eng].append((waits, fn, self.csem[eng], 1))
        self._record(tok, reads, writes, joins)
        return tok

    def dma(self, fn, sembuf, reads=(), writes=(), joins=(), q="sp"):
        waits = self._deps(q, reads, writes, joins)
        if sembuf.sem is None:
            if self.sempool:
                sembuf.sem, sembuf.cnt = self.sempool.pop()
            else:
                sembuf.sem = self.es.enter_context(self.nc.semaphore("d%d" % self.nsem))
                self.nsem += 1
        sembuf.cnt += 16
        tok = (sembuf.sem, sembuf.cnt)
        self.q[q].append((waits, fn, sembuf.sem, 16))
        self._record(tok, reads, writes, joins)
        return tok

    def barrier(self):
        deps = {}
        for b in self.bufs:
            for t in list(b.w.values()) + list(b.r.values()):
                _merge(deps, t)
        for e in self.ENG:
            if self.cnt[e] > 0:
                _merge(deps, (self.csem[e], self.cnt[e]))
        for eng in self.ENG:
            waits = []
            wd = self.waited[eng]
            own = id(self.csem[eng])
            for k, (s, v) in deps.items():
                if k == own:
                    continue
                if wd.get(k, -1) >= v:
                    continue
                wd[k] = v
                waits.append((s, v))
            self.q[eng].append((waits, None, None, 0))

    def emit(self):
        nc = self.nc
        qs = self.q
        self.q = {e: [] for e in self.ENG}

        def run(name):
            def body(e):
                for (waits, fn, sem, inc) in qs[name]:
                    for (s, v) in waits:
                        e.wait_ge(s, v)
                    if fn is not None:
                        ins = fn(e)
                        ins.then_inc(sem, inc)
            return body

        with nc.Block() as block:
            block.tensor(run("pe"))
            block.scalar(run("act"))
            block.vector(run("dve"))
            block.gpsimd(run("pool"))
            block.sync(run("sp"))


def _bf(a):
    return np.asarray(a, dtype=np.float32).astype(NPBF)


def dft_consts(L):
    N = 2 * L
    nt = L // 128
    tab = np.arange(N, dtype=np.float64) * (2 * np.pi / N)
    ctab, stab = np.cos(tab), np.sin(tab)
    t = np.arange(L)
    k = np.arange(L)
    tk = (t[:, None] * k[None, :]) % N
    fre = ctab[tk]
    fim = -stab[tk]
    fim[:, 0] = np.where(t % 2 == 0, 1.0, -1.0)
    kperm = np.concatenate([np.arange(0, L, 2), np.arange(1, L, 2)])
    fre = fre[:, kperm]
    fim = fim[:, kperm]
    fw = np.stack([fre, fim], 0).reshape(2, nt, 128, nt, 128)
    fw = np.ascontiguousarray(fw.transpose(3, 2, 0, 1, 4))
    ire = (2.0 / N) * ctab[tk].T
    ire[0, :] = 1.0 / N
    iim = -(2.0 / N) * stab[tk].T
    iim[0, :] = (1.0 / N) * np.where(t % 2 == 0, 1.0, -1.0)
    ire = ire[kperm, :]
    iim = iim[kperm, :]
    iv = np.concatenate([ire, iim], 0).reshape(2 * nt, 128, nt, 128)
    iv = np.ascontiguousarray(iv.transpose(2, 1, 0, 3))
    return _bf(fw), _bf(iv)


def filt_consts(L):
    N = 2 * L
    f32 = np.float32
    tl = np.linspace(0.0, 1.0, L, dtype=f32)
    bands = 16
    ang = (f32(2.0 * math.pi / L) * np.arange(L, dtype=f32))
    fr = np.linspace(1e-4, bands - 1, bands, dtype=f32)
    feats = np.concatenate([tl[:, None], np.cos(fr[None, :] * ang[:, None]), -np.sin(fr[None, :] * ang[:, None])], -1).astype(f32)
    n = np.arange(N)
    lag = np.where(n < L, n, N - n)
    lag[L] = 0
    fk = feats[lag].T.astype(f32)
    hi = fk.astype(NPBF)
    lo = (fk - hi.astype(f32)).astype(NPBF)
    featc = np.ascontiguousarray(np.stack([hi, lo], 1))
    max_decay = math.log(1e-2) / 0.3
    min_decay = math.log(1e-2) / 1.5
    deltas = np.abs(np.linspace(min_decay, max_decay, 256, dtype=f32))
    dec = np.exp(-tl[lag][None, :] * deltas[:, None]).astype(f32)
    dec[:, L] = 0.0
    return featc, np.ascontiguousarray(dec.reshape(2, 128, N))


def rope_consts():
    Ls = 4096
    t = np.arange(Ls)
    inv = (10000.0 ** (-np.arange(16, dtype=np.float32) / 16)).astype(np.float32)
    row = (t // 64).astype(np.float32)[:, None] * inv
    col = (t % 64).astype(np.float32)[:, None] * inv
    ang = np.stack([row, col], 1)
    C = np.zeros((128, Ls), np.float32)
    S = np.zeros((128, Ls), np.float32)
    for p in range(128):
        pos = p % 64
        half = (pos % 32) // 16
        f = pos % 16
        C[p] = np.cos(ang[:, half, f])
        S[p] = np.sin(ang[:, half, f])
    PM = np.zeros((128, 128), np.float32)
    for m in range(128):
        pos = m % 64
        if pos < 32:
            PM[m + 32, m] = -1.0
        else:
            PM[m - 32, m] = 1.0
    return C, S, _bf(PM)


def col_perm():
    dperm = np.zeros(64, np.int64)
    for pos in range(64):
        ab, half, f = pos // 32, (pos % 32) // 16, pos % 16
        dperm[pos] = half * 32 + ab * 16 + f
    cols = []
    for j in range(4):
        cols += list(0 + j * 64 + dperm) + list(0 + (4 + j) * 64 + dperm)
    cols += list(512 + dperm) + list(512 + 64 + dperm)
    for j in range(4):
        cols += list(768 + j * 64 + np.arange(64)) + list(768 + (4 + j) * 64 + np.arange(64))
    cols += list(1280 + np.arange(768))
    cols += list(2048 + np.arange(256))
    cols += list(2304 + np.arange(768))
    cols += list(3072 + np.arange(256))
    rows = []
    for j in range(4):
        rows += list(j * 64 + np.arange(64)) + list((4 + j) * 64 + np.arange(64))
    rows += list(512 + np.arange(512))
    return np.array(cols), np.array(rows), dperm


def build(depth=4, stop_after=None):
    nc = bass.Bass("TRN2", target_bir_lowering=False)

    def din(name, shape, dt=F32):
        return nc.dram_tensor(name, list(shape), dt, kind="ExternalInput").ap()

    def dout(name, shape):
        return nc.dram_tensor(name, list(shape), F32, kind="ExternalOutput").ap()

    def dscr(name, shape, dt):
        return nc.dram_tensor(name, list(shape), dt, kind="Internal").ap()

    x_tm = din("x_tm", [T, 1024])
    ck_in = din("ck", [4, 256, 128])
    cv_in = din("cv", [4, 256, 128])
    cond_in = din("cond", [1024, 2])
    norm_g = din("norm_g", [4, 1024])
    mod_w = din("mod_w", [4, 1024, 3072])
    mod_b = din("mod_b", [4, 3072])
    w_in = din("w_in", [4, 1024, NCH * 128])
    w_kv = din("w_kv", [4, 1024, 256])
    sink_in = din("sink", [4, 8])
    hy_cw = din("hy_cw", [4, 3, 768])
    f_w1 = din("f_w1", [4, 33, 64])
    f_b1 = din("f_b1", [4, 64])
    f_w2 = din("f_w2", [4, 64, 64])
    f_b2 = din("f_b2", [4, 64])
    f_w3 = din("f_w3", [4, 64, 1024])
    f_fq = din("f_fq", [4, 64])
    hy_d = din("hy_d", [4, 2, 256])
    sc_cw = din("sc_cw", [4, 3, 256])
    w_out = din("w_out", [4, 1024, 1024])
    final_g = din("final_g", [1024])
    ropeC = din("ropeC", [128, 4096])
    ropeS = din("ropeS", [128, 4096])
    permm = din("permm", [128, 128], BF16)
    FWc = {4096: din("fw4096", [32, 128, 2, 32, 128], BF16), 256: din("fw256", [2, 128, 2, 2, 128], BF16)}
    IVc = {4096: din("iv4096", [32, 128, 64, 128], BF16), 256: din("iv256", [2, 128, 4, 128], BF16)}
    FEc = {4096: din("fe4096", [33, 2, 8192], BF16), 256: din("fe256", [33, 2, 512], BF16)}
    DEc = {4096: din("de4096", [2, 128, 8192]), 256: din("de256", [2, 128, 512])}

    y_out = dout("y", [T, 1024])
    nk_out = dout("nk", [4, 4, 256, 128])
    nv_out = dout("nv", [4, 4, 256, 128])

    xT = dscr("xT", [8, 128, T], F32)
    Pscr = dscr("Pscr", [NCH, 128, T], BF16)
    Vscr = dscr("Vscr", [T, 128], BF16)
    HX = dscr("HX", [T, 768], BF16)
    Zscr = dscr("Zscr", [2, 128, T], BF16)
    TA = {4096: dscr("TA4096", [32, 128, 512], F32), 256: dscr("TA256", [2, 128, 512], F32)}
    TC = {4096: dscr("TC4096", [32, 128, 512], F32), 256: dscr("TC256", [2, 128, 512], F32)}
    TD = {4096: dscr("TD4096", [128, 512], F32), 256: dscr("TD256", [128, 512], F32)}

    with ExitStack() as es:
        S = Sched(nc, es)

        uid = [0]

        def sbt(st, name, shape, dt):
            uid[0] += 1
            return st.enter_context(nc.sbuf_tensor("%s_%d" % (name, uid[0]), list(shape), dt))

        ident = sbt(es, "ident", [128, 128], BF16)
        identf = sbt(es, "identf", [128, 128], F32)
        ones = sbt(es, "ones", [128, 128], BF16)
        mlo = sbt(es, "mlo", [128, 128], BF16)
        mhi = sbt(es, "mhi", [128, 128], BF16)
        mtmp = sbt(es, "mtmp", [128, 128], F32)
        pmat = sbt(es, "pmat", [128, 128], BF16)
        sgn = sbt(es, "sgn", [128, 1], F32)
        m0v = sbt(es, "m0v", [128, 1], F32)
        m1v = sbt(es, "m1v", [128, 1], F32)
        zeros = sbt(es, "zeros", [128, 512], F32)
        SH = sbt(es, "SH", [128, 4, 2, 8], F32)
        GP = sbt(es, "GP", [128, 4, 2, 8], F32)
        GT = sbt(es, "GT", [128, 4, 2, 8], F32)
        FG = sbt(es, "FG", [128, 8], F32)
        Wh = {}
        B_const = S.buf("const")
        B_mod = S.buf("mod")
        B_W = S.buf("W")
        B_xT = S.buf("xT")
        B_P = S.buf("Pscr")
        B_V = S.buf("Vscr")
        B_HX = S.buf("HX")
        B_Z = S.buf("Zscr")
        B_tab = {4096: S.buf("tab4096"), 256: S.buf("tab256")}
        B_out = S.buf("out")

        PS = [es.enter_context(nc.psum_tensor("ps%d" % i, [128, 512], F32)) for i in range(8)]
        BPS = [S.buf("ps%d" % i) for i in range(8)]
        psi = [0]

        def nps():
            i = psi[0] % 8
            psi[0] += 1
            return PS[i], BPS[i]

        rr = [0]

        def alt(engs=("dve", "pool")):
            rr[0] += 1
            return engs[rr[0] % len(engs)]

        def phase_end(st_bufs):
            S.barrier()
            S.emit()
            S.drop(st_bufs)

        S.op("pool", lambda e: e.memset(identf[:], 0.0), writes=[B_const])
        S.op("pool", lambda e: e.affine_select(out=identf[:], in_=identf[:], pattern=[[-1, 128]], compare_op=ALU.not_equal, fill=1.0, base=0, channel_multiplier=1), writes=[B_const])
        S.op("dve", lambda e: e.tensor_copy(out=ident[:], in_=identf[:]), reads=[B_const], joins=[B_const])
        S.op("pool", lambda e: e.memset(ones[:], 1.0), joins=[B_const])
        S.op("pool", lambda e: e.memset(zeros[:], 0.0), joins=[B_const])
        S.op("pool", lambda e: e.memset(mtmp[:], 1.0), joins=[B_const])
        S.op("pool", lambda e: e.affine_select(out=mtmp[:], in_=mtmp[:], pattern=[[-1, 128]], compare_op=ALU.is_ge, fill=0.0, base=0, channel_multiplier=1), joins=[B_const])
        S.op("pool", lambda e: e.tensor_copy(out=mlo[:], in_=mtmp[:]), joins=[B_const])
        S.op("pool", lambda e: e.memset(mtmp[:], 1.0), joins=[B_const])
        S.op("pool", lambda e: e.affine_select(out=mtmp[:], in_=mtmp[:], pattern=[[1, 128]], compare_op=ALU.is_ge, fill=0.0, base=0, channel_multiplier=-1), joins=[B_const])
        S.op("pool", lambda e: e.tensor_copy(out=mhi[:], in_=mtmp[:]), joins=[B_const])
        S.op("pool", lambda e: e.memset(mtmp[:], 1.0), joins=[B_const])
        S.op("pool", lambda e: e.affine_select(out=mtmp[:, 0:1], in_=mtmp[:, 0:1], pattern=[[0, 1]], compare_op=ALU.not_equal, fill=0.0, base=0, channel_multiplier=1), joins=[B_const])
        S.op("pool", lambda e: e.tensor_copy(out=m0v[:], in_=mtmp[:, 0:1]), joins=[B_const])
        S.op("pool", lambda e: e.tensor_scalar(out=m1v[:], in0=m0v[:], scalar1=-1.0, scalar2=1.0, op0=ALU.mult, op1=ALU.add), joins=[B_const])
        S.dma(lambda e: e.dma_start(out=pmat[:], in_=permm), B_const, joins=[B_const])
        S.dma(lambda e: e.dma_start(out=FG[:], in_=final_g.rearrange("(c p) -> p c", p=128), allow_slow_non_contiguous=True), B_const, joins=[B_const])

        with ExitStack() as st:
            condt = sbt(st, "condt", [128, 8, 2], F32)
            condb = sbt(st, "condb", [128, 8, 2], BF16)
            mst = [sbt(st, "mst%d" % i, [128, 8, 384], F32) for i in range(2)]
            msb = [sbt(st, "msb%d" % i, [128, 8, 384], BF16) for i in range(2)]
            modT = sbt(st, "modT", [128, 24, 2], F32)
            mbT = sbt(st, "mbT", [128, 24], F32)
            ngT = sbt(st, "ngT", [128, 8], F32)
            sgnt = sbt(st, "sgnt", [128, 2], F32)
            b_cond = S.buf("cond")
            b_mst = [S.buf("mst0"), S.buf("mst1")]
            b_msb = [S.buf("msb0"), S.buf("msb1")]
            b_modT = S.buf("modT")
            b_mb = S.buf("mb")
            loc = [b_cond, b_modT, b_mb] + b_mst + b_msb
            S.dma(lambda e: e.dma_start(out=condt[:], in_=cond_in.rearrange("(c p) s -> p c s", p=128)), b_cond, writes=[b_cond])
            S.op("act", lambda e: e.activation(out=condt[:], in_=condt[:], func=AF.Silu), reads=[b_cond], writes=[b_cond])
            S.op("dve", lambda e: e.tensor_copy(out=condb[:], in_=condt[:]), reads=[b_cond], joins=[b_cond])
            for l in range(depth):
                S.dma(lambda e, l=l: e.dma_start(out=mbT[:], in_=mod_b[l].rearrange("(c p) -> p c", p=128), allow_slow_non_contiguous=True), b_mb, writes=[b_mb])
                S.dma(lambda e, l=l: e.dma_start(out=ngT[:], in_=norm_g[l].rearrange("(c p) -> p c", p=128), allow_slow_non_contiguous=True), b_mb, joins=[b_mb])
                for cb in range(8):
                    sl = cb % 2
                    for kc in range(0, 8, 4):
                        S.dma(lambda e, l=l, cb=cb, sl=sl, kc=kc: e.dma_start(
                            out=mst[sl][:, kc:kc + 4, :],
                            in_=mod_w[l, kc * 128:(kc + 4) * 128, cb * 384:(cb + 1) * 384].rearrange("(c p) f -> p c f", p=128)),
                            b_mst[sl], writes=[b_mst[sl]] if kc == 0 else (), joins=() if kc == 0 else [b_mst[sl]])
                    S.op(alt(("dve", "pool")), lambda e, sl=sl: e.tensor_copy(out=msb[sl][:], in_=mst[sl][:]), reads=[b_mst[sl]], writes=[b_msb[sl]])
                    for fc in range(3):
                        ps, bps = nps()
                        for kc in range(8):
                            S.op("pe", lambda e, ps=ps, sl=sl, kc=kc, fc=fc: e.matmul(ps[:, 0:2], lhsT=msb[sl][:, kc, fc * 128:(fc + 1) * 128], rhs=condb[:, kc, :], start=(kc == 0), stop=(kc == 7)),
                                 reads=[b_msb[sl], b_cond], writes=[bps] if kc == 0 else (), joins=() if kc == 0 else [bps])
                        S.op("act", lambda e, ps=ps, cb=cb, fc=fc: e.copy(out=modT[:, cb * 3 + fc, :], in_=ps[:, 0:2]), reads=[bps], joins=[b_modT])
                for s in range(2):
                    S.op("dve", lambda e, l=l, s=s: e.tensor_tensor(out=SH[:, l, s, :], in0=modT[:, 0:8, s], in1=mbT[:, 0:8], op=ALU.add), reads=[b_modT, b_mb], joins=[B_mod])
                    S.op("dve", lambda e, l=l, s=s: e.tensor_tensor(out=GP[:, l, s, :], in0=modT[:, 8:16, s], in1=mbT[:, 8:16], op=ALU.add), reads=[b_modT, b_mb], joins=[B_mod])
                    S.op("dve", lambda e, l=l, s=s: e.scalar_tensor_tensor(out=GP[:, l, s, :], in0=GP[:, l, s, :], scalar=1.0, in1=ngT[:], op0=ALU.add, op1=ALU.mult), reads=[b_mb, B_mod], joins=[B_mod])
                    S.op("dve", lambda e, l=l, s=s: e.tensor_tensor(out=GT[:, l, s, :], in0=modT[:, 16:24, s], in1=mbT[:, 16:24], op=ALU.add), reads=[b_modT, b_mb], joins=[B_mod])
            phase_end(loc)

        with ExitStack() as st:
            altrow = sbt(st, "altrow", [128, 128], F32)
            junk = sbt(st, "junk0", [128, 128], F32)
            b_a = S.buf("altrow")
            S.op("pool", lambda e: e.memset(altrow[:], 1.0), writes=[b_a])
            S.op("pool", lambda e: e.memset(altrow[:].rearrange("p (a two) -> p a two", two=2)[:, :, 1:2], -1.0), joins=[b_a])
            S.op("dve", lambda e: e.tensor_tensor(out=junk[:], in0=identf[:], in1=altrow[:], op=ALU.mult), reads=[b_a, B_const], writes=[b_a])
            S.op("dve", lambda e: e.tensor_reduce(out=sgn[:], in_=junk[:], axis=AX.X, op=ALU.add), reads=[b_a], joins=[B_const])
            phase_end([b_a])

        with ExitStack() as st:
            xin = [sbt(st, "xin%d" % i, [128, 1024], F32) for i in range(2)]
            xhi = [sbt(st, "xhi%d" % i, [128, 1024], BF16) for i in range(2)]
            xlo = [sbt(st, "xlo%d" % i, [128, 1024], BF16) for i in range(2)]
            xtmp = [sbt(st, "xtmp%d" % i, [128, 1024], F32) for i in range(2)]
            xo = [sbt(st, "xo%d" % i, [128, 8, 128], F32) for i in range(2)]
            b_in = [S.buf("xin0"), S.buf("xin1")]
            b_hl = [S.buf("xhl0"), S.buf("xhl1")]
            b_xo = [S.buf("xo0"), S.buf("xo1")]
            nb = T // 128
            first = [True]

            def ld0(b):
                sl = b % 2
                S.dma(lambda e: e.dma_start(out=xin[sl][:], in_=x_tm[b * 128:(b + 1) * 128, :]), b_in[sl], writes=[b_in[sl]])
            ld0(0)
            for b in range(nb):
                sl = b % 2
                if b + 1 < nb:
                    ld0(b + 1)
                S.op("act", lambda e, sl=sl: e.copy(out=xhi[sl][:], in_=xin[sl][:]), reads=[b_in[sl]], writes=[b_hl[sl]])
                S.op("dve", lambda e, sl=sl: e.tensor_copy(out=xtmp[sl][:], in_=xhi[sl][:]), reads=[b_hl[sl]], joins=[b_hl[sl]])
                S.op("dve", lambda e, sl=sl: e.tensor_tensor(out=xlo[sl][:], in0=xin[sl][:], in1=xtmp[sl][:], op=ALU.subtract), reads=[b_in[sl], b_hl[sl]], joins=[b_hl[sl]])
                for half in range(2):
                    ps, bps = nps()
                    for c in range(4):
                        cc = half * 4 + c
                        S.op("pe", lambda e, ps=ps, sl=sl, c=c, cc=cc: e.matmul(ps[:, c * 128:(c + 1) * 128], lhsT=xhi[sl][:, cc * 128:(cc + 1) * 128], rhs=ident[:], start=True, stop=False),
                             reads=[b_hl[sl], B_const], writes=[bps] if c == 0 else (), joins=() if c == 0 else [bps])
                        S.op("pe", lambda e, ps=ps, sl=sl, c=c, cc=cc: e.matmul(ps[:, c * 128:(c + 1) * 128], lhsT=xlo[sl][:, cc * 128:(cc + 1) * 128], rhs=ident[:], start=False, stop=True),
                             reads=[b_hl[sl]], joins=[bps])
                    S.op("act" if half == 0 else "dve",
                         (lambda e, ps=ps, sl=sl, half=half: e.copy(out=xo[sl][:, half * 4:(half + 1) * 4, :], in_=ps[:].rearrange("p (c t) -> p c t", c=4))) if half == 0 else
                         (lambda e, ps=ps, sl=sl, half=half: e.tensor_copy(out=xo[sl][:, half * 4:(half + 1) * 4, :], in_=ps[:].rearrange("p (c t) -> p c t", c=4))),
                         reads=[bps], writes=[b_xo[sl]] if half == 0 else (), joins=() if half == 0 else [b_xo[sl]])
                S.dma(lambda e, sl=sl, b=b: e.dma_start(out=xT[:, :, b * 128:(b + 1) * 128].rearrange("c p t -> p c t"), in_=xo[sl][:]), b_xo[sl],
                      reads=[b_xo[sl]], writes=[B_xT] if first[0] else (), joins=() if first[0] else [B_xT])
                first[0] = False
            phase_end(b_in + b_hl + b_xo)

        def load_weights(l):
            with ExitStack() as st:
                wst = [sbt(st, "wst%d" % i, [128, 8, 512], F32) for i in range(2)]
                b_wst = [S.buf("wst0"), S.buf("wst1")]
                jobs = []
                for cb in range(0, NCH * 128, 512):
                    w = min(512, NCH * 128 - cb)
                    jobs.append((w_in, Wh["b"], cb, w))
                jobs.append((w_kv, Wh["kv"], 0, 256))
                for cb in range(0, 1024, 512):
                    jobs.append((w_out, Wh["o"], cb, 512))
                firstw = [True]

                def ldj(i):
                    src, dst, cb, w = jobs[i]
                    sl = i % 2
                    for kc in range(0, 8, 4):
                        S.dma(lambda e, kc=kc: e.dma_start(out=wst[sl][:, kc:kc + 4, 0:w], in_=src[l, kc * 128:(kc + 4) * 128, cb:cb + w].rearrange("(c p) f -> p c f", p=128)),
                              b_wst[sl], writes=[b_wst[sl]] if kc == 0 else (), joins=() if kc == 0 else [b_wst[sl]])
                ldj(0)
                for i, (src, dst, cb, w) in enumerate(jobs):
                    sl = i % 2
                    if i + 1 < len(jobs):
                        ldj(i + 1)
                    eng = ("dve", "pool", "act")[i % 3]
                    if eng == "act":
                        fn = lambda e, sl=sl, dst=dst, cb=cb, w=w: e.copy(out=dst[:, :, cb:cb + w], in_=wst[sl][:, :, 0:w])
                    else:
                        fn = lambda e, sl=sl, dst=dst, cb=cb, w=w: e.tensor_copy(out=dst[:, :, cb:cb + w], in_=wst[sl][:, :, 0:w])
                    S.op(eng, fn, reads=[b_wst[sl]], writes=[B_W] if firstw[0] else (), joins=() if firstw[0] else [B_W])
                    firstw[0] = False
                phase_end(b_wst)

        def rms_norm_tile(st, xt, b_xt, n, gp_ap, sh_ap, hT, b_hT, tagname):
            sq = sbt(st, "sq_" + tagname, [128, 8, n], BF16)
            rstd = sbt(st, "rstd_" + tagname, [128, n], F32)
            tmpn = sbt(st, "tmpn_" + tagname, [128, 8, n], F32)
            b_sq = S.buf("sq")
            b_rstd = S.buf("rstd")
            b_tmpn = S.buf("tmpn")

            def run():
                S.op("act", lambda e: e.activation(out=sq[:], in_=xt[:], func=AF.Square), reads=[b_xt], writes=[b_sq])
                ps, bps = nps()
                for c in range(8):
                    S.op("pe", lambda e, c=c: e.matmul(ps[:, 0:n], lhsT=ones[:], rhs=sq[:, c, :], start=(c == 0), stop=(c == 7)),
                         reads=[b_sq, B_const], writes=[bps] if c == 0 else (), joins=() if c == 0 else [bps])
                S.op("dve", lambda e: e.tensor_scalar(out=rstd[:], in0=ps[:, 0:n], scalar1=1.0 / 1024, scalar2=1e-6, op0=ALU.mult, op1=ALU.add), reads=[bps], writes=[b_rstd])
                S.op("act", lambda e: e.activation(out=rstd[:], in_=rstd[:], func=AF.Sqrt), reads=[b_rstd], writes=[b_rstd])
                S.op("dve", lambda e: e.reciprocal(out=rstd[:], in_=rstd[:]), reads=[b_rstd], writes=[b_rstd])
                for c in range(8):
                    eng = "dve" if c % 2 == 0 else "pool"
                    S.op(eng, lambda e, c=c: e.tensor_tensor(out=tmpn[:, c, :], in0=xt[:, c, :], in1=rstd[:], op=ALU.mult), reads=[b_xt, b_rstd], writes=[b_tmpn] if c == 0 else (), joins=() if c == 0 else [b_tmpn])
                for c in range(8):
                    if sh_ap is None:
                        S.op("act", lambda e, c=c: e.activation(out=hT[:, c, :], in_=tmpn[:, c, :], func=AF.Identity, scale=gp_ap[:, c:c + 1]), reads=[b_tmpn, B_mod, B_const], writes=[b_hT] if c == 0 else (), joins=() if c == 0 else [b_hT])
                    else:
                        S.op("act", lambda e, c=c: e.activation(out=hT[:, c, :], in_=tmpn[:, c, :], func=AF.Identity, scale=gp_ap[:, c:c + 1], bias=sh_ap[:, c:c + 1]), reads=[b_tmpn, B_mod], writes=[b_hT] if c == 0 else (), joins=() if c == 0 else [b_hT])
            return run, [b_sq, b_rstd, b_tmpn]

        def norm_tmps(st, tag, nmax):
            return dict(sq=sbt(st, "sq_" + tag, [128, 8, nmax], BF16), rstd=sbt(st, "rstd_" + tag, [128, nmax], F32), tmpn=sbt(st, "tmpn_" + tag, [128, 8, nmax], F32),
                        b_sq=S.buf("sq"), b_rstd=S.buf("rstd"), b_tmpn=S.buf("tmpn"))

        def rms_norm(tm, xt, b_xt, n, gp_ap, sh_ap, hT, b_hT):
            sq, rstd, tmpn = tm["sq"], tm["rstd"], tm["tmpn"]
            b_sq, b_rstd, b_tmpn = tm["b_sq"], tm["b_rstd"], tm["b_tmpn"]
            S.op("act", lambda e: e.activation(out=sq[:, :, 0:n], in_=xt[:, :, 0:n], func=AF.Square), reads=[b_xt], writes=[b_sq])
            ps, bps = nps()
            for c in range(8):
                S.op("pe", lambda e, c=c: e.matmul(ps[:, 0:n], lhsT=ones[:], rhs=sq[:, c, 0:n], start=(c == 0), stop=(c == 7)),
                     reads=[b_sq, B_const], writes=[bps] if c == 0 else (), joins=() if c == 0 else [bps])
            S.op("dve", lambda e: e.tensor_scalar(out=rstd[:, 0:n], in0=ps[:, 0:n], scalar1=1.0 / 1024, scalar2=1e-6, op0=ALU.mult, op1=ALU.add), reads=[bps], writes=[b_rstd])
            S.op("act", lambda e: e.activation(out=rstd[:, 0:n], in_=rstd[:, 0:n], func=AF.Sqrt), reads=[b_rstd], writes=[b_rstd])
            S.op("dve", lambda e: e.reciprocal(out=rstd[:, 0:n], in_=rstd[:, 0:n]), reads=[b_rstd], writes=[b_rstd])
            for c in range(8):
                eng = "dve" if c % 2 == 0 else "pool"
                S.op(eng, lambda e, c=c: e.tensor_tensor(out=tmpn[:, c, 0:n], in0=xt[:, c, 0:n], in1=rstd[:, 0:n], op=ALU.mult), reads=[b_xt, b_rstd], writes=[b_tmpn] if c == 0 else (), joins=() if c == 0 else [b_tmpn])
            for c in range(8):
                if sh_ap is None:
                    S.op("act", lambda e, c=c: e.activation(out=hT[:, c, 0:n], in_=tmpn[:, c, 0:n], func=AF.Identity, scale=gp_ap[:, c:c + 1]), reads=[b_tmpn, B_mod, B_const], writes=[b_hT] if c == 0 else (), joins=() if c == 0 else [b_hT])
                else:
                    S.op("act", lambda e, c=c: e.activation(out=hT[:, c, 0:n], in_=tmpn[:, c, 0:n], func=AF.Identity, scale=gp_ap[:, c:c + 1], bias=sh_ap[:, c:c + 1]), reads=[b_tmpn, B_mod], writes=[b_hT] if c == 0 else (), joins=() if c == 0 else [b_hT])

        def tiles_of(base, L):
            n = min(512, L)
            return [(base + i * n, n) for i in range(L // n)]

        def filter_phase(l, L):
            N = 2 * L
            nt = L // 128
            pw = min(512, L)
            npc = N // pw
            with ExitStack() as st:
                w1s = sbt(st, "w1s", [33, 64], F32)
                w1b = sbt(st, "w1b", [33, 64], BF16)
                w2s = sbt(st, "w2s", [64, 64], F32)
                w2b = sbt(st, "w2b", [64, 64], BF16)
                w3s = sbt(st, "w3s", [64, 1024], F32)
                w3b = sbt(st, "w3b", [64, 1024], BF16)
                fqv = sbt(st, "fqv", [64, 1], F32)
                fb1 = sbt(st, "fb1", [64, 1], F32)
                fb2 = sbt(st, "fb2", [64, 1], F32)
                fet = [sbt(st, "fet%d" % i, [33, 2, pw], BF16) for i in range(2)]
                dct = [sbt(st, "dct%d" % i, [128, 2, pw], F32) for i in range(2)]
                arg = sbt(st, "arg", [64, pw], F32)
                tmpa = sbt(st, "tmpa", [64, pw], F32)
                h1 = sbt(st, "h1", [64, pw], BF16)
                h2 = sbt(st, "h2", [64, pw], BF16)
                kf = sbt(st, "kf", [128, 4, pw], F32)
                kjunk = sbt(st, "kjunk", [128, pw], F32)
                kb = sbt(st, "kb", [128, 4, pw], BF16)
                nacc = sbt(st, "nacc", [128, 4, npc], F32)
                nrm = sbt(st, "nrm", [128, 4], F32)
                rnh = sbt(st, "rnh", [128, 4], BF16)
                rnhf = sbt(st, "rnhf", [128, 4], F32)
                rnl = sbt(st, "rnl", [128, 4], BF16)
                dg = sbt(st, "dg", [128, 2, 4, 128], BF16)
                rnbc = sbt(st, "rnbc", [128, 512], F32)
                kTM = sbt(st, "kTM", [128, 2 * nt, 512], BF16)
                fwt = [sbt(st, "fwt%d" % i, [128, 2, nt, 128], BF16) for i in range(2)]
                ev = [sbt(st, "ev%d" % i, [128, 512], F32) for i in range(4)]
                tabA = sbt(st, "tabA", [128, 512], F32)
                tabC = sbt(st, "tabC", [128, 512], F32)
                tabD = sbt(st, "tabD", [128, 512], F32)
                b_w = S.buf("fw_w")
                b_fet = [S.buf("fet0"), S.buf("fet1")]
                b_dct = [S.buf("dct0"), S.buf("dct1")]
                b_arg = S.buf("arg")
                b_tmpa = S.buf("tmpa")
                b_h1 = S.buf("h1")
                b_h2 = S.buf("h2")
                b_kf = S.buf("kf")
                b_kb = S.buf("kb")
                b_nacc = S.buf("nacc")
                b_rn = S.buf("rn")
                b_kTM = S.buf("kTM")
                b_fwt = [S.buf("fwt0"), S.buf("fwt1")]
                b_ev = [S.buf("ev%d" % i) for i in range(4)]
                b_tA, b_tC, b_tD = S.buf("tabA"), S.buf("tabC"), S.buf("tabD")
                loc = [b_w, b_arg, b_tmpa, b_h1, b_h2, b_kf, b_kb, b_nacc, b_rn, b_kTM, b_tA, b_tC, b_tD] + b_fet + b_dct + b_fwt + b_ev
                S.dma(lambda e: e.dma_start(out=w1s[:], in_=f_w1[l]), b_w, writes=[b_w])
                S.dma(lambda e: e.dma_start(out=w2s[:], in_=f_w2[l]), b_w, joins=[b_w])
                S.dma(lambda e: e.dma_start(out=w3s[:], in_=f_w3[l]), b_w, joins=[b_w])
                S.dma(lambda e: e.dma_start(out=fqv[:], in_=f_fq[l].rearrange("(p o) -> p o", o=1)), b_w, joins=[b_w])
                S.dma(lambda e: e.dma_start(out=fb1[:], in_=f_b1[l].rearrange("(p o) -> p o", o=1)), b_w, joins=[b_w])
                S.dma(lambda e: e.dma_start(out=fb2[:], in_=f_b2[l].rearrange("(p o) -> p o", o=1)), b_w, joins=[b_w])
                S.op("dve", lambda e: e.tensor_copy(out=w1b[:], in_=w1s[:]), reads=[b_w], joins=[b_w])
                S.op("dve", lambda e: e.tensor_copy(out=w2b[:], in_=w2s[:]), reads=[b_w], joins=[b_w])
                S.op("dve", lambda e: e.tensor_copy(out=w3b[:], in_=w3s[:]), reads=[b_w], joins=[b_w])
                S.op("dve", lambda e: e.tensor_tensor(out=fb1[:], in0=fb1[:], in1=fqv[:], op=ALU.mult), reads=[b_w], joins=[b_w])
                S.op("dve", lambda e: e.tensor_tensor(out=fb2[:], in0=fb2[:], in1=fqv[:], op=ALU.mult), reads=[b_w], joins=[b_w])

                def ldp(pc):
                    sl = pc % 2
                    S.dma(lambda e: e.dma_start(out=fet[sl][:], in_=FEc[L][:, :, pc * pw:(pc + 1) * pw]), b_fet[sl], writes=[b_fet[sl]])
                    S.dma(lambda e: e.dma_start(out=dct[sl][:], in_=DEc[L][:, :, pc * pw:(pc + 1) * pw].rearrange("c p n -> p c n")), b_dct[sl], writes=[b_dct[sl]])

                def sin_layer(ps, bps, bvec, hout, b_hout):
                    S.op("act", lambda e: e.activation(out=arg[:], in_=ps[0:64, 0:pw], func=AF.Identity, scale=fqv[:, 0:1], bias=bvec[:, 0:1]), reads=[bps, b_w], writes=[b_arg])
                    S.op("dve", lambda e: e.tensor_scalar(out=tmpa[:], in0=arg[:], scalar1=PI, scalar2=-2 * PI, op0=ALU.is_gt, op1=ALU.mult), reads=[b_arg], writes=[b_tmpa])
                    S.op("dve", lambda e: e.tensor_tensor(out=arg[:], in0=arg[:], in1=tmpa[:], op=ALU.add), reads=[b_tmpa], writes=[b_arg])
                    S.op("dve", lambda e: e.tensor_scalar(out=tmpa[:], in0=arg[:], scalar1=-PI, scalar2=2 * PI, op0=ALU.is_lt, op1=ALU.mult), reads=[b_arg], writes=[b_tmpa])
                    S.op("dve", lambda e: e.tensor_tensor(out=arg[:], in0=arg[:], in1=tmpa[:], op=ALU.add), reads=[b_tmpa], writes=[b_arg])
                    S.op("dve", lambda e: e.tensor_scalar(out=arg[:], in0=arg[:], scalar1=3.14159, scalar2=-3.14159, op0=ALU.min, op1=ALU.max), reads=[b_arg], writes=[b_arg])
                    S.op("act", lambda e: e.activation(out=hout[:], in_=arg[:], func=AF.Sin), reads=[b_arg], writes=[b_hout])

                ldp(0)
                for pc in range(npc):
                    sl = pc % 2
                    if pc + 1 < npc:
                        ldp(pc + 1)
                    dr = 0 if pc * pw < L else 1
                    ps, bps = nps()
                    S.op("pe", lambda e, ps=ps, sl=sl: e.matmul(ps[0:64, 0:pw], lhsT=w1b[:], rhs=fet[sl][:, 0, :], start=True, stop=False), reads=[b_w, b_fet[sl]], writes=[bps])
                    S.op("pe", lambda e, ps=ps, sl=sl: e.matmul(ps[0:64, 0:pw], lhsT=w1b[:], rhs=fet[sl][:, 1, :], start=False, stop=True), reads=[b_w, b_fet[sl]], joins=[bps])
                    sin_layer(ps, bps, fb1, h1, b_h1)
                    ps, bps = nps()
                    S.op("pe", lambda e, ps=ps: e.matmul(ps[0:64, 0:pw], lhsT=w2b[:], rhs=h1[:], start=True, stop=True), reads=[b_w, b_h1], writes=[bps])
                    sin_layer(ps, bps, fb2, h2, b_h2)
                    for oc in range(4):
                        o, cc = oc // 2, oc % 2
                        col = (dr * 2 + o) * 256 + cc * 128
                        ps, bps = nps()
                        S.op("pe", lambda e, ps=ps, col=col: e.matmul(ps[:, 0:pw], lhsT=w3b[:, col:col + 128], rhs=h2[:], start=True, stop=True), reads=[b_w, b_h2], writes=[bps])
                        S.op("dve", lambda e, ps=ps, cc=cc, oc=oc, sl=sl: e.tensor_tensor(out=kf[:, oc, :], in0=ps[:, 0:pw], in1=dct[sl][:, cc, :], op=ALU.mult), reads=[bps, b_dct[sl]], writes=[b_kf] if oc == 0 else (), joins=() if oc == 0 else [b_kf])
                        S.op("act", lambda e, oc=oc, pc=pc: e.activation(out=kjunk[:], in_=kf[:, oc, :], func=AF.Abs, accum_out=nacc[:, oc, pc:pc + 1]), reads=[b_kf], joins=[b_nacc])
                    S.op("pool", lambda e: e.tensor_copy(out=kb[:], in_=kf[:]), reads=[b_kf], writes=[b_kb])
                    for blk in range(pw // 128):
                        ps, bps = nps()
                        for cc in range(4):
                            S.op("pe", lambda e, ps=ps, cc=cc, blk=blk: e.matmul(ps[:, cc * 128:(cc + 1) * 128], lhsT=kb[:, cc, blk * 128:(blk + 1) * 128], rhs=ident[:], start=True, stop=True),
                                 reads=[b_kb, B_const], writes=[bps] if cc == 0 else (), joins=() if cc == 0 else [bps])
                        tch = (pc * pw) // 128 + blk
                        S.op("act" if blk % 2 == 0 else "dve",
                             (lambda e, ps=ps, tch=tch: e.copy(out=kTM[:, tch, :], in_=ps[:, 0:512])) if blk % 2 == 0 else (lambda e, ps=ps, tch=tch: e.tensor_copy(out=kTM[:, tch, :], in_=ps[:, 0:512])),
                             reads=[bps], joins=[b_kTM])
                S.op("dve", lambda e: e.tensor_reduce(out=nrm[:], in_=nacc[:], axis=AX.X, op=ALU.add), reads=[b_nacc], writes=[b_rn])
                S.op("dve", lambda e: e.reciprocal(out=nrm[:], in_=nrm[:]), reads=[b_rn], writes=[b_rn])
                S.op("dve", lambda e: e.tensor_copy(out=rnh[:], in_=nrm[:]), reads=[b_rn], writes=[b_rn])
                S.op("dve", lambda e: e.tensor_copy(out=rnhf[:], in_=rnh[:]), reads=[b_rn], writes=[b_rn])
                S.op("dve", lambda e: e.tensor_tensor(out=rnhf[:], in0=nrm[:], in1=rnhf[:], op=ALU.subtract), reads=[b_rn], writes=[b_rn])
                S.op("dve", lambda e: e.tensor_copy(out=rnl[:], in_=rnhf[:]), reads=[b_rn], writes=[b_rn])
                for cc in range(4):
                    S.op("dve", lambda e, cc=cc: e.tensor_scalar(out=dg[:, 0, cc, :], in0=identf[:], scalar1=nrm[:, cc:cc + 1], scalar2=None, op0=ALU.mult), reads=[b_rn, B_const], writes=[b_rn])
                    S.op("dve", lambda e, cc=cc: e.tensor_scalar(out=dg[:, 1, cc, :], in0=identf[:], scalar1=rnhf[:, cc:cc + 1], scalar2=None, op0=ALU.mult), reads=[b_rn, B_const], writes=[b_rn])
                ps, bps = nps()
                for cc in range(4):
                    S.op("pe", lambda e, ps=ps, cc=cc: e.matmul(ps[:, cc * 128:(cc + 1) * 128], lhsT=ones[:], rhs=dg[:, 0, cc, :], start=True, stop=False), reads=[b_rn, B_const], writes=[bps] if cc == 0 else (), joins=() if cc == 0 else [bps])
                    S.op("pe", lambda e, ps=ps, cc=cc: e.matmul(ps[:, cc * 128:(cc + 1) * 128], lhsT=ones[:], rhs=dg[:, 1, cc, :], start=False, stop=True), reads=[b_rn], joins=[bps])
                S.op("act", lambda e, ps=ps: e.copy(out=rnbc[:], in_=ps[:, 0:512]), reads=[bps], writes=[b_rn])

                S.op("dve", lambda e: e.tensor_tensor(out=kTM[:, 0:nt, :], in0=kTM[:, 0:nt, :], in1=kTM[:, nt:2 * nt, :], op=ALU.add), reads=[b_kTM], writes=[b_kTM])
                S.op("dve", lambda e: e.scalar_tensor_tensor(out=kTM[:, nt:2 * nt, :], in0=kTM[:, nt:2 * nt, :], scalar=-2.0, in1=kTM[:, 0:nt, :], op0=ALU.mult, op1=ALU.add), reads=[b_kTM], writes=[b_kTM])

                def ldf(m):
                    sl = m % 2
                    S.dma(lambda e: e.dma_start(out=fwt[sl][:], in_=FWc[L][m]), b_fwt[sl], writes=[b_fwt[sl]])
                ldf(0)
                for m in range(nt):
                    sl = m % 2
                    if m + 1 < nt:
                        ldf(m + 1)
                    pss = [nps() for _ in range(2)]
                    hf = 0 if m < max(nt // 2, 1) else 1
                    for ri in range(2):
                        ps, bps = pss[ri]
                        for tau in range(nt):
                            S.op("pe", lambda e, ps=ps, sl=sl, ri=ri, tau=tau, hf=hf: e.matmul(ps[:, 0:512], lhsT=fwt[sl][:, ri, tau, :], rhs=kTM[:, hf * nt + tau, :], start=(tau == 0), stop=(tau == nt - 1)),
                                 reads=[b_fwt[sl], b_kTM], writes=[bps] if tau == 0 else (), joins=() if tau == 0 else [bps])
                    for ri in range(2):
                        p1, b1 = pss[ri]
                        eb, beb = ev[ri * 2 + 1], b_ev[ri * 2 + 1]
                        S.op("act", lambda e, p1=p1, eb=eb: e.copy(out=eb[:], in_=p1[:, 0:512]), reads=[b1], writes=[beb])
                    S.op("pool", lambda e: e.tensor_tensor(out=tabA[:], in0=ev[1][:], in1=rnbc[:], op=ALU.mult), reads=[b_ev[1], b_rn], writes=[b_tA])
                    S.op("pool", lambda e: e.tensor_tensor(out=tabC[:], in0=ev[3][:], in1=rnbc[:], op=ALU.mult), reads=[b_ev[3], b_rn], writes=[b_tC])
                    if m == 0:
                        S.op("dve", lambda e: e.tensor_scalar(out=tabD[:], in0=tabC[:], scalar1=m1v[:, 0:1], scalar2=None, op0=ALU.mult), reads=[b_tC, B_const], writes=[b_tD])
                        S.op("dve", lambda e: e.scalar_tensor_tensor(out=tabD[:], in0=tabA[:], scalar=m0v[:, 0:1], in1=tabD[:], op0=ALU.mult, op1=ALU.add), reads=[b_tA], writes=[b_tD])
                        S.op("dve", lambda e: e.tensor_scalar(out=tabC[:], in0=tabC[:], scalar1=m0v[:, 0:1], scalar2=None, op0=ALU.mult), reads=[b_tD], writes=[b_tC])
                        S.dma(lambda e: e.dma_start(out=TD[L], in_=tabD[:]), b_tD, reads=[b_tD], writes=[B_tab[L]])
                    S.dma(lambda e, m=m: e.dma_start(out=TA[L][m], in_=tabA[:]), b_tA, reads=[b_tA], joins=[B_tab[L]])
                    S.dma(lambda e, m=m: e.dma_start(out=TC[L][m], in_=tabC[:]), b_tC, reads=[b_tC], joins=[B_tab[L]])
                phase_end(loc)

        def phase_P(l):
            with ExitStack() as st:
                xt = [sbt(st, "xt%d" % i, [128, 8, 512], F32) for i in range(2)]
                hT = [sbt(st, "hT%d" % i, [128, 8, 512], BF16) for i in range(2)]
                Pt = sbt(st, "Pt", [128, NCH, 512], BF16)
                kvf = sbt(st, "kvf", [128, 4, 256], F32)
                vbt = sbt(st, "vbt", [128, 4, 128], BF16)
                tm = norm_tmps(st, "P", 512)
                b_xt = [S.buf("xt0"), S.buf("xt1")]
                b_hT = [S.buf("hT0"), S.buf("hT1")]
                b_Pt = [S.buf("Pt%d" % g) for g in range(5)]
                b_kvf = S.buf("kvf")
                b_vbt = S.buf("vbt")
                alltiles = []
                for si, (base, L, ctx) in enumerate(SEQS):
                    for (t0, n) in tiles_of(base, L):
                        alltiles.append((si, t0, n, ctx))
                first = {"P": True, "V": True}

                def ldx(i):
                    si, t0, n, ctx = alltiles[i]
                    sl = i % 2
                    for c0 in range(0, 8, 4):
                        S.dma(lambda e, c0=c0: e.dma_start(out=xt[sl][:, c0:c0 + 4, 0:n], in_=xT[c0:c0 + 4, :, t0:t0 + n].rearrange("c p t -> p c t")), b_xt[sl],
                              reads=[B_xT], writes=[b_xt[sl]] if c0 == 0 else (), joins=() if c0 == 0 else [b_xt[sl]])

                def do_norm(i):
                    si, t0, n, ctx = alltiles[i]
                    sl = i % 2
                    s = 0 if ctx else 1
                    rms_norm(tm, xt[sl], b_xt[sl], n, GP[:, l, s, :], SH[:, l, s, :], hT[sl], b_hT[sl])

                def do_tile(i):
                    si, t0, n, ctx = alltiles[i]
                    sl = i % 2
                    h = hT[sl]
                    bh = b_hT[sl]
                    for ch in range(NCH):
                        g = ch // 5
                        ps, bps = nps()
                        for kc in range(8):
                            S.op("pe", lambda e, ps=ps, ch=ch, kc=kc: e.matmul(ps[:, 0:n], lhsT=Wh["b"][:, kc, ch * 128:(ch + 1) * 128], rhs=h[:, kc, 0:n], start=(kc == 0), stop=(kc == 7)),
                                 reads=[B_W, bh], writes=[bps] if kc == 0 else (), joins=() if kc == 0 else [bps])
                        wr = dict(writes=[b_Pt[g]]) if ch % 5 == 0 else dict(joins=[b_Pt[g]])
                        if ch in GATE_CH:
                            S.op("act", lambda e, ps=ps, ch=ch: e.activation(out=Pt[:, ch, 0:n], in_=ps[:, 0:n], func=AF.Silu), reads=[bps], **wr)
                        elif ch % 2 == 0:
                            S.op("dve", lambda e, ps=ps, ch=ch: e.tensor_copy(out=Pt[:, ch, 0:n], in_=ps[:, 0:n]), reads=[bps], **wr)
                        else:
                            S.op("act", lambda e, ps=ps, ch=ch: e.copy(out=Pt[:, ch, 0:n], in_=ps[:, 0:n]), reads=[bps], **wr)
                        if ch % 5 == 4:
                            c0 = ch - 4
                            S.dma(lambda e, c0=c0: e.dma_start(out=Pscr[c0:c0 + 5, :, t0:t0 + n].rearrange("c p t -> p c t"), in_=Pt[:, c0:c0 + 5, 0:n]), b_Pt[g],
                                  reads=[b_Pt[g]], writes=[B_P] if first["P"] else (), joins=() if first["P"] else [B_P])
                            first["P"] = False
                    nbk = n // 128
                    for bk in range(nbk):
                        ps, bps = nps()
                        for kc in range(8):
                            S.op("pe", lambda e, ps=ps, bk=bk, kc=kc: e.matmul(ps[:, 0:256], lhsT=h[:, kc, bk * 128:(bk + 1) * 128], rhs=Wh["kv"][:, kc, :], start=(kc == 0), stop=(kc == 7)),
                                 reads=[B_W, bh], writes=[bps] if kc == 0 else (), joins=() if kc == 0 else [bps])
                        wv = dict(writes=[b_vbt]) if bk == 0 else dict(joins=[b_vbt])
                        if ctx:
                            S.op("act", lambda e, ps=ps, bk=bk: e.copy(out=vbt[:, bk, :], in_=ps[:, 128:256]), reads=[bps], **wv)
                        else:
                            wk = dict(writes=[b_kvf]) if bk == 0 else dict(joins=[b_kvf])
                            S.op("dve", lambda e, ps=ps, bk=bk: e.tensor_copy(out=kvf[:, bk, :], in_=ps[:, 0:256]), reads=[bps], **wk)
                            S.op("pool", lambda e, bk=bk: e.tensor_copy(out=vbt[:, bk, :], in_=kvf[:, bk, 128:256]), reads=[b_kvf], **wv)
                    S.dma(lambda e: e.dma_start(out=Vscr[t0:t0 + n, :].rearrange("(b p) d -> p b d", p=128), in_=vbt[:, 0:nbk, :]), b_vbt,
                          reads=[b_vbt], writes=[B_V] if first["V"] else (), joins=() if first["V"] else [B_V])
                    first["V"] = False
                    if not ctx:
                        j = si - 1
                        S.dma(lambda e: e.dma_start(out=nk_out[j, l].rearrange("(b p) d -> p b d", p=128), in_=kvf[:, 0:nbk, 0:128]), b_kvf, reads=[b_kvf], joins=[B_out])
                        S.dma(lambda e: e.dma_start(out=nv_out[j, l].rearrange("(b p) d -> p b d", p=128), in_=kvf[:, 0:nbk, 128:256]), b_kvf, reads=[b_kvf], joins=[B_out])

                ldx(0)
                do_norm(0)
                for i in range(len(alltiles)):
                    if i + 1 < len(alltiles):
                        ldx(i + 1)
                        do_norm(i + 1)
                    do_tile(i)
                    if _os_mod.environ.get('KBAR_P'):
                        S.barrier()
                phase_end(b_xt + b_hT + b_Pt + [b_kvf, b_vbt, tm["b_sq"], tm["b_rstd"], tm["b_tmpn"]])

        class _View3:
            def __init__(self, t, n):
                self.t = t
                self.n = n

            def __getitem__(self, key):
                if not isinstance(key, tuple):
                    key = (key,)
                key = list(key) + [slice(None)] * (3 - len(key))
                if isinstance(key[2], slice) and key[2] == slice(None):
                    key[2] = slice(0, self.n)
                return self.t[tuple(key)]

        def phase_H(l, si):
            base, L, ctx = SEQS[si]
            nt = L // 128
            n = min(512, L)
            with ExitStack() as st:
                XT = sbt(st, "XT", [128, nt, 256], BF16)
                Y = sbt(st, "Y", [128, 2 * nt, 256], BF16)
                cw = sbt(st, "cw", [128, 6, 3], F32)
                dbc = sbt(st, "dbc", [128, 2, 256], F32)
                b_XT = S.buf("XT")
                b_Y = S.buf("Y")
                b_cw = S.buf("cw")
                loc = [b_XT, b_Y, b_cw]
                for k3 in range(3):
                    S.dma(lambda e, k3=k3: e.dma_start(out=cw[:, :, k3:k3 + 1], in_=hy_cw[l, k3:k3 + 1, :].rearrange("k (c p) -> p c k", p=128), allow_slow_non_contiguous=True), b_cw, joins=[b_cw])
                for o in range(2):
                    S.dma(lambda e, o=o: e.dma_start(out=dbc[:, o, :], in_=hy_d[l, o:o + 1, :].partition_broadcast(128)), b_cw, joins=[b_cw])
                with ExitStack() as st1:
                    hr = [sbt(st1, "hr%d" % i, [128, 8, n + 2], BF16) for i in range(2)]
                    hc = sbt(st1, "hc", [128, 8, n], BF16)
                    hacc = sbt(st1, "hacc", [128, 6, n], F32)
                    tmo = sbt(st1, "tmo", [128, n // 128, 768], BF16)
                    b_hr = [S.buf("hr0"), S.buf("hr1")]
                    b_hc = S.buf("hc")
                    b_hacc = S.buf("hacc")
                    b_tmo = S.buf("tmo")
                    tl = tiles_of(base, L)
                    firstHX = [True]

                    def ldh(i):
                        t0, nn = tl[i]
                        sl = i % 2
                        lo = max(t0 - 1, base)
                        hi = min(t0 + nn + 1, base + L)
                        S.dma(lambda e: e.dma_start(out=hr[sl][:, 0:4, lo - (t0 - 1):hi - (t0 - 1)], in_=Pscr[CHY:CHY + 4, :, lo:hi].rearrange("c p t -> p c t")), b_hr[sl], reads=[B_P], writes=[b_hr[sl]])
                        if lo == t0:
                            S.op("pool", lambda e: e.memset(hr[sl][:, :, 0:1], 0.0), joins=[b_hr[sl]])
                        if hi == t0 + nn:
                            S.op("pool", lambda e: e.memset(hr[sl][:, :, nn + 1:nn + 2], 0.0), joins=[b_hr[sl]])
                        S.dma(lambda e: e.dma_start(out=hr[sl][:, 4:8, lo - (t0 - 1):hi - (t0 - 1)], in_=Pscr[CHY + 4:CHY + 8, :, lo:hi].rearrange("c p t -> p c t")), b_hr[sl], reads=[B_P], joins=[b_hr[sl]])
                    ldh(0)
                    for i, (t0, nn) in enumerate(tl):
                        sl = i % 2
                        if i + 1 < len(tl):
                            ldh(i + 1)
                        for c in range(6):
                            eng = "dve"
                            S.op(eng, lambda e, c=c, sl=sl: e.tensor_scalar(out=hacc[:, c, :], in0=hr[sl][:, c, 1:nn + 1], scalar1=cw[:, c, 1:2], scalar2=None, op0=ALU.mult), reads=[b_hr[sl], b_cw], writes=[b_hacc] if c == 0 else (), joins=() if c == 0 else [b_hacc])
                            S.op(eng, lambda e, c=c, sl=sl: e.scalar_tensor_tensor(out=hacc[:, c, :], in0=hr[sl][:, c, 0:nn], scalar=cw[:, c, 0:1], in1=hacc[:, c, :], op0=ALU.mult, op1=ALU.add), reads=[b_hr[sl], b_hacc], joins=[b_hacc])
                            S.op(eng, lambda e, c=c, sl=sl: e.scalar_tensor_tensor(out=hc[:, c, :], in0=hr[sl][:, c, 2:nn + 2], scalar=cw[:, c, 2:3], in1=hacc[:, c, :], op0=ALU.mult, op1=ALU.add), reads=[b_hr[sl], b_hacc], writes=[b_hc] if c == 0 else (), joins=() if c == 0 else [b_hc])
                        S.op("pool", lambda e, sl=sl: e.tensor_copy(out=hc[:, 6:8, :], in_=hr[sl][:, 6:8, 1:nn + 1]), reads=[b_hr[sl]], joins=[b_hc])
                        for bk in range(nn // 128):
                            pa, ba = nps()
                            pb, bb = nps()
                            for c in range(8):
                                ps_, bp_ = (pa, ba) if c < 4 else (pb, bb)
                                cc = c % 4
                                S.op("pe", lambda e, ps_=ps_, c=c, cc=cc, bk=bk: e.matmul(ps_[:, cc * 128:(cc + 1) * 128], lhsT=hc[:, c, bk * 128:(bk + 1) * 128], rhs=ident[:], start=True, stop=True),
                                     reads=[b_hc, B_const], writes=[bp_] if cc == 0 else (), joins=() if cc == 0 else [bp_])
                            tch = (t0 - base) // 128 + bk
                            S.op("act", lambda e, pa=pa, tch=tch: e.copy(out=XT[:, tch, :], in_=pa[:, 0:256]), reads=[ba], joins=[b_XT])
                            S.op("act", lambda e, pa=pa, bk=bk: e.copy(out=tmo[:, bk, 0:256], in_=pa[:, 256:512]), reads=[ba], writes=[b_tmo] if bk == 0 else (), joins=() if bk == 0 else [b_tmo])
                            S.op("dve", lambda e, pb=pb, bk=bk: e.tensor_copy(out=tmo[:, bk, 256:768], in_=pb[:, 0:512]), reads=[bb], joins=[b_tmo])
                        S.dma(lambda e, t0=t0, nn=nn: e.dma_start(out=HX[t0:t0 + nn, :].rearrange("(b p) d -> p b d", p=128), in_=tmo[:, 0:nn // 128, :]), b_tmo,
                              reads=[b_tmo], writes=[B_HX] if firstHX[0] else (), joins=() if firstHX[0] else [B_HX])
                        firstHX[0] = False
                    S.barrier()
                    S.emit()
                    S.drop(b_hr + [b_hc, b_hacc, b_tmo])
                with ExitStack() as st2:
                    fwt = [sbt(st2, "hfw%d" % i, [128, 2, nt, 128], BF16) for i in range(2)]
                    ivt = [fwt[i][:].rearrange("p r t j -> p (r t) j") for i in range(2)]
                    ta = [sbt(st2, "ta%d" % i, [128, 256], F32) for i in range(2)]
                    tc_ = [sbt(st2, "tc%d" % i, [128, 256], F32) for i in range(2)]
                    td = sbt(st2, "td", [128, 256], F32)
                    er = sbt(st2, "er", [128, 256], F32)
                    ei = sbt(st2, "ei", [128, 256], F32)
                    t1 = sbt(st2, "t1", [128, 256], F32)
                    t2 = sbt(st2, "t2", [128, 256], F32)
                    t3 = sbt(st2, "t3", [128, 256], F32)
                    t4 = sbt(st2, "t4", [128, 256], F32)
                    hx = [sbt(st2, "hx%d" % i, [128, 768], BF16) for i in range(2)]
                    u1 = sbt(st2, "u1", [128, 256], F32)
                    zt = sbt(st2, "zt", [128, 256], BF16)
                    zo = sbt(st2, "zo", [128, 2, 512], BF16)
                    b_fw = [S.buf("hfw0"), S.buf("hfw1")]
                    b_iv = b_fw
                    b_ta = [S.buf("ta0"), S.buf("ta1")]
                    b_td = S.buf("td")
                    b_e = S.buf("e")
                    b_t12 = S.buf("t12")
                    b_t34 = S.buf("t34")
                    b_hx = [S.buf("hx0"), S.buf("hx1")]
                    b_u1 = S.buf("u1")
                    b_zt = S.buf("zt")
                    b_zo = S.buf("zo")
                    loc2 = b_fw + b_ta + b_hx + [b_td, b_e, b_t12, b_t34, b_u1, b_zt, b_zo]
                    firstZ = [True]
                    for o in range(2):
                        S.dma(lambda e, o=o: e.dma_start(out=td[:], in_=TD[L][:, o * 256:(o + 1) * 256]), b_td, reads=[B_tab[L]], writes=[b_td])

                        def ldfw(m, o=o):
                            sl = m % 2
                            S.dma(lambda e: e.dma_start(out=fwt[sl][:], in_=FWc[L][m]), b_fw[sl], writes=[b_fw[sl]])
                            S.dma(lambda e: e.dma_start(out=ta[sl][:], in_=TA[L][m, :, o * 256:(o + 1) * 256]), b_ta[sl], reads=[B_tab[L]], writes=[b_ta[sl]])
                            S.dma(lambda e: e.dma_start(out=tc_[sl][:], in_=TC[L][m, :, o * 256:(o + 1) * 256]), b_ta[sl], reads=[B_tab[L]], joins=[b_ta[sl]])
                        ldfw(0)
                        for m in range(nt):
                            sl = m % 2
                            if m + 1 < nt:
                                ldfw(m + 1)
                            (pr, bpr), (pi_, bpi) = nps(), nps()
                            for ri, (ps, bps) in enumerate(((pr, bpr), (pi_, bpi))):
                                for tau in range(nt):
                                    S.op("pe", lambda e, ps=ps, sl=sl, ri=ri, tau=tau: e.matmul(ps[:, 0:256], lhsT=fwt[sl][:, ri, tau, :], rhs=XT[:, tau, :], start=(tau == 0), stop=(tau == nt - 1)),
                                         reads=[b_fw[sl], b_XT], writes=[bps] if tau == 0 else (), joins=() if tau == 0 else [bps])
                            S.op("act", lambda e, pr=pr: e.copy(out=er[:], in_=pr[:, 0:256]), reads=[bpr], writes=[b_e])
                            S.op("act", lambda e, pi_=pi_: e.copy(out=ei[:], in_=pi_[:, 0:256]), reads=[bpi], joins=[b_e])
                            dt_ = td if m == 0 else ta[sl]
                            rd = [b_td] if m == 0 else []
                            S.op("dve", lambda e, sl=sl: e.tensor_tensor(out=t1[:], in0=er[:], in1=ta[sl][:], op=ALU.mult), reads=[b_e, b_ta[sl]], writes=[b_t12])
                            S.op("dve", lambda e, sl=sl: e.tensor_tensor(out=t2[:], in0=ei[:], in1=tc_[sl][:], op=ALU.mult), reads=[b_e, b_ta[sl]], joins=[b_t12])
                            S.op("dve", lambda e, m=m: e.tensor_tensor(out=Y[:, m, :], in0=t1[:], in1=t2[:], op=ALU.subtract), reads=[b_t12], joins=[b_Y])
                            S.op("pool", lambda e, sl=sl: e.tensor_tensor(out=t3[:], in0=er[:], in1=tc_[sl][:], op=ALU.mult), reads=[b_e, b_ta[sl]], writes=[b_t34])
                            S.op("pool", lambda e, dt_=dt_: e.tensor_tensor(out=t4[:], in0=ei[:], in1=dt_[:], op=ALU.mult), reads=[b_e, b_ta[sl]] + rd, joins=[b_t34])
                            S.op("pool", lambda e, m=m: e.tensor_tensor(out=Y[:, nt + m, :], in0=t3[:], in1=t4[:], op=ALU.add), reads=[b_t34], joins=[b_Y])

                        def ldiv(tau, o=o):
                            sl = tau % 2
                            S.dma(lambda e: e.dma_start(out=ivt[sl], in_=IVc[L][tau]), b_iv[sl], writes=[b_iv[sl]])
                            S.dma(lambda e: e.dma_start(out=hx[sl][:], in_=HX[base + tau * 128:base + (tau + 1) * 128, :]), b_hx[sl], reads=[B_HX], writes=[b_hx[sl]])
                        ldiv(0)
                        for tau in range(nt):
                            sl = tau % 2
                            if tau + 1 < nt:
                                ldiv(tau + 1)
                            py, bpy = nps()
                            for r in range(2 * nt):
                                S.op("pe", lambda e, py=py, sl=sl, r=r: e.matmul(py[:, 0:256], lhsT=ivt[sl][:, r, :], rhs=Y[:, r, :], start=(r == 0), stop=(r == 2 * nt - 1)),
                                     reads=[b_iv[sl], b_Y], writes=[bpy] if r == 0 else (), joins=() if r == 0 else [bpy])
                            S.op("dve", lambda e, tau=tau, o=o: e.tensor_tensor(out=u1[:], in0=XT[:, tau, :], in1=dbc[:, o, :], op=ALU.mult), reads=[b_XT, b_cw], writes=[b_u1])
                            S.op("dve", lambda e, py=py: e.tensor_tensor(out=u1[:], in0=u1[:], in1=py[:, 0:256], op=ALU.add), reads=[bpy, b_u1], writes=[b_u1])
                            if o == 0:
                                S.op("dve", lambda e, tau=tau, sl=sl: e.tensor_tensor(out=XT[:, tau, :], in0=u1[:], in1=hx[sl][:, 0:256], op=ALU.mult), reads=[b_u1, b_hx[sl]], joins=[b_XT])
                            else:
                                S.op("dve", lambda e, sl=sl: e.tensor_tensor(out=u1[:], in0=u1[:], in1=hx[sl][:, 256:512], op=ALU.mult), reads=[b_u1, b_hx[sl]], writes=[b_u1])
                                S.op("pool", lambda e, sl=sl: e.tensor_tensor(out=zt[:], in0=u1[:], in1=hx[sl][:, 512:768], op=ALU.mult), reads=[b_u1, b_hx[sl]], writes=[b_zt])
                                pz, bpz = nps()
                                for cc in range(2):
                                    S.op("pe", lambda e, pz=pz, cc=cc: e.matmul(pz[:, cc * 128:(cc + 1) * 128], lhsT=zt[:, cc * 128:(cc + 1) * 128], rhs=ident[:], start=True, stop=True),
                                         reads=[b_zt, B_const], writes=[bpz] if cc == 0 else (), joins=() if cc == 0 else [bpz])
                                q4 = tau % 4
                                S.op("act", lambda e, pz=pz, q4=q4: e.copy(out=zo[:, :, q4 * 128:(q4 + 1) * 128], in_=pz[:, 0:256].rearrange("p (c t) -> p c t", c=2)), reads=[bpz], writes=[b_zo] if q4 == 0 else (), joins=() if q4 == 0 else [b_zo])
                                if q4 == 3 or tau == nt - 1:
                                    w = (q4 + 1) * 128
                                    t0 = base + (tau - q4) * 128
                                    S.dma(lambda e, w=w, t0=t0: e.dma_start(out=Zscr[:, :, t0:t0 + w].rearrange("c p t -> p c t"), in_=zo[:, :, 0:w]), b_zo,
                                          reads=[b_zo], writes=[B_Z] if firstZ[0] else (), joins=() if firstZ[0] else [B_Z])
                                    firstZ[0] = False
                    S.barrier()
                    S.emit()
                    S.drop(loc2)
                phase_end(loc)

        def phase_C(l, sis):
            base, L, ctx = SEQS[sis[0]]
            n = min(512, L)
            nbq = n // 128
            s = 0 if ctx else 1
            W = n + 256 if ctx else L
            nkb = W // 128
            with ExitStack() as st:
                xt_2 = [sbt(st, "cxt", [128, 8, n], F32) for _ in range(2)]
                qraw_2 = [sbt(st, "qraw", [128, 4, n], BF16) for _ in range(2)]
                qrot_2 = [sbt(st, "qrot", [128, 4, n], BF16) for _ in range(2)]
                kraw_2 = [sbt(st, "kraw", [128, W], BF16) for _ in range(2)]
                krot_2 = [sbt(st, "krot", [128, W], BF16) for _ in range(2)]
                rc_2 = [sbt(st, "rc", [128, W], F32) for _ in range(2)]
                rs_2 = [sbt(st, "rs", [128, W], F32) for _ in range(2)]
                rt1 = sbt(st, "rt1", [128, 512], F32)
                rt2 = sbt(st, "rt2", [128, 512], F32)
                vt_2 = [sbt(st, "vt", [128, nkb, 128], BF16) for _ in range(2)]
                ga_2 = [sbt(st, "ga", [128, 4, n], BF16) for _ in range(2)]
                scb_2 = [sbt(st, "scb", [128, 6, n + 2], BF16) for _ in range(2)]
                gcz_2 = [sbt(st, "gcz", [128, 4, n], BF16) for _ in range(2)]
                mixT = sbt(st, "mixT", [128, 8, n], BF16)
                attf = sbt(st, "attf", [128, 4, n], F32)
                PT = [sbt(st, "PT%d" % i, [128, 512], BF16) for i in range(10)]
                rden = sbt(st, "rden", [64, 512], F32)
                sinkt = sbt(st, "sinkt", [64, 2, 512], F32)
                esk = sbt(st, "esk", [64, 8], F32)
                scw = sbt(st, "scw", [128, 2, 3], F32)
                su = sbt(st, "su", [128, 2, n + 2], F32)
                sacc = sbt(st, "sacc", [128, 2, n], F32)
                b_xt_2, b_q_2, b_k_2, b_rope_2, b_vt_2, b_ga_2, b_scb_2, b_gcz_2 = [[S.buf(x + "0"), S.buf(x + "1")] for x in ("cxt", "q", "k", "rope", "vt", "ga", "scb", "gcz")]
                b_rt, b_mix, b_att = [S.buf(x) for x in ("rt", "mix", "att")]
                b_PT = [S.buf("PT%d" % i) for i in range(10)]
                b_rden, b_sink, b_scw, b_su, b_sacc, b_xn = [S.buf(x) for x in ("rden", "sink", "scw", "su", "sacc", "xn")]
                b_rt2 = S.buf("rt2")
                loc = [b_rt2, b_rt, b_mix, b_att, b_rden, b_sink, b_scw, b_su, b_sacc, b_xn] + b_PT + b_xt_2 + b_q_2 + b_k_2 + b_rope_2 + b_vt_2 + b_ga_2 + b_scb_2 + b_gcz_2
                if ctx:
                    cks = sbt(st, "cks", [128, 2, 128], F32)
                    ckb = sbt(st, "ckb", [128, 2, 128], BF16)
                    ckT = sbt(st, "ckT", [128, 256], BF16)
                    cvs = sbt(st, "cvs", [128, 2, 128], F32)
                    cvb = sbt(st, "cvb", [128, 2, 128], BF16)
                    b_ck = S.buf("ck")
                    loc.append(b_ck)
                    S.dma(lambda e: e.dma_start(out=cks[:], in_=ck_in[l].rearrange("(b p) d -> p b d", p=128)), b_ck, writes=[b_ck])
                    S.dma(lambda e: e.dma_start(out=cvs[:], in_=cv_in[l].rearrange("(b p) d -> p b d", p=128)), b_ck, joins=[b_ck])
                    S.op("dve", lambda e: e.tensor_copy(out=ckb[:], in_=cks[:]), reads=[b_ck], joins=[b_ck])
                    S.op("dve", lambda e: e.tensor_copy(out=cvb[:], in_=cvs[:]), reads=[b_ck], joins=[b_ck])
                    ps, bps = nps()
                    for b in range(2):
                        S.op("pe", lambda e, ps=ps, b=b: e.matmul(ps[:, b * 128:(b + 1) * 128], lhsT=ckb[:, b, :], rhs=ident[:], start=True, stop=True), reads=[b_ck, B_const], writes=[bps] if b == 0 else (), joins=() if b == 0 else [bps])
                    S.op("act", lambda e, ps=ps: e.copy(out=ckT[:], in_=ps[:, 0:256]), reads=[bps], joins=[b_ck])
                S.dma(lambda e: e.dma_start(out=esk[:], in_=sink_in[l:l + 1, :].partition_broadcast(64)), b_sink, writes=[b_sink])
                S.op("act", lambda e: e.activation(out=esk[:], in_=esk[:], func=AF.Exp), reads=[b_sink], writes=[b_sink])
                for g in range(8):
                    h, j = g // 4, g % 4
                    S.op("dve", lambda e, g=g, h=h, j=j: e.tensor_scalar(out=sinkt[:, h, j * 128:(j + 1) * 128], in0=zeros[0:64, 0:128], scalar1=esk[:, g:g + 1], scalar2=None, op0=ALU.add), reads=[b_sink, B_const], joins=[b_sink])
                for k3 in range(3):
                    S.dma(lambda e, k3=k3: e.dma_start(out=scw[:, :, k3:k3 + 1], in_=sc_cw[l, k3:k3 + 1, :].rearrange("k (c p) -> p c k", p=128), allow_slow_non_contiguous=True), b_scw, joins=[b_scw])

                def loads(ti, base, t0, nn):
                    sl = ti % 2
                    xt, qraw, qrot, kraw, krot, rc, rs, vt, ga, scb, gcz = [x[sl] for x in (xt_2, qraw_2, qrot_2, kraw_2, krot_2, rc_2, rs_2, vt_2, ga_2, scb_2, gcz_2)]
                    b_xt, b_q, b_k, b_rope, b_vt, b_ga, b_scb, b_gcz = [x[sl] for x in (b_xt_2, b_q_2, b_k_2, b_rope_2, b_vt_2, b_ga_2, b_scb_2, b_gcz_2)]
                    lo = max(t0 - 1, base)
                    hi = min(t0 + nn + 1, base + L)
                    if ctx:
                        k0 = t0 - 128
                        klo = max(k0, base)
                        khi = min(t0 + nn + 128, base + L)
                    else:
                        k0, klo, khi = base, base, base + L
                    for c0 in range(0, 8, 4):
                        S.dma(lambda e, c0=c0: e.dma_start(out=xt[:, c0:c0 + 4, :], in_=xT[c0:c0 + 4, :, t0:t0 + nn].rearrange("c p t -> p c t")), b_xt, reads=[B_xT], writes=[b_xt] if c0 == 0 else (), joins=() if c0 == 0 else [b_xt])
                    S.dma(lambda e: e.dma_start(out=qraw[:], in_=Pscr[CQ:CQ + 4, :, t0:t0 + nn].rearrange("c p t -> p c t")), b_q, reads=[B_P], writes=[b_q])
                    S.dma(lambda e: e.dma_start(out=ga[:], in_=Pscr[CGA:CGA + 4, :, t0:t0 + nn].rearrange("c p t -> p c t")), b_ga, reads=[B_P], writes=[b_ga])
                    S.dma(lambda e: e.dma_start(out=gcz[:, 0:2, :], in_=Pscr[CGC:CGC + 2, :, t0:t0 + nn].rearrange("c p t -> p c t")), b_gcz, reads=[B_P], writes=[b_gcz])
                    S.dma(lambda e: e.dma_start(out=gcz[:, 2:4, :], in_=Zscr[:, :, t0:t0 + nn].rearrange("c p t -> p c t")), b_gcz, reads=[B_Z], joins=[b_gcz])
                    lo = max(t0 - 1, base)
                    hi = min(t0 + nn + 1, base + L)
                    S.dma(lambda e: e.dma_start(out=scb[:, :, lo - (t0 - 1):hi - (t0 - 1)], in_=Pscr[CSB:CSB + 6, :, lo:hi].rearrange("c p t -> p c t")), b_scb, reads=[B_P], writes=[b_scb])
                    if lo == t0:
                        S.op("pool", lambda e: e.memset(scb[:, :, 0:1], 0.0), joins=[b_scb])
                    if hi == t0 + nn:
                        S.op("pool", lambda e: e.memset(scb[:, :, nn + 1:nn + 2], 0.0), joins=[b_scb])
                    if ctx:
                        k0 = t0 - 128
                        klo = max(k0, base)
                        khi = min(t0 + nn + 128, base + L)
                    else:
                        k0, klo, khi = base, base, base + L
                    S.dma(lambda e: e.dma_start(out=kraw[:, klo - k0:khi - k0], in_=Pscr[CK, :, klo:khi]), b_k, reads=[B_P], writes=[b_k])
                    S.dma(lambda e: e.dma_start(out=vt[:, (klo - k0) // 128:(khi - k0) // 128, :], in_=Vscr[klo:khi, :].rearrange("(b p) d -> p b d", p=128)), b_vt, reads=[B_V], writes=[b_vt])

                    if ctx:
                        S.dma(lambda e: e.dma_start(out=rc[:, klo - k0:khi - k0], in_=ropeC[:, klo - base:khi - base]), b_rope, writes=[b_rope])
                        S.dma(lambda e: e.dma_start(out=rs[:, klo - k0:khi - k0], in_=ropeS[:, klo - base:khi - base]), b_rope, joins=[b_rope])

                def do_rope(ti, base, t0, nn):
                    sl = ti % 2
                    xt, qraw, qrot, kraw, krot, rc, rs, vt, ga, scb, gcz = [x[sl] for x in (xt_2, qraw_2, qrot_2, kraw_2, krot_2, rc_2, rs_2, vt_2, ga_2, scb_2, gcz_2)]
                    b_xt, b_q, b_k, b_rope, b_vt, b_ga, b_scb, b_gcz = [x[sl] for x in (b_xt_2, b_q_2, b_k_2, b_rope_2, b_vt_2, b_ga_2, b_scb_2, b_gcz_2)]
                    lo = max(t0 - 1, base)
                    hi = min(t0 + nn + 1, base + L)
                    if ctx:
                        k0 = t0 - 128
                        klo = max(k0, base)
                        khi = min(t0 + nn + 128, base + L)
                    else:
                        k0, klo, khi = base, base, base + L
                    if ctx:
                        def rope(src_ap, dst_ap, c_ap, s_ap, wdt, rb, wb, first):
                            ps, bps = nps()
                            S.op("pe", lambda e: e.matmul(ps[:, 0:wdt], lhsT=pmat[:], rhs=src_ap, start=True, stop=True), reads=[rb, B_const], writes=[bps])
                            S.op("dve", lambda e: e.tensor_tensor(out=rt1[:, 0:wdt], in0=ps[:, 0:wdt], in1=s_ap, op=ALU.mult), reads=[bps, b_rope], writes=[b_rt])
                            S.op("pool", lambda e: e.tensor_tensor(out=rt2[:, 0:wdt], in0=src_ap, in1=c_ap, op=ALU.mult), reads=[rb, b_rope], writes=[b_rt2])
                            S.op("dve", lambda e: e.tensor_tensor(out=dst_ap, in0=rt1[:, 0:wdt], in1=rt2[:, 0:wdt], op=ALU.add), reads=[b_rt, b_rt2], joins=[wb])
                        for c in range(4):
                            rope(qraw[:, c, :], qrot[:, c, :], rc[:, 128:128 + nn], rs[:, 128:128 + nn], nn, b_q, b_q, False)
                        a0, a1 = klo - k0, khi - k0
                        for off in range(a0, a1, 512):
                            wdt = min(512, a1 - off)
                            rope(kraw[:, off:off + wdt], krot[:, off:off + wdt], rc[:, off:off + wdt], rs[:, off:off + wdt], wdt, b_k, b_k, False)

                def compute(ti, base, t0, nn, nxt=None):
                    sl = ti % 2
                    xt, qraw, qrot, kraw, krot, rc, rs, vt, ga, scb, gcz = [x[sl] for x in (xt_2, qraw_2, qrot_2, kraw_2, krot_2, rc_2, rs_2, vt_2, ga_2, scb_2, gcz_2)]
                    b_xt, b_q, b_k, b_rope, b_vt, b_ga, b_scb, b_gcz = [x[sl] for x in (b_xt_2, b_q_2, b_k_2, b_rope_2, b_vt_2, b_ga_2, b_scb_2, b_gcz_2)]
                    lo = max(t0 - 1, base)
                    hi = min(t0 + nn + 1, base + L)
                    if ctx:
                        k0 = t0 - 128
                        klo = max(k0, base)
                        khi = min(t0 + nn + 128, base + L)
                    else:
                        k0, klo, khi = base, base, base + L
                    S.op("dve", lambda e: e.tensor_tensor(out=su[:], in0=scb[:, 2:4, :], in1=scb[:, 4:6, :], op=ALU.mult), reads=[b_scb], writes=[b_su])
                    for c in range(2):
                        eng = "dve"
                        S.op(eng, lambda e, c=c: e.tensor_scalar(out=sacc[:, c, :], in0=su[:, c, 1:nn + 1], scalar1=scw[:, c, 1:2], scalar2=None, op0=ALU.mult), reads=[b_su, b_scw], writes=[b_sacc] if c == 0 else (), joins=() if c == 0 else [b_sacc])
                        S.op(eng, lambda e, c=c: e.scalar_tensor_tensor(out=sacc[:, c, :], in0=su[:, c, 0:nn], scalar=scw[:, c, 0:1], in1=sacc[:, c, :], op0=ALU.mult, op1=ALU.add), reads=[b_su, b_sacc], joins=[b_sacc])
                        S.op(eng, lambda e, c=c: e.scalar_tensor_tensor(out=sacc[:, c, :], in0=su[:, c, 2:nn + 2], scalar=scw[:, c, 2:3], in1=sacc[:, c, :], op0=ALU.mult, op1=ALU.add), reads=[b_su, b_sacc], joins=[b_sacc])
                    S.op("dve", lambda e: e.tensor_tensor(out=sacc[:], in0=sacc[:], in1=scb[:, 0:2, 1:nn + 1], op=ALU.mult), reads=[b_scb, b_sacc], writes=[b_sacc])
                    S.op("dve", lambda e: e.tensor_tensor(out=mixT[:, 6:8, :], in0=sacc[:], in1=gcz[:, 0:2, :], op=ALU.mult), reads=[b_sacc, b_gcz], writes=[b_mix])
                    S.op("pool", lambda e: e.tensor_copy(out=mixT[:, 4:6, :], in_=gcz[:, 2:4, :]), reads=[b_gcz], joins=[b_mix])
                    for qb in range(nn // 128):
                        nblk = (t0 - base) // 128 + qb
                        for h in range(2):
                            hp = slice(64 * h, 64 * h + 64)
                            blocks = []
                            if ctx:
                                for d, msk in ((-1, mlo), (0, None), (1, mhi)):
                                    kb_ = nblk + d
                                    if kb_ < 0 or kb_ >= L // 128:
                                        continue
                                    w0 = (kb_ * 128 + base) - k0
                                    blocks.append((krot[hp, w0:w0 + 128], qrot, msk, vt[:, w0 // 128, hp]))
                                for cb_ in range(2):
                                    blocks.append((ckT[hp, cb_ * 128:(cb_ + 1) * 128], qraw, None, cvb[:, cb_, hp]))
                            else:
                                for kb_ in range(L // 128):
                                    blocks.append((kraw[hp, kb_ * 128:(kb_ + 1) * 128], qraw, None, vt[:, kb_, hp]))
                            pts = []
                            for bi, (kap, qt, msk, vap) in enumerate(blocks):
                                ps, bps = nps()
                                for j in range(4):
                                    S.op("pe", lambda e, ps=ps, kap=kap, qt=qt, j=j, qb=qb, hp=hp: e.matmul(ps[:, j * 128:(j + 1) * 128], lhsT=kap, rhs=qt[hp, j, qb * 128:(qb + 1) * 128], start=True, stop=True),
                                         reads=[b_k, b_q] + ([b_ck] if ctx else []), writes=[bps] if j == 0 else (), joins=() if j == 0 else [bps])
                                pt, bpt = PT[((qb * 2 + h) % 2) * 5 + bi], b_PT[((qb * 2 + h) % 2) * 5 + bi]
                                S.op("act", lambda e, ps=ps, pt=pt: e.activation(out=pt[:], in_=ps[:], func=AF.Exp, scale=0.125), reads=[bps], writes=[bpt])
                                if msk is not None:
                                    for j in range(4):
                                        eng = "dve" if j % 2 == 0 else "pool"
                                        S.op(eng, lambda e, pt=pt, msk=msk, j=j: e.tensor_tensor(out=pt[:, j * 128:(j + 1) * 128], in0=pt[:, j * 128:(j + 1) * 128], in1=msk[:], op=ALU.mult), reads=[B_const, bpt], joins=[bpt])
                                pts.append((pt, bpt, vap))
                            po, bpo = nps()
                            pd, bpd = nps()
                            for bi, (pt, bpt, vap) in enumerate(pts):
                                S.op("pe", lambda e, po=po, pt=pt, vap=vap, bi=bi, last=(bi == len(pts) - 1): e.matmul(po[0:64, :], lhsT=vap, rhs=pt[:], start=(bi == 0), stop=last),
                                     reads=[bpt, b_vt] + ([b_ck] if ctx else []), writes=[bpo] if bi == 0 else (), joins=() if bi == 0 else [bpo])
                            for bi, (pt, bpt, vap) in enumerate(pts):
                                S.op("pe", lambda e, pd=pd, pt=pt, bi=bi, last=(bi == len(pts) - 1): e.matmul(pd[0:64, :], lhsT=ones[:, 0:64], rhs=pt[:], start=(bi == 0), stop=last),
                                     reads=[bpt, B_const], writes=[bpd] if bi == 0 else (), joins=() if bi == 0 else [bpd])
                            S.op("dve", lambda e, pd=pd, h=h: e.tensor_tensor(out=rden[:], in0=pd[0:64, :], in1=sinkt[:, h, :], op=ALU.add), reads=[bpd, b_sink], writes=[b_rden])
                            S.op("dve", lambda e: e.reciprocal(out=rden[:], in_=rden[:]), reads=[b_rden], writes=[b_rden])
                            S.op("dve", lambda e, po=po, hp=hp, qb=qb: e.tensor_tensor(out=attf[hp, :, qb * 128:(qb + 1) * 128], in0=po[0:64, :].rearrange("p (j t) -> p j t", j=4), in1=rden[:].rearrange("p (j t) -> p j t", j=4), op=ALU.mult),
                                 reads=[bpo, b_rden], joins=[b_att])
                    S.op("pool", lambda e: e.tensor_tensor(out=mixT[:, 0:4, :], in0=attf[:], in1=ga[:], op=ALU.mult), reads=[b_att, b_ga], joins=[b_mix])
                    _kbr = _os_mod.environ.get('KBR')
                    if _kbr:
                        if 'attn' not in _kbr:
                            S.op("pool", lambda e: e.memset(mixT[:, 0:4, :], 0.0), writes=[b_mix])
                        if 'hy' not in _kbr:
                            S.op("pool", lambda e: e.memset(mixT[:, 4:6, :], 0.0), writes=[b_mix])
                        if 'sc' not in _kbr:
                            S.op("pool", lambda e: e.memset(mixT[:, 6:8, :], 0.0), writes=[b_mix])
                    if nxt is not None:
                        do_rope(*nxt)
                    for oc in range(8):
                        ps, bps = nps()
                        for kc in range(8):
                            S.op("pe", lambda e, ps=ps, oc=oc, kc=kc: e.matmul(ps[:, 0:nn], lhsT=Wh["o"][:, kc, oc * 128:(oc + 1) * 128], rhs=mixT[:, kc, :], start=(kc == 0), stop=(kc == 7)),
                                 reads=[B_W, b_mix], writes=[bps] if kc == 0 else (), joins=() if kc == 0 else [bps])
                        S.op("dve", lambda e, ps=ps, oc=oc: e.scalar_tensor_tensor(out=xt[:, oc, :], in0=ps[:, 0:nn], scalar=GT[:, l, s, oc:oc + 1], in1=xt[:, oc, :], op0=ALU.mult, op1=ALU.add),
                             reads=[bps, b_xt, B_mod], joins=[b_xt])
                    for c0 in range(0, 8, 4):
                        S.dma(lambda e, c0=c0: e.dma_start(out=xT[c0:c0 + 4, :, t0:t0 + nn].rearrange("c p t -> p c t"), in_=xt[:, c0:c0 + 4, :]), b_xt, reads=[b_xt], joins=[B_xT])
                tl = [(SEQS[si_][0], t0_, nn_) for si_ in sis for (t0_, nn_) in tiles_of(SEQS[si_][0], L)]
                loads(0, *tl[0])
                do_rope(0, *tl[0])
                for ti, (bs_, t0, nn) in enumerate(tl):
                    if ti + 1 < len(tl):
                        loads(ti + 1, *tl[ti + 1])
                    compute(ti, bs_, t0, nn, nxt=((ti + 1,) + tuple(tl[ti + 1])) if ti + 1 < len(tl) else None)
                    if _os_mod.environ.get('KBAR_C'):
                        S.barrier()
                phase_end(loc)

        def phase_F():
            with ExitStack() as st:
                xt = [sbt(st, "fxt%d" % i, [128, 8, 512], F32) for i in range(2)]
                yT = sbt(st, "fyT", [128, 8, 512], F32)
                yhi = sbt(st, "fyhi", [128, 8, 512], BF16)
                ylo = sbt(st, "fylo", [128, 8, 512], BF16)
                yo = [sbt(st, "fyo%d" % i, [128, 4, 1024], F32) for i in range(2)]
                tm = norm_tmps(st, "F", 512)
                ytmp = tm["tmpn"]
                b_xt = [S.buf("fxt0"), S.buf("fxt1")]
                b_yT, b_yhl = S.buf("fyT"), S.buf("fyhl")
                b_yo = [S.buf("fyo0"), S.buf("fyo1")]
                alltiles = []
                for (base, L, ctx) in SEQS:
                    alltiles += tiles_of(base, L)

                def ldx(i):
                    t0, n = alltiles[i]
                    sl = i % 2
                    for c0 in range(0, 8, 4):
                        S.dma(lambda e, c0=c0: e.dma_start(out=xt[sl][:, c0:c0 + 4, 0:n], in_=xT[c0:c0 + 4, :, t0:t0 + n].rearrange("c p t -> p c t")), b_xt[sl],
                              reads=[B_xT], writes=[b_xt[sl]] if c0 == 0 else (), joins=() if c0 == 0 else [b_xt[sl]])

                def do_tile(i):
                    t0, n = alltiles[i]
                    sl = i % 2
                    rms_norm(tm, xt[sl], b_xt[sl], n, FG, None, yT, b_yT)
                    S.op("act", lambda e: e.copy(out=yhi[:, :, 0:n], in_=yT[:, :, 0:n]), reads=[b_yT], writes=[b_yhl])
                    S.op("pool", lambda e: e.tensor_copy(out=ytmp[:, :, 0:n], in_=yhi[:, :, 0:n]), reads=[b_yhl], writes=[tm["b_tmpn"]])
                    S.op("pool", lambda e: e.tensor_tensor(out=ylo[:, :, 0:n], in0=yT[:, :, 0:n], in1=ytmp[:, :, 0:n], op=ALU.subtract), reads=[b_yT, b_yhl, tm["b_tmpn"]], joins=[b_yhl])
                    for bk in range(n // 128):
                        for half in range(2):
                            ps, bps = nps()
                            for c in range(4):
                                cc = half * 4 + c
                                S.op("pe", lambda e, ps=ps, c=c, cc=cc, bk=bk: e.matmul(ps[:, c * 128:(c + 1) * 128], lhsT=yhi[:, cc, bk * 128:(bk + 1) * 128], rhs=ident[:], start=True, stop=False), reads=[b_yhl, B_const], writes=[bps] if c == 0 else (), joins=() if c == 0 else [bps])
                                S.op("pe", lambda e, ps=ps, c=c, cc=cc, bk=bk: e.matmul(ps[:, c * 128:(c + 1) * 128], lhsT=ylo[:, cc, bk * 128:(bk + 1) * 128], rhs=ident[:], start=False, stop=True), reads=[b_yhl], joins=[bps])
                            wr = dict(writes=[b_yo[sl]]) if (bk == 0 and half == 0) else dict(joins=[b_yo[sl]])
                            if half == 0:
                                S.op("act", lambda e, ps=ps, bk=bk: e.copy(out=yo[sl][:, bk, 0:512], in_=ps[:]), reads=[bps], **wr)
                            else:
                                S.op("dve", lambda e, ps=ps, bk=bk: e.tensor_copy(out=yo[sl][:, bk, 512:1024], in_=ps[:]), reads=[bps], **wr)
                    S.dma(lambda e: e.dma_start(out=y_out[t0:t0 + n, :].rearrange("(b p) f -> p b f", p=128), in_=yo[sl][:, 0:n // 128, :]), b_yo[sl], reads=[b_yo[sl]], joins=[B_out])

                ldx(0)
                for i in range(len(alltiles)):
                    if i + 1 < len(alltiles):
                        ldx(i + 1)
                    do_tile(i)
                    if _os_mod.environ.get('KBAR_F'):
                        S.barrier()
                phase_end(b_xt + b_yo + [b_yT, b_yhl, tm["b_sq"], tm["b_rstd"], tm["b_tmpn"]])

        class _Stop(Exception):
            pass

        def chk(name):
            if stop_after == name:
                raise _Stop()
        try:
            chk("0")
            for l in range(depth):
                filter_phase(l, 4096)
                chk("f4096")
                filter_phase(l, 256)
                chk("f256")
                with ExitStack() as wstk:
                    Wh["o"] = sbt(wstk, "Wo", [128, 8, 1024], BF16)
                    with ExitStack() as wstk2:
                        Wh["b"] = sbt(wstk2, "Wb", [128, 8, NCH * 128], BF16)
                        Wh["kv"] = sbt(wstk2, "Wkv", [128, 8, 256], BF16)
                        load_weights(l)
                        chk("W")
                        phase_P(l)
                        chk("P")
                    for si in range(len(SEQS)):
                        phase_H(l, si)
                        chk("H%d" % si)
                    phase_C(l, [0])
                    chk("C0")
                    phase_C(l, [1, 2, 3, 4])
                    chk("C1")
            phase_F()
        except _Stop:
            pass
        S.barrier()
        S.emit()
    return nc


_CONST_CACHE = {}


def _consts():
    if not _CONST_CACHE:
        fw4096, iv4096 = dft_consts(4096)
        fw256, iv256 = dft_consts(256)
        fe4096, de4096 = filt_consts(4096)
        fe256, de256 = filt_consts(256)
        C, Sn, PM = rope_consts()
        _CONST_CACHE.update(dict(fw4096=fw4096, iv4096=iv4096, fw256=fw256, iv256=iv256, fe4096=fe4096, de4096=de4096,
                                 fe256=fe256, de256=de256, ropeC=C, ropeS=Sn, permm=PM))
    return _CONST_CACHE


def kernel(x_prompt, x_sample, cache_k, cache_v, c, c_ctx, norm_g, mod_w, mod_b, w_in, attn_sink,
           hy_conv_w, hy_filt_w1, hy_filt_b1, hy_filt_w2, hy_filt_b2, hy_filt_w3, hy_filt_freq, hy_d,
           sc_conv_w, w_out, final_g, _depth=4):
    f = lambda a: np.ascontiguousarray(np.asarray(a, dtype=np.float32))
    x_prompt, x_sample, cache_k, cache_v, c, c_ctx = map(f, (x_prompt, x_sample, cache_k, cache_v, c, c_ctx))
    cols, rows, dperm = col_perm()
    w_in = f(w_in)
    w_in_p = np.ascontiguousarray(w_in[:, :, cols])
    w_kv = np.ascontiguousarray(w_in[:, :, 512:768])
    w_out_p = np.ascontiguousarray(f(w_out)[:, rows, :])
    cst = _consts()
    shared = dict(norm_g=f(norm_g), mod_w=f(mod_w), mod_b=f(mod_b), w_in=w_in_p, w_kv=w_kv, sink=f(attn_sink),
                  hy_cw=f(hy_conv_w), f_w1=f(hy_filt_w1), f_b1=f(hy_filt_b1), f_w2=f(hy_filt_w2), f_b2=f(hy_filt_b2),
                  f_w3=f(hy_filt_w3), f_fq=f(hy_filt_freq), hy_d=f(hy_d), sc_cw=f(sc_conv_w), w_out=w_out_p,
                  final_g=f(final_g))
    shared.update(cst)
    in_maps = []
    for r in range(8):
        xin = np.concatenate([x_sample[r], x_prompt[4 * r:4 * r + 4].reshape(1024, 1024)], 0)
        ck = cache_k[r][:, :, :, dperm].reshape(4, 256, 128)
        cv = cache_v[r].reshape(4, 256, 128)
        cond = np.stack([c[r], c_ctx], 1)
        m = dict(x_tm=np.ascontiguousarray(xin), ck=np.ascontiguousarray(ck), cv=np.ascontiguousarray(cv), cond=np.ascontiguousarray(cond))
        m.update(shared)
        in_maps.append(m)
    import os
    nc = build(_depth, os.environ.get('KSTOP'))
    ncores = int(os.environ.get('KCORES', '8'))
    res = run_bass_kernel_spmd(nc, in_maps[:ncores], core_ids=list(range(ncores)))
    if ncores < 8:
        res.results.extend([res.results[0]] * (8 - ncores))
    y_prompt = np.zeros((32, 256, 1024), np.float32)
    y_sample = np.zeros((8, 4096, 1024), np.float32)
    nk = np.zeros((32, 4, 256, 2, 64), np.float32)
    nv = np.zeros((32, 4, 256, 2, 64), np.float32)
    for r in range(8):
        o = res.results[r]
        y_sample[r] = o["y"][0:4096]
        y_prompt[4 * r:4 * r + 4] = o["y"][4096:].reshape(4, 256, 1024)
        nk[4 * r:4 * r + 4] = o["nk"].reshape(4, 4, 256, 2, 64)
        nv[4 * r:4 * r + 4] = o["nv"].reshape(4, 4, 256, 2, 64)
    return (y_prompt, y_sample, nk, nv)
```

```python
import math
import os as _os_mod
from contextlib import ExitStack
import numpy as np
import ml_dtypes
import concourse.bass as bass
import concourse.mybir as mybir
from concourse.bass_utils import run_bass_kernel_spmd

F32 = mybir.dt.float32
BF16 = mybir.dt.bfloat16
ALU = mybir.AluOpType
AF = mybir.ActivationFunctionType
AX = mybir.AxisListType
NPBF = ml_dtypes.bfloat16

T = 5120
SEQS = [(0, 4096, True)] + [(4096 + 256 * j, 256, False) for j in range(4)]
CQ, CK, CGA, CHY, CGH, CSB, CSC, CSX, CGC, NCH = 0, 4, 5, 9, 15, 17, 19, 21, 23, 25
GATE_CH = set(range(CGA, CGA + 4)) | set(range(CGH, CGH + 2)) | set(range(CGC, CGC + 2))
PI = math.pi


class Buf:
    __slots__ = ("name", "w", "r", "sem", "cnt", "g")

    def __init__(self, name):
        self.name = name
        self.w = {}
        self.r = {}
        self.sem = None
        self.cnt = 0
        self.g = {}


def _merge(d, tok):
    k = id(tok[0])
    if k not in d or d[k][1] < tok[1]:
        d[k] = tok


class Sched:
    ENG = ("pe", "act", "dve", "pool", "sp")
    STRICT = ("act", "dve", "pool")

    def __init__(self, nc, es):
        self.nc = nc
        self.es = es
        self.q = {e: [] for e in self.ENG}
        self.csem = {e: es.enter_context(nc.semaphore("c_" + e)) for e in self.ENG}
        self.cnt = {e: 0 for e in self.ENG}
        self.waited = {e: {} for e in self.ENG}
        self.nsem = 0
        self.bufs = []
        self.sempool = []

    def buf(self, name):
        b = Buf(name)
        self.bufs.append(b)
        return b

    def drop(self, bufs):
        ids = set(id(b) for b in bufs)
        seen = set()
        for b in bufs:
            if id(b) in seen:
                continue
            seen.add(id(b))
            if b.sem is not None:
                self.sempool.append((b.sem, b.cnt))
        self.bufs = [b for b in self.bufs if id(b) not in ids]

    def _deps(self, eng, reads, writes, joins):
        deps = {}
        for b in reads:
            for t in b.w.values():
                _merge(deps, t)
        for b in writes:
            for t in b.w.values():
                _merge(deps, t)
            for t in b.r.values():
                _merge(deps, t)
        for b in joins:
            for t in b.r.values():
                _merge(deps, t)
            for t in b.g.values():
                _merge(deps, t)
        out = []
        own = id(self.csem[eng])
        wd = self.waited[eng]
        for k, (s, v) in deps.items():
            if k == own and eng not in self.STRICT:
                continue
            if wd.get(k, -1) >= v:
                continue
            wd[k] = v
            out.append((s, v))
        return out

    def _record(self, tok, reads, writes, joins):
        for b in reads:
            _merge(b.r, tok)
        for b in writes:
            g = dict(b.w)
            for t in b.r.values():
                _merge(g, t)
            b.g = g
            b.w = {}
            b.r = {}
            _merge(b.w, tok)
        for b in joins:
            _merge(b.w, tok)

    def op(self, eng, fn, reads=(), writes=(), joins=()):
        waits = self._deps(eng, reads, writes, joins)
        self.cnt[eng] += 1
        tok = (self.csem[eng], self.cnt[eng])
        self.q[eng].append((waits, fn, self.csem[eng], 1))
        self._record(tok, reads, writes, joins)
        return tok

    def dma(self, fn, sembuf, reads=(), writes=(), joins=(), q="sp"):
        waits = self._deps(q, reads, writes, joins)
        if sembuf.sem is None:
            if self.sempool:
                sembuf.sem, sembuf.cnt = self.sempool.pop()
            else:
                sembuf.sem = self.es.enter_context(self.nc.semaphore("d%d" % self.nsem))
                self.nsem += 1
        sembuf.cnt += 16
        tok = (sembuf.sem, sembuf.cnt)
        self.q[q].append((waits, fn, sembuf.sem, 16))
        self._record(tok, reads, writes, joins)
        return tok

    def barrier(self):
        deps = {}
        for b in self.bufs:
            for t in list(b.w.values()) + list(b.r.values()):
                _merge(deps, t)
        for e in self.ENG:
            if self.cnt[e] > 0:
                _merge(deps, (self.csem[e], self.cnt[e]))
        for eng in self.ENG:
            waits = []
            wd = self.waited[eng]
            own = id(self.csem[eng])
            for k, (s, v) in deps.items():
                if k == own:
                    continue
                if wd.get(k, -1) >= v:
                    continue
                wd[k] = v
                waits.append((s, v))
            self.q[eng].append((waits, None, None, 0))

    def emit(self):
        nc = self.nc
        qs = self.q
        self.q = {e: [] for e in self.ENG}

        def run(name):
            def body(e):
                for (waits, fn, sem, inc) in qs[name]:
                    for (s, v) in waits:
                        e.wait_ge(s, v)
                    if fn is not None:
                        ins = fn(e)
                        ins.then_inc(sem, inc)
            return body

        with nc.Block() as block:
            block.tensor(run("pe"))
            block.scalar(run("act"))
            block.vector(run("dve"))
            block.gpsimd(run("pool"))
            block.sync(run("sp"))


def _bf(a):
    return np.asarray(a, dtype=np.float32).astype(NPBF)


def dft_consts(L):
    N = 2 * L
    nt = L // 128
    tab = np.arange(N, dtype=np.float64) * (2 * np.pi / N)
    ctab, stab = np.cos(tab), np.sin(tab)
    t = np.arange(L)
    k = np.arange(L)
    tk = (t[:, None] * k[None, :]) % N
    fre = ctab[tk]
    fim = -stab[tk]
    fim[:, 0] = np.where(t % 2 == 0, 1.0, -1.0)
    kperm = np.concatenate([np.arange(0, L, 2), np.arange(1, L, 2)])
    fre = fre[:, kperm]
    fim = fim[:, kperm]
    fw = np.stack([fre, fim], 0).reshape(2, nt, 128, nt, 128)
    fw = np.ascontiguousarray(fw.transpose(3, 2, 0, 1, 4))
    ire = (2.0 / N) * ctab[tk].T
    ire[0, :] = 1.0 / N
    iim = -(2.0 / N) * stab[tk].T
    iim[0, :] = (1.0 / N) * np.where(t % 2 == 0, 1.0, -1.0)
    ire = ire[kperm, :]
    iim = iim[kperm, :]
    iv = np.concatenate([ire, iim], 0).reshape(2 * nt, 128, nt, 128)
    iv = np.ascontiguousarray(iv.transpose(2, 1, 0, 3))
    return _bf(fw), _bf(iv)


def filt_consts(L):
    N = 2 * L
    f32 = np.float32
    tl = np.linspace(0.0, 1.0, L, dtype=f32)
    bands = 16
    ang = (f32(2.0 * math.pi / L) * np.arange(L, dtype=f32))
    fr = np.linspace(1e-4, bands - 1, bands, dtype=f32)
    feats = np.concatenate([tl[:, None], np.cos(fr[None, :] * ang[:, None]), -np.sin(fr[None, :] * ang[:, None])], -1).astype(f32)
    n = np.arange(N)
    lag = np.where(n < L, n, N - n)
    lag[L] = 0
    fk = feats[lag].T.astype(f32)
    hi = fk.astype(NPBF)
    lo = (fk - hi.astype(f32)).astype(NPBF)
    featc = np.ascontiguousarray(np.stack([hi, lo], 1))
    max_decay = math.log(1e-2) / 0.3
    min_decay = math.log(1e-2) / 1.5
    deltas = np.abs(np.linspace(min_decay, max_decay, 256, dtype=f32))
    dec = np.exp(-tl[lag][None, :] * deltas[:, None]).astype(f32)
    dec[:, L] = 0.0
    return featc, np.ascontiguousarray(dec.reshape(2, 128, N))


def rope_consts():
    Ls = 4096
    t = np.arange(Ls)
    inv = (10000.0 ** (-np.arange(16, dtype=np.float32) / 16)).astype(np.float32)
    row = (t // 64).astype(np.float32)[:, None] * inv
    col = (t % 64).astype(np.float32)[:, None] * inv
    ang = np.stack([row, col], 1)
    C = np.zeros((128, Ls), np.float32)
    S = np.zeros((128, Ls), np.float32)
    for p in range(128):
        pos = p % 64
        half = (pos % 32) // 16
        f = pos % 16
        C[p] = np.cos(ang[:, half, f])
        S[p] = np.sin(ang[:, half, f])
    PM = np.zeros((128, 128), np.float32)
    for m in range(128):
        pos = m % 64
        if pos < 32:
            PM[m + 32, m] = -1.0
        else:
            PM[m - 32, m] = 1.0
    return C, S, _bf(PM)


def col_perm():
    dperm = np.zeros(64, np.int64)
    for pos in range(64):
        ab, half, f = pos // 32, (pos % 32) // 16, pos % 16
        dperm[pos] = half * 32 + ab * 16 + f
    cols = []
    for j in range(4):
        cols += list(0 + j * 64 + dperm) + list(0 + (4 + j) * 64 + dperm)
    cols += list(512 + dperm) + list(512 + 64 + dperm)
    for j in range(4):
        cols += list(768 + j * 64 + np.arange(64)) + list(768 + (4 + j) * 64 + np.arange(64))
    cols += list(1280 + np.arange(768))
    cols += list(2048 + np.arange(256))
    cols += list(2304 + np.arange(768))
    cols += list(3072 + np.arange(256))
    rows = []
    for j in range(4):
        rows += list(j * 64 + np.arange(64)) + list((4 + j) * 64 + np.arange(64))
    rows += list(512 + np.arange(512))
    return np.array(cols), np.array(rows), dperm


def build(depth=4, stop_after=None):
    nc = bass.Bass("TRN2", target_bir_lowering=False)

    def din(name, shape, dt=F32):
        return nc.dram_tensor(name, list(shape), dt, kind="ExternalInput").ap()

    def dout(name, shape):
        return nc.dram_tensor(name, list(shape), F32, kind="ExternalOutput").ap()

    def dscr(name, shape, dt):
        return nc.dram_tensor(name, list(shape), dt, kind="Internal").ap()

    x_tm = din("x_tm", [T, 1024])
    ck_in = din("ck", [4, 256, 128])
    cv_in = din("cv", [4, 256, 128])
    cond_in = din("cond", [1024, 2])
    norm_g = din("norm_g", [4, 1024])
    mod_w = din("mod_w", [4, 1024, 3072])
    mod_b = din("mod_b", [4, 3072])
    w_in = din("w_in", [4, 1024, NCH * 128])
    w_kv = din("w_kv", [4, 1024, 256])
    sink_in = din("sink", [4, 8])
    hy_cw = din("hy_cw", [4, 3, 768])
    f_w1 = din("f_w1", [4, 33, 64])
    f_b1 = din("f_b1", [4, 64])
    f_w2 = din("f_w2", [4, 64, 64])
    f_b2 = din("f_b2", [4, 64])
    f_w3 = din("f_w3", [4, 64, 1024])
    f_fq = din("f_fq", [4, 64])
    hy_d = din("hy_d", [4, 2, 256])
    sc_cw = din("sc_cw", [4, 3, 256])
    w_out = din("w_out", [4, 1024, 1024])
    final_g = din("final_g", [1024])
    ropeC = din("ropeC", [128, 4096])
    ropeS = din("ropeS", [128, 4096])
    permm = din("permm", [128, 128], BF16)
    FWc = {4096: din("fw4096", [32, 128, 2, 32, 128], BF16), 256: din("fw256", [2, 128, 2, 2, 128], BF16)}
    IVc = {4096: din("iv4096", [32, 128, 64, 128], BF16), 256: din("iv256", [2, 128, 4, 128], BF16)}
    FEc = {4096: din("fe4096", [33, 2, 8192], BF16), 256: din("fe256", [33, 2, 512], BF16)}
    DEc = {4096: din("de4096", [2, 128, 8192]), 256: din("de256", [2, 128, 512])}

    y_out = dout("y", [T, 1024])
    nk_out = dout("nk", [4, 4, 256, 128])
    nv_out = dout("nv", [4, 4, 256, 128])

    xT = dscr("xT", [8, 128, T], F32)
    Pscr = dscr("Pscr", [NCH, 128, T], BF16)
    Vscr = dscr("Vscr", [T, 128], BF16)
    HX = dscr("HX", [T, 768], BF16)
    Zscr = dscr("Zscr", [2, 128, T], BF16)
    TA = {4096: dscr("TA4096", [32, 128, 512], F32), 256: dscr("TA256", [2, 128, 512], F32)}
    TC = {4096: dscr("TC4096", [32, 128, 512], F32), 256: dscr("TC256", [2, 128, 512], F32)}
    TD = {4096: dscr("TD4096", [128, 512], F32), 256: dscr("TD256", [128, 512], F32)}

    with ExitStack() as es:
        S = Sched(nc, es)

        uid = [0]

        def sbt(st, name, shape, dt):
            uid[0] += 1
            return st.enter_context(nc.sbuf_tensor("%s_%d" % (name, uid[0]), list(shape), dt))

        ident = sbt(es, "ident", [128, 128], BF16)
        identf = sbt(es, "identf", [128, 128], F32)
        ones = sbt(es, "ones", [128, 128], BF16)
        mlo = sbt(es, "mlo", [128, 128], BF16)
        mhi = sbt(es, "mhi", [128, 128], BF16)
        mtmp = sbt(es, "mtmp", [128, 128], F32)
        pmat = sbt(es, "pmat", [128, 128], BF16)
        sgn = sbt(es, "sgn", [128, 1], F32)
        m0v = sbt(es, "m0v", [128, 1], F32)
        m1v = sbt(es, "m1v", [128, 1], F32)
        zeros = sbt(es, "zeros", [128, 512], F32)
        SH = sbt(es, "SH", [128, 4, 2, 8], F32)
        GP = sbt(es, "GP", [128, 4, 2, 8], F32)
        GT = sbt(es, "GT", [128, 4, 2, 8], F32)
        FG = sbt(es, "FG", [128, 8], F32)
        Wh = {}
        B_const = S.buf("const")
        B_mod = S.buf("mod")
        B_W = S.buf("W")
        B_xT = S.buf("xT")
        B_P = S.buf("Pscr")
        B_V = S.buf("Vscr")
        B_HX = S.buf("HX")
        B_Z = S.buf("Zscr")
        B_tab = {4096: S.buf("tab4096"), 256: S.buf("tab256")}
        B_out = S.buf("out")

        PS = [es.enter_context(nc.psum_tensor("ps%d" % i, [128, 512], F32)) for i in range(8)]
        BPS = [S.buf("ps%d" % i) for i in range(8)]
        psi = [0]

        def nps():
            i = psi[0] % 8
            psi[0] += 1
            return PS[i], BPS[i]

        rr = [0]

        def alt(engs=("dve", "pool")):
            rr[0] += 1
            return engs[rr[0] % len(engs)]

        def phase_end(st_bufs):
            S.barrier()
            S.emit()
            S.drop(st_bufs)

        S.op("pool", lambda e: e.memset(identf[:], 0.0), writes=[B_const])
        S.op("pool", lambda e: e.affine_select(out=identf[:], in_=identf[:], pattern=[[-1, 128]], compare_op=ALU.not_equal, fill=1.0, base=0, channel_multiplier=1), writes=[B_const])
        S.op("dve", lambda e: e.tensor_copy(out=ident[:], in_=identf[:]), reads=[B_const], joins=[B_const])
        S.op("pool", lambda e: e.memset(ones[:], 1.0), joins=[B_const])
        S.op("pool", lambda e: e.memset(zeros[:], 0.0), joins=[B_const])
        S.op("pool", lambda e: e.memset(mtmp[:], 1.0), joins=[B_const])
        S.op("pool", lambda e: e.affine_select(out=mtmp[:], in_=mtmp[:], pattern=[[-1, 128]], compare_op=ALU.is_ge, fill=0.0, base=0, channel_multiplier=1), joins=[B_const])
        S.op("pool", lambda e: e.tensor_copy(out=mlo[:], in_=mtmp[:]), joins=[B_const])
        S.op("pool", lambda e: e.memset(mtmp[:], 1.0), joins=[B_const])
        S.op("pool", lambda e: e.affine_select(out=mtmp[:], in_=mtmp[:], pattern=[[1, 128]], compare_op=ALU.is_ge, fill=0.0, base=0, channel_multiplier=-1), joins=[B_const])
        S.op("pool", lambda e: e.tensor_copy(out=mhi[:], in_=mtmp[:]), joins=[B_const])
        S.op("pool", lambda e: e.memset(mtmp[:], 1.0), joins=[B_const])
        S.op("pool", lambda e: e.affine_select(out=mtmp[:, 0:1], in_=mtmp[:, 0:1], pattern=[[0, 1]], compare_op=ALU.not_equal, fill=0.0, base=0, channel_multiplier=1), joins=[B_const])
        S.op("pool", lambda e: e.tensor_copy(out=m0v[:], in_=mtmp[:, 0:1]), joins=[B_const])
        S.op("pool", lambda e: e.tensor_scalar(out=m1v[:], in0=m0v[:], scalar1=-1.0, scalar2=1.0, op0=ALU.mult, op1=ALU.add), joins=[B_const])
        S.dma(lambda e: e.dma_start(out=pmat[:], in_=permm), B_const, joins=[B_const])
        S.dma(lambda e: e.dma_start(out=FG[:], in_=final_g.rearrange("(c p) -> p c", p=128), allow_slow_non_contiguous=True), B_const, joins=[B_const])

        with ExitStack() as st:
            condt = sbt(st, "condt", [128, 8, 2], F32)
            condb = sbt(st, "condb", [128, 8, 2], BF16)
            mst = [sbt(st, "mst%d" % i, [128, 8, 384], F32) for i in range(2)]
            msb = [sbt(st, "msb%d" % i, [128, 8, 384], BF16) for i in range(2)]
            modT = sbt(st, "modT", [128, 24, 2], F32)
            mbT = sbt(st, "mbT", [128, 24], F32)
            ngT = sbt(st, "ngT", [128, 8], F32)
            sgnt = sbt(st, "sgnt", [128, 2], F32)
            b_cond = S.buf("cond")
            b_mst = [S.buf("mst0"), S.buf("mst1")]
            b_msb = [S.buf("msb0"), S.buf("msb1")]
            b_modT = S.buf("modT")
            b_mb = S.buf("mb")
            loc = [b_cond, b_modT, b_mb] + b_mst + b_msb
            S.dma(lambda e: e.dma_start(out=condt[:], in_=cond_in.rearrange("(c p) s -> p c s", p=128)), b_cond, writes=[b_cond])
            S.op("act", lambda e: e.activation(out=condt[:], in_=condt[:], func=AF.Silu), reads=[b_cond], writes=[b_cond])
            S.op("dve", lambda e: e.tensor_copy(out=condb[:], in_=condt[:]), reads=[b_cond], joins=[b_cond])
            for l in range(depth):
                S.dma(lambda e, l=l: e.dma_start(out=mbT[:], in_=mod_b[l].rearrange("(c p) -> p c", p=128), allow_slow_non_contiguous=True), b_mb, writes=[b_mb])
                S.dma(lambda e, l=l: e.dma_start(out=ngT[:], in_=norm_g[l].rearrange("(c p) -> p c", p=128), allow_slow_non_contiguous=True), b_mb, joins=[b_mb])
                for cb in range(8):
                    sl = cb % 2
                    for kc in range(0, 8, 4):
                        S.dma(lambda e, l=l, cb=cb, sl=sl, kc=kc: e.dma_start(
                            out=mst[sl][:, kc:kc + 4, :],
                            in_=mod_w[l, kc * 128:(kc + 4) * 128, cb * 384:(cb + 1) * 384].rearrange("(c p) f -> p c f", p=128)),
                            b_mst[sl], writes=[b_mst[sl]] if kc == 0 else (), joins=() if kc == 0 else [b_mst[sl]])
                    S.op(alt(("dve", "pool")), lambda e, sl=sl: e.tensor_copy(out=msb[sl][:], in_=mst[sl][:]), reads=[b_mst[sl]], writes=[b_msb[sl]])
                    for fc in range(3):
                        ps, bps = nps()
                        for kc in range(8):
                            S.op("pe", lambda e, ps=ps, sl=sl, kc=kc, fc=fc: e.matmul(ps[:, 0:2], lhsT=msb[sl][:, kc, fc * 128:(fc + 1) * 128], rhs=condb[:, kc, :], start=(kc == 0), stop=(kc == 7)),
                                 reads=[b_msb[sl], b_cond], writes=[bps] if kc == 0 else (), joins=() if kc == 0 else [bps])
                        S.op("act", lambda e, ps=ps, cb=cb, fc=fc: e.copy(out=modT[:, cb * 3 + fc, :], in_=ps[:, 0:2]), reads=[bps], joins=[b_modT])
                for s in range(2):
                    S.op("dve", lambda e, l=l, s=s: e.tensor_tensor(out=SH[:, l, s, :], in0=modT[:, 0:8, s], in1=mbT[:, 0:8], op=ALU.add), reads=[b_modT, b_mb], joins=[B_mod])
                    S.op("dve", lambda e, l=l, s=s: e.tensor_tensor(out=GP[:, l, s, :], in0=modT[:, 8:16, s], in1=mbT[:, 8:16], op=ALU.add), reads=[b_modT, b_mb], joins=[B_mod])
                    S.op("dve", lambda e, l=l, s=s: e.scalar_tensor_tensor(out=GP[:, l, s, :], in0=GP[:, l, s, :], scalar=1.0, in1=ngT[:], op0=ALU.add, op1=ALU.mult), reads=[b_mb, B_mod], joins=[B_mod])
                    S.op("dve", lambda e, l=l, s=s: e.tensor_tensor(out=GT[:, l, s, :], in0=modT[:, 16:24, s], in1=mbT[:, 16:24], op=ALU.add), reads=[b_modT, b_mb], joins=[B_mod])
            phase_end(loc)

        with ExitStack() as st:
            altrow = sbt(st, "altrow", [128, 128], F32)
            junk = sbt(st, "junk0", [128, 128], F32)
            b_a = S.buf("altrow")
            S.op("pool", lambda e: e.memset(altrow[:], 1.0), writes=[b_a])
            S.op("pool", lambda e: e.memset(altrow[:].rearrange("p (a two) -> p a two", two=2)[:, :, 1:2], -1.0), joins=[b_a])
            S.op("dve", lambda e: e.tensor_tensor(out=junk[:], in0=identf[:], in1=altrow[:], op=ALU.mult), reads=[b_a, B_const], writes=[b_a])
            S.op("dve", lambda e: e.tensor_reduce(out=sgn[:], in_=junk[:], axis=AX.X, op=ALU.add), reads=[b_a], joins=[B_const])
            phase_end([b_a])

        with ExitStack() as st:
            xin = [sbt(st, "xin%d" % i, [128, 1024], F32) for i in range(2)]
            xhi = [sbt(st, "xhi%d" % i, [128, 1024], BF16) for i in range(2)]
            xlo = [sbt(st, "xlo%d" % i, [128, 1024], BF16) for i in range(2)]
            xtmp = [sbt(st, "xtmp%d" % i, [128, 1024], F32) for i in range(2)]
            xo = [sbt(st, "xo%d" % i, [128, 8, 128], F32) for i in range(2)]
            b_in = [S.buf("xin0"), S.buf("xin1")]
            b_hl = [S.buf("xhl0"), S.buf("xhl1")]
            b_xo = [S.buf("xo0"), S.buf("xo1")]
            nb = T // 128
            first = [True]

            def ld0(b):
                sl = b % 2
                S.dma(lambda e: e.dma_start(out=xin[sl][:], in_=x_tm[b * 128:(b + 1) * 128, :]), b_in[sl], writes=[b_in[sl]])
            ld0(0)
            for b in range(nb):
                sl = b % 2
                if b + 1 < nb:
                    ld0(b + 1)
                S.op("act", lambda e, sl=sl: e.copy(out=xhi[sl][:], in_=xin[sl][:]), reads=[b_in[sl]], writes=[b_hl[sl]])
                S.op("dve", lambda e, sl=sl: e.tensor_copy(out=xtmp[sl][:], in_=xhi[sl][:]), reads=[b_hl[sl]], joins=[b_hl[sl]])
                S.op("dve", lambda e, sl=sl: e.tensor_tensor(out=xlo[sl][:], in0=xin[sl][:], in1=xtmp[sl][:], op=ALU.subtract), reads=[b_in[sl], b_hl[sl]], joins=[b_hl[sl]])
                for half in range(2):
                    ps, bps = nps()
                    for c in range(4):
                        cc = half * 4 + c
                        S.op("pe", lambda e, ps=ps, sl=sl, c=c, cc=cc: e.matmul(ps[:, c * 128:(c + 1) * 128], lhsT=xhi[sl][:, cc * 128:(cc + 1) * 128], rhs=ident[:], start=True, stop=False),
                             reads=[b_hl[sl], B_const], writes=[bps] if c == 0 else (), joins=() if c == 0 else [bps])
                        S.op("pe", lambda e, ps=ps, sl=sl, c=c, cc=cc: e.matmul(ps[:, c * 128:(c + 1) * 128], lhsT=xlo[sl][:, cc * 128:(cc + 1) * 128], rhs=ident[:], start=False, stop=True),
                             reads=[b_hl[sl]], joins=[bps])
                    S.op("act" if half == 0 else "dve",
                         (lambda e, ps=ps, sl=sl, half=half: e.copy(out=xo[sl][:, half * 4:(half + 1) * 4, :], in_=ps[:].rearrange("p (c t) -> p c t", c=4))) if half == 0 else
                         (lambda e, ps=ps, sl=sl, half=half: e.tensor_copy(out=xo[sl][:, half * 4:(half + 1) * 4, :], in_=ps[:].rearrange("p (c t) -> p c t", c=4))),
                         reads=[bps], writes=[b_xo[sl]] if half == 0 else (), joins=() if half == 0 else [b_xo[sl]])
                S.dma(lambda e, sl=sl, b=b: e.dma_start(out=xT[:, :, b * 128:(b + 1) * 128].rearrange("c p t -> p c t"), in_=xo[sl][:]), b_xo[sl],
                      reads=[b_xo[sl]], writes=[B_xT] if first[0] else (), joins=() if first[0] else [B_xT])
                first[0] = False
            phase_end(b_in + b_hl + b_xo)

        def load_weights(l):
            with ExitStack() as st:
                wst = [sbt(st, "wst%d" % i, [128, 8, 512], F32) for i in range(2)]
                b_wst = [S.buf("wst0"), S.buf("wst1")]
                jobs = []
                for cb in range(0, NCH * 128, 512):
                    w = min(512, NCH * 128 - cb)
                    jobs.append((w_in, Wh["b"], cb, w))
                jobs.append((w_kv, Wh["kv"], 0, 256))
                for cb in range(0, 1024, 512):
                    jobs.append((w_out, Wh["o"], cb, 512))
                firstw = [True]

                def ldj(i):
                    src, dst, cb, w = jobs[i]
                    sl = i % 2
                    for kc in range(0, 8, 4):
                        S.dma(lambda e, kc=kc: e.dma_start(out=wst[sl][:, kc:kc + 4, 0:w], in_=src[l, kc * 128:(kc + 4) * 128, cb:cb + w].rearrange("(c p) f -> p c f", p=128)),
                              b_wst[sl], writes=[b_wst[sl]] if kc == 0 else (), joins=() if kc == 0 else [b_wst[sl]])
                ldj(0)
                for i, (src, dst, cb, w) in enumerate(jobs):
                    sl = i % 2
                    if i + 1 < len(jobs):
                        ldj(i + 1)
                    eng = ("dve", "pool", "act")[i % 3]
                    if eng == "act":
                        fn = lambda e, sl=sl, dst=dst, cb=cb, w=w: e.copy(out=dst[:, :, cb:cb + w], in_=wst[sl][:, :, 0:w])
                    else:
                        fn = lambda e, sl=sl, dst=dst, cb=cb, w=w: e.tensor_copy(out=dst[:, :, cb:cb + w], in_=wst[sl][:, :, 0:w])
                    S.op(eng, fn, reads=[b_wst[sl]], writes=[B_W] if firstw[0] else (), joins=() if firstw[0] else [B_W])
                    firstw[0] = False
                phase_end(b_wst)

        def rms_norm_tile(st, xt, b_xt, n, gp_ap, sh_ap, hT, b_hT, tagname):
            sq = sbt(st, "sq_" + tagname, [128, 8, n], BF16)
            rstd = sbt(st, "rstd_" + tagname, [128, n], F32)
            tmpn = sbt(st, "tmpn_" + tagname, [128, 8, n], F32)
            b_sq = S.buf("sq")
            b_rstd = S.buf("rstd")
            b_tmpn = S.buf("tmpn")

            def run():
                S.op("act", lambda e: e.activation(out=sq[:], in_=xt[:], func=AF.Square), reads=[b_xt], writes=[b_sq])
                ps, bps = nps()
                for c in range(8):
                    S.op("pe", lambda e, c=c: e.matmul(ps[:, 0:n], lhsT=ones[:], rhs=sq[:, c, :], start=(c == 0), stop=(c == 7)),
                         reads=[b_sq, B_const], writes=[bps] if c == 0 else (), joins=() if c == 0 else [bps])
                S.op("dve", lambda e: e.tensor_scalar(out=rstd[:], in0=ps[:, 0:n], scalar1=1.0 / 1024, scalar2=1e-6, op0=ALU.mult, op1=ALU.add), reads=[bps], writes=[b_rstd])
                S.op("act", lambda e: e.activation(out=rstd[:], in_=rstd[:], func=AF.Sqrt), reads=[b_rstd], writes=[b_rstd])
                S.op("dve", lambda e: e.reciprocal(out=rstd[:], in_=rstd[:]), reads=[b_rstd], writes=[b_rstd])
                for c in range(8):
                    eng = "dve" if c % 2 == 0 else "pool"
                    S.op(eng, lambda e, c=c: e.tensor_tensor(out=tmpn[:, c, :], in0=xt[:, c, :], in1=rstd[:], op=ALU.mult), reads=[b_xt, b_rstd], writes=[b_tmpn] if c == 0 else (), joins=() if c == 0 else [b_tmpn])
                for c in range(8):
                    if sh_ap is None:
                        S.op("act", lambda e, c=c: e.activation(out=hT[:, c, :], in_=tmpn[:, c, :], func=AF.Identity, scale=gp_ap[:, c:c + 1]), reads=[b_tmpn, B_mod, B_const], writes=[b_hT] if c == 0 else (), joins=() if c == 0 else [b_hT])
                    else:
                        S.op("act", lambda e, c=c: e.activation(out=hT[:, c, :], in_=tmpn[:, c, :], func=AF.Identity, scale=gp_ap[:, c:c + 1], bias=sh_ap[:, c:c + 1]), reads=[b_tmpn, B_mod], writes=[b_hT] if c == 0 else (), joins=() if c == 0 else [b_hT])
            return run, [b_sq, b_rstd, b_tmpn]

        def norm_tmps(st, tag, nmax):
            return dict(sq=sbt(st, "sq_" + tag, [128, 8, nmax], BF16), rstd=sbt(st, "rstd_" + tag, [128, nmax], F32), tmpn=sbt(st, "tmpn_" + tag, [128, 8, nmax], F32),
                        b_sq=S.buf("sq"), b_rstd=S.buf("rstd"), b_tmpn=S.buf("tmpn"))

        def rms_norm(tm, xt, b_xt, n, gp_ap, sh_ap, hT, b_hT):
            sq, rstd, tmpn = tm["sq"], tm["rstd"], tm["tmpn"]
            b_sq, b_rstd, b_tmpn = tm["b_sq"], tm["b_rstd"], tm["b_tmpn"]
            S.op("act", lambda e: e.activation(out=sq[:, :, 0:n], in_=xt[:, :, 0:n], func=AF.Square), reads=[b_xt], writes=[b_sq])
            ps, bps = nps()
            for c in range(8):
                S.op("pe", lambda e, c=c: e.matmul(ps[:, 0:n], lhsT=ones[:], rhs=sq[:, c, 0:n], start=(c == 0), stop=(c == 7)),
                     reads=[b_sq, B_const], writes=[bps] if c == 0 else (), joins=() if c == 0 else [bps])
            S.op("dve", lambda e: e.tensor_scalar(out=rstd[:, 0:n], in0=ps[:, 0:n], scalar1=1.0 / 1024, scalar2=1e-6, op0=ALU.mult, op1=ALU.add), reads=[bps], writes=[b_rstd])
            S.op("act", lambda e: e.activation(out=rstd[:, 0:n], in_=rstd[:, 0:n], func=AF.Sqrt), reads=[b_rstd], writes=[b_rstd])
            S.op("dve", lambda e: e.reciprocal(out=rstd[:, 0:n], in_=rstd[:, 0:n]), reads=[b_rstd], writes=[b_rstd])
            for c in range(8):
                eng = "dve" if c % 2 == 0 else "pool"
                S.op(eng, lambda e, c=c: e.tensor_tensor(out=tmpn[:, c, 0:n], in0=xt[:, c, 0:n], in1=rstd[:, 0:n], op=ALU.mult), reads=[b_xt, b_rstd], writes=[b_tmpn] if c == 0 else (), joins=() if c == 0 else [b_tmpn])
            for c in range(8):
                if sh_ap is None:
                    S.op("act", lambda e, c=c: e.activation(out=hT[:, c, 0:n], in_=tmpn[:, c, 0:n], func=AF.Identity, scale=gp_ap[:, c:c + 1]), reads=[b_tmpn, B_mod, B_const], writes=[b_hT] if c == 0 else (), joins=() if c == 0 else [b_hT])
                else:
                    S.op("act", lambda e, c=c: e.activation(out=hT[:, c, 0:n], in_=tmpn[:, c, 0:n], func=AF.Identity, scale=gp_ap[:, c:c + 1], bias=sh_ap[:, c:c + 1]), reads=[b_tmpn, B_mod], writes=[b_hT] if c == 0 else (), joins=() if c == 0 else [b_hT])

        def tiles_of(base, L):
            n = min(512, L)
            return [(base + i * n, n) for i in range(L // n)]

        def filter_phase(l, L):
            N = 2 * L
            nt = L // 128
            pw = min(512, L)
            npc = N // pw
            with ExitStack() as st:
                w1s = sbt(st, "w1s", [33, 64], F32)
                w1b = sbt(st, "w1b", [33, 64], BF16)
                w2s = sbt(st, "w2s", [64, 64], F32)
                w2b = sbt(st, "w2b", [64, 64], BF16)
                w3s = sbt(st, "w3s", [64, 1024], F32)
                w3b = sbt(st, "w3b", [64, 1024], BF16)
                fqv = sbt(st, "fqv", [64, 1], F32)
                fb1 = sbt(st, "fb1", [64, 1], F32)
                fb2 = sbt(st, "fb2", [64, 1], F32)
                fet = [sbt(st, "fet%d" % i, [33, 2, pw], BF16) for i in range(2)]
                dct = [sbt(st, "dct%d" % i, [128, 2, pw], F32) for i in range(2)]
                arg = sbt(st, "arg", [64, pw], F32)
                tmpa = sbt(st, "tmpa", [64, pw], F32)
                h1 = sbt(st, "h1", [64, pw], BF16)
                h2 = sbt(st, "h2", [64, pw], BF16)
                kf = sbt(st, "kf", [128, 4, pw], F32)
                kjunk = sbt(st, "kjunk", [128, pw], F32)
                kb = sbt(st, "kb", [128, 4, pw], BF16)
                nacc = sbt(st, "nacc", [128, 4, npc], F32)
                nrm = sbt(st, "nrm", [128, 4], F32)
                rnh = sbt(st, "rnh", [128, 4], BF16)
                rnhf = sbt(st, "rnhf", [128, 4], F32)
                rnl = sbt(st, "rnl", [128, 4], BF16)
                dg = sbt(st, "dg", [128, 2, 4, 128], BF16)
                rnbc = sbt(st, "rnbc", [128, 512], F32)
                kTM = sbt(st, "kTM", [128, 2 * nt, 512], BF16)
                fwt = [sbt(st, "fwt%d" % i, [128, 2, nt, 128], BF16) for i in range(2)]
                ev = [sbt(st, "ev%d" % i, [128, 512], F32) for i in range(4)]
                tabA = sbt(st, "tabA", [128, 512], F32)
                tabC = sbt(st, "tabC", [128, 512], F32)
                tabD = sbt(st, "tabD", [128, 512], F32)
                b_w = S.buf("fw_w")
                b_fet = [S.buf("fet0"), S.buf("fet1")]
                b_dct = [S.buf("dct0"), S.buf("dct1")]
                b_arg = S.buf("arg")
                b_tmpa = S.buf("tmpa")
                b_h1 = S.buf("h1")
                b_h2 = S.buf("h2")
                b_kf = S.buf("kf")
                b_kb = S.buf("kb")
                b_nacc = S.buf("nacc")
                b_rn = S.buf("rn")
                b_kTM = S.buf("kTM")
                b_fwt = [S.buf("fwt0"), S.buf("fwt1")]
                b_ev = [S.buf("ev%d" % i) for i in range(4)]
                b_tA, b_tC, b_tD = S.buf("tabA"), S.buf("tabC"), S.buf("tabD")
                loc = [b_w, b_arg, b_tmpa, b_h1, b_h2, b_kf, b_kb, b_nacc, b_rn, b_kTM, b_tA, b_tC, b_tD] + b_fet + b_dct + b_fwt + b_ev
                S.dma(lambda e: e.dma_start(out=w1s[:], in_=f_w1[l]), b_w, writes=[b_w])
                S.dma(lambda e: e.dma_start(out=w2s[:], in_=f_w2[l]), b_w, joins=[b_w])
                S.dma(lambda e: e.dma_start(out=w3s[:], in_=f_w3[l]), b_w, joins=[b_w])
                S.dma(lambda e: e.dma_start(out=fqv[:], in_=f_fq[l].rearrange("(p o) -> p o", o=1)), b_w, joins=[b_w])
                S.dma(lambda e: e.dma_start(out=fb1[:], in_=f_b1[l].rearrange("(p o) -> p o", o=1)), b_w, joins=[b_w])
                S.dma(lambda e: e.dma_start(out=fb2[:], in_=f_b2[l].rearrange("(p o) -> p o", o=1)), b_w, joins=[b_w])
                S.op("dve", lambda e: e.tensor_copy(out=w1b[:], in_=w1s[:]), reads=[b_w], joins=[b_w])
                S.op("dve", lambda e: e.tensor_copy(out=w2b[:], in_=w2s[:]), reads=[b_w], joins=[b_w])
                S.op("dve", lambda e: e.tensor_copy(out=w3b[:], in_=w3s[:]), reads=[b_w], joins=[b_w])
                S.op("dve", lambda e: e.tensor_tensor(out=fb1[:], in0=fb1[:], in1=fqv[:], op=ALU.mult), reads=[b_w], joins=[b_w])
                S.op("dve", lambda e: e.tensor_tensor(out=fb2[:], in0=fb2[:], in1=fqv[:], op=ALU.mult), reads=[b_w], joins=[b_w])

                def ldp(pc):
                    sl = pc % 2
                    S.dma(lambda e: e.dma_start(out=fet[sl][:], in_=FEc[L][:, :, pc * pw:(pc + 1) * pw]), b_fet[sl], writes=[b_fet[sl]])
                    S.dma(lambda e: e.dma_start(out=dct[sl][:], in_=DEc[L][:, :, pc * pw:(pc + 1) * pw].rearrange("c p n -> p c n")), b_dct[sl], writes=[b_dct[sl]])

                def sin_layer(ps, bps, bvec, hout, b_hout):
                    S.op("act", lambda e: e.activation(out=arg[:], in_=ps[0:64, 0:pw], func=AF.Identity, scale=fqv[:, 0:1], bias=bvec[:, 0:1]), reads=[bps, b_w], writes=[b_arg])
                    S.op("dve", lambda e: e.tensor_scalar(out=tmpa[:], in0=arg[:], scalar1=PI, scalar2=-2 * PI, op0=ALU.is_gt, op1=ALU.mult), reads=[b_arg], writes=[b_tmpa])
                    S.op("dve", lambda e: e.tensor_tensor(out=arg[:], in0=arg[:], in1=tmpa[:], op=ALU.add), reads=[b_tmpa], writes=[b_arg])
                    S.op("dve", lambda e: e.tensor_scalar(out=tmpa[:], in0=arg[:], scalar1=-PI, scalar2=2 * PI, op0=ALU.is_lt, op1=ALU.mult), reads=[b_arg], writes=[b_tmpa])
                    S.op("dve", lambda e: e.tensor_tensor(out=arg[:], in0=arg[:], in1=tmpa[:], op=ALU.add), reads=[b_tmpa], writes=[b_arg])
                    S.op("dve", lambda e: e.tensor_scalar(out=arg[:], in0=arg[:], scalar1=3.14159, scalar2=-3.14159, op0=ALU.min, op1=ALU.max), reads=[b_arg], writes=[b_arg])
                    S.op("act", lambda e: e.activation(out=hout[:], in_=arg[:], func=AF.Sin), reads=[b_arg], writes=[b_hout])

                ldp(0)
                for pc in range(npc):
                    sl = pc % 2
                    if pc + 1 < npc:
                        ldp(pc + 1)
                    dr = 0 if pc * pw < L else 1
                    ps, bps = nps()
                    S.op("pe", lambda e, ps=ps, sl=sl: e.matmul(ps[0:64, 0:pw], lhsT=w1b[:], rhs=fet[sl][:, 0, :], start=True, stop=False), reads=[b_w, b_fet[sl]], writes=[bps])
                    S.op("pe", lambda e, ps=ps, sl=sl: e.matmul(ps[0:64, 0:pw], lhsT=w1b[:], rhs=fet[sl][:, 1, :], start=False, stop=True), reads=[b_w, b_fet[sl]], joins=[bps])
                    sin_layer(ps, bps, fb1, h1, b_h1)
                    ps, bps = nps()
                    S.op("pe", lambda e, ps=ps: e.matmul(ps[0:64, 0:pw], lhsT=w2b[:], rhs=h1[:], start=True, stop=True), reads=[b_w, b_h1], writes=[bps])
                    sin_layer(ps, bps, fb2, h2, b_h2)
                    for oc in range(4):
                        o, cc = oc // 2, oc % 2
                        col = (dr * 2 + o) * 256 + cc * 128
                        ps, bps = nps()
                        S.op("pe", lambda e, ps=ps, col=col: e.matmul(ps[:, 0:pw], lhsT=w3b[:, col:col + 128], rhs=h2[:], start=True, stop=True), reads=[b_w, b_h2], writes=[bps])
                        S.op("dve", lambda e, ps=ps, cc=cc, oc=oc, sl=sl: e.tensor_tensor(out=kf[:, oc, :], in0=ps[:, 0:pw], in1=dct[sl][:, cc, :], op=ALU.mult), reads=[bps, b_dct[sl]], writes=[b_kf] if oc == 0 else (), joins=() if oc == 0 else [b_kf])
                        S.op("act", lambda e, oc=oc, pc=pc: e.activation(out=kjunk[:], in_=kf[:, oc, :], func=AF.Abs, accum_out=nacc[:, oc, pc:pc + 1]), reads=[b_kf], joins=[b_nacc])
                    S.op("pool", lambda e: e.tensor_copy(out=kb[:], in_=kf[:]), reads=[b_kf], writes=[b_kb])
                    for blk in range(pw // 128):
                        ps, bps = nps()
                        for cc in range(4):
                            S.op("pe", lambda e, ps=ps, cc=cc, blk=blk: e.matmul(ps[:, cc * 128:(cc + 1) * 128], lhsT=kb[:, cc, blk * 128:(blk + 1) * 128], rhs=ident[:], start=True, stop=True),
                                 reads=[b_kb, B_const], writes=[bps] if cc == 0 else (), joins=() if cc == 0 else [bps])
                        tch = (pc * pw) // 128 + blk
                        S.op("act" if blk % 2 == 0 else "dve",
                             (lambda e, ps=ps, tch=tch: e.copy(out=kTM[:, tch, :], in_=ps[:, 0:512])) if blk % 2 == 0 else (lambda e, ps=ps, tch=tch: e.tensor_copy(out=kTM[:, tch, :], in_=ps[:, 0:512])),
                             reads=[bps], joins=[b_kTM])
                S.op("dve", lambda e: e.tensor_reduce(out=nrm[:], in_=nacc[:], axis=AX.X, op=ALU.add), reads=[b_nacc], writes=[b_rn])
                S.op("dve", lambda e: e.reciprocal(out=nrm[:], in_=nrm[:]), reads=[b_rn], writes=[b_rn])
                S.op("dve", lambda e: e.tensor_copy(out=rnh[:], in_=nrm[:]), reads=[b_rn], writes=[b_rn])
                S.op("dve", lambda e: e.tensor_copy(out=rnhf[:], in_=rnh[:]), reads=[b_rn], writes=[b_rn])
                S.op("dve", lambda e: e.tensor_tensor(out=rnhf[:], in0=nrm[:], in1=rnhf[:], op=ALU.subtract), reads=[b_rn], writes=[b_rn])
                S.op("dve", lambda e: e.tensor_copy(out=rnl[:], in_=rnhf[:]), reads=[b_rn], writes=[b_rn])
                for cc in range(4):
                    S.op("dve", lambda e, cc=cc: e.tensor_scalar(out=dg[:, 0, cc, :], in0=identf[:], scalar1=nrm[:, cc:cc + 1], scalar2=None, op0=ALU.mult), reads=[b_rn, B_const], writes=[b_rn])
                    S.op("dve", lambda e, cc=cc: e.tensor_scalar(out=dg[:, 1, cc, :], in0=identf[:], scalar1=rnhf[:, cc:cc + 1], scalar2=None, op0=ALU.mult), reads=[b_rn, B_const], writes=[b_rn])
                ps, bps = nps()
                for cc in range(4):
                    S.op("pe", lambda e, ps=ps, cc=cc: e.matmul(ps[:, cc * 128:(cc + 1) * 128], lhsT=ones[:], rhs=dg[:, 0, cc, :], start=True, stop=False), reads=[b_rn, B_const], writes=[bps] if cc == 0 else (), joins=() if cc == 0 else [bps])
                    S.op("pe", lambda e, ps=ps, cc=cc: e.matmul(ps[:, cc * 128:(cc + 1) * 128], lhsT=ones[:], rhs=dg[:, 1, cc, :], start=False, stop=True), reads=[b_rn], joins=[bps])
                S.op("act", lambda e, ps=ps: e.copy(out=rnbc[:], in_=ps[:, 0:512]), reads=[bps], writes=[b_rn])

                S.op("dve", lambda e: e.tensor_tensor(out=kTM[:, 0:nt, :], in0=kTM[:, 0:nt, :], in1=kTM[:, nt:2 * nt, :], op=ALU.add), reads=[b_kTM], writes=[b_kTM])
                S.op("dve", lambda e: e.scalar_tensor_tensor(out=kTM[:, nt:2 * nt, :], in0=kTM[:, nt:2 * nt, :], scalar=-2.0, in1=kTM[:, 0:nt, :], op0=ALU.mult, op1=ALU.add), reads=[b_kTM], writes=[b_kTM])

                def ldf(m):
                    sl = m % 2
                    S.dma(lambda e: e.dma_start(out=fwt[sl][:], in_=FWc[L][m]), b_fwt[sl], writes=[b_fwt[sl]])
                ldf(0)
                for m in range(nt):
                    sl = m % 2
                    if m + 1 < nt:
                        ldf(m + 1)
                    pss = [nps() for _ in range(2)]
                    hf = 0 if m < max(nt // 2, 1) else 1
                    for ri in range(2):
                        ps, bps = pss[ri]
                        for tau in range(nt):
                            S.op("pe", lambda e, ps=ps, sl=sl, ri=ri, tau=tau, hf=hf: e.matmul(ps[:, 0:512], lhsT=fwt[sl][:, ri, tau, :], rhs=kTM[:, hf * nt + tau, :], start=(tau == 0), stop=(tau == nt - 1)),
                                 reads=[b_fwt[sl], b_kTM], writes=[bps] if tau == 0 else (), joins=() if tau == 0 else [bps])
                    for ri in range(2):
                        p1, b1 = pss[ri]
                        eb, beb = ev[ri * 2 + 1], b_ev[ri * 2 + 1]
                        S.op("act", lambda e, p1=p1, eb=eb: e.copy(out=eb[:], in_=p1[:, 0:512]), reads=[b1], writes=[beb])
                    S.op("pool", lambda e: e.tensor_tensor(out=tabA[:], in0=ev[1][:], in1=rnbc[:], op=ALU.mult), reads=[b_ev[1], b_rn], writes=[b_tA])
                    S.op("pool", lambda e: e.tensor_tensor(out=tabC[:], in0=ev[3][:], in1=rnbc[:], op=ALU.mult), reads=[b_ev[3], b_rn], writes=[b_tC])
                    if m == 0:
                        S.op("dve", lambda e: e.tensor_scalar(out=tabD[:], in0=tabC[:], scalar1=m1v[:, 0:1], scalar2=None, op0=ALU.mult), reads=[b_tC, B_const], writes=[b_tD])
                        S.op("dve", lambda e: e.scalar_tensor_tensor(out=tabD[:], in0=tabA[:], scalar=m0v[:, 0:1], in1=tabD[:], op0=ALU.mult, op1=ALU.add), reads=[b_tA], writes=[b_tD])
                        S.op("dve", lambda e: e.tensor_scalar(out=tabC[:], in0=tabC[:], scalar1=m0v[:, 0:1], scalar2=None, op0=ALU.mult), reads=[b_tD], writes=[b_tC])
                        S.dma(lambda e: e.dma_start(out=TD[L], in_=tabD[:]), b_tD, reads=[b_tD], writes=[B_tab[L]])
                    S.dma(lambda e, m=m: e.dma_start(out=TA[L][m], in_=tabA[:]), b_tA, reads=[b_tA], joins=[B_tab[L]])
                    S.dma(lambda e, m=m: e.dma_start(out=TC[L][m], in_=tabC[:]), b_tC, reads=[b_tC], joins=[B_tab[L]])
                phase_end(loc)

        def phase_P(l):
            with ExitStack() as st:
                xt = [sbt(st, "xt%d" % i, [128, 8, 512], F32) for i in range(2)]
                hT = [sbt(st, "hT%d" % i, [128, 8, 512], BF16) for i in range(2)]
                Pt = sbt(st, "Pt", [128, NCH, 512], BF16)
                kvf = sbt(st, "kvf", [128, 4, 256], F32)
                vbt = sbt(st, "vbt", [128, 4, 128], BF16)
                tm = norm_tmps(st, "P", 512)
                b_xt = [S.buf("xt0"), S.buf("xt1")]
                b_hT = [S.buf("hT0"), S.buf("hT1")]
                b_Pt = [S.buf("Pt%d" % g) for g in range(5)]
                b_kvf = S.buf("kvf")
                b_vbt = S.buf("vbt")
                alltiles = []
                for si, (base, L, ctx) in enumerate(SEQS):
                    for (t0, n) in tiles_of(base, L):
                        alltiles.append((si, t0, n, ctx))
                first = {"P": True, "V": True}

                def ldx(i):
                    si, t0, n, ctx = alltiles[i]
                    sl = i % 2
                    for c0 in range(0, 8, 4):
                        S.dma(lambda e, c0=c0: e.dma_start(out=xt[sl][:, c0:c0 + 4, 0:n], in_=xT[c0:c0 + 4, :, t0:t0 + n].rearrange("c p t -> p c t")), b_xt[sl],
                              reads=[B_xT], writes=[b_xt[sl]] if c0 == 0 else (), joins=() if c0 == 0 else [b_xt[sl]])

                def do_norm(i):
                    si, t0, n, ctx = alltiles[i]
                    sl = i % 2
                    s = 0 if ctx else 1
                    rms_norm(tm, xt[sl], b_xt[sl], n, GP[:, l, s, :], SH[:, l, s, :], hT[sl], b_hT[sl])

                def do_tile(i):
                    si, t0, n, ctx = alltiles[i]
                    sl = i % 2
                    h = hT[sl]
                    bh = b_hT[sl]
                    for ch in range(NCH):
                        g = ch // 5
                        ps, bps = nps()
                        for kc in range(8):
                            S.op("pe", lambda e, ps=ps, ch=ch, kc=kc: e.matmul(ps[:, 0:n], lhsT=Wh["b"][:, kc, ch * 128:(ch + 1) * 128], rhs=h[:, kc, 0:n], start=(kc == 0), stop=(kc == 7)),
                                 reads=[B_W, bh], writes=[bps] if kc == 0 else (), joins=() if kc == 0 else [bps])
                        wr = dict(writes=[b_Pt[g]]) if ch % 5 == 0 else dict(joins=[b_Pt[g]])
                        if ch in GATE_CH:
                            S.op("act", lambda e, ps=ps, ch=ch: e.activation(out=Pt[:, ch, 0:n], in_=ps[:, 0:n], func=AF.Silu), reads=[bps], **wr)
                        elif ch % 2 == 0:
                            S.op("dve", lambda e, ps=ps, ch=ch: e.tensor_copy(out=Pt[:, ch, 0:n], in_=ps[:, 0:n]), reads=[bps], **wr)
                        else:
                            S.op("act", lambda e, ps=ps, ch=ch: e.copy(out=Pt[:, ch, 0:n], in_=ps[:, 0:n]), reads=[bps], **wr)
                        if ch % 5 == 4:
                            c0 = ch - 4
                            S.dma(lambda e, c0=c0: e.dma_start(out=Pscr[c0:c0 + 5, :, t0:t0 + n].rearrange("c p t -> p c t"), in_=Pt[:, c0:c0 + 5, 0:n]), b_Pt[g],
                                  reads=[b_Pt[g]], writes=[B_P] if first["P"] else (), joins=() if first["P"] else [B_P])
                            first["P"] = False
                    nbk = n // 128
                    for bk in range(nbk):
                        ps, bps = nps()
                        for kc in range(8):
                            S.op("pe", lambda e, ps=ps, bk=bk, kc=kc: e.matmul(ps[:, 0:256], lhsT=h[:, kc, bk * 128:(bk + 1) * 128], rhs=Wh["kv"][:, kc, :], start=(kc == 0), stop=(kc == 7)),
                                 reads=[B_W, bh], writes=[bps] if kc == 0 else (), joins=() if kc == 0 else [bps])
                        wv = dict(writes=[b_vbt]) if bk == 0 else dict(joins=[b_vbt])
                        if ctx:
                            S.op("act", lambda e, ps=ps, bk=bk: e.copy(out=vbt[:, bk, :], in_=ps[:, 128:256]), reads=[bps], **wv)
                        else:
                            wk = dict(writes=[b_kvf]) if bk == 0 else dict(joins=[b_kvf])
                            S.op("dve", lambda e, ps=ps, bk=bk: e.tensor_copy(out=kvf[:, bk, :], in_=ps[:, 0:256]), reads=[bps], **wk)
                            S.op("pool", lambda e, bk=bk: e.tensor_copy(out=vbt[:, bk, :], in_=kvf[:, bk, 128:256]), reads=[b_kvf], **wv)
                    S.dma(lambda e: e.dma_start(out=Vscr[t0:t0 + n, :].rearrange("(b p) d -> p b d", p=128), in_=vbt[:, 0:nbk, :]), b_vbt,
                          reads=[b_vbt], writes=[B_V] if first["V"] else (), joins=() if first["V"] else [B_V])
                    first["V"] = False
                    if not ctx:
                        j = si - 1
                        S.dma(lambda e: e.dma_start(out=nk_out[j, l].rearrange("(b p) d -> p b d", p=128), in_=kvf[:, 0:nbk, 0:128]), b_kvf, reads=[b_kvf], joins=[B_out])
                        S.dma(lambda e: e.dma_start(out=nv_out[j, l].rearrange("(b p) d -> p b d", p=128), in_=kvf[:, 0:nbk, 128:256]), b_kvf, reads=[b_kvf], joins=[B_out])

                ldx(0)
                do_norm(0)
                for i in range(len(alltiles)):
                    if i + 1 < len(alltiles):
                        ldx(i + 1)
                        do_norm(i + 1)
                    do_tile(i)
                    if _os_mod.environ.get('KBAR_P'):
                        S.barrier()
                phase_end(b_xt + b_hT + b_Pt + [b_kvf, b_vbt, tm["b_sq"], tm["b_rstd"], tm["b_tmpn"]])

        class _View3:
            def __init__(self, t, n):
                self.t = t
                self.n = n

            def __getitem__(self, key):
                if not isinstance(key, tuple):
                    key = (key,)
                key = list(key) + [slice(None)] * (3 - len(key))
                if isinstance(key[2], slice) and key[2] == slice(None):
                    key[2] = slice(0, self.n)
                return self.t[tuple(key)]

        def phase_H(l, sis):
            base, L, ctx = SEQS[sis[0]]
            nt = L // 128
            n = min(512, L)
            with ExitStack() as st:
                XT = sbt(st, "XT", [128, nt, 256], BF16)
                Y = sbt(st, "Y", [128, 2 * nt, 256], BF16)
                cw = sbt(st, "cw", [128, 6, 3], F32)
                dbc = sbt(st, "dbc", [128, 2, 256], F32)
                b_XT = S.buf("XT")
                b_Y = S.buf("Y")
                b_cw = S.buf("cw")
                loc = [b_XT, b_Y, b_cw]
                for k3 in range(3):
                    S.dma(lambda e, k3=k3: e.dma_start(out=cw[:, :, k3:k3 + 1], in_=hy_cw[l, k3:k3 + 1, :].rearrange("k (c p) -> p c k", p=128), allow_slow_non_contiguous=True), b_cw, joins=[b_cw])
                for o in range(2):
                    S.dma(lambda e, o=o: e.dma_start(out=dbc[:, o, :], in_=hy_d[l, o:o + 1, :].partition_broadcast(128)), b_cw, joins=[b_cw])
                for si_ in sis:
                    base = SEQS[si_][0]
                    with ExitStack() as st1:
                        hr = [sbt(st1, "hr%d" % i, [128, 8, n + 2], BF16) for i in range(2)]
                        hc = sbt(st1, "hc", [128, 8, n], BF16)
                        hacc = sbt(st1, "hacc", [128, 6, n], F32)
                        tmo = sbt(st1, "tmo", [128, n // 128, 768], BF16)
                        b_hr = [S.buf("hr0"), S.buf("hr1")]
                        b_hc = S.buf("hc")
                        b_hacc = S.buf("hacc")
                        b_tmo = S.buf("tmo")
                        tl = tiles_of(base, L)
                        firstHX = [True]

                        def ldh(i):
                            t0, nn = tl[i]
                            sl = i % 2
                            lo = max(t0 - 1, base)
                            hi = min(t0 + nn + 1, base + L)
                            S.dma(lambda e: e.dma_start(out=hr[sl][:, 0:4, lo - (t0 - 1):hi - (t0 - 1)], in_=Pscr[CHY:CHY + 4, :, lo:hi].rearrange("c p t -> p c t")), b_hr[sl], reads=[B_P], writes=[b_hr[sl]])
                            if lo == t0:
                                S.op("pool", lambda e: e.memset(hr[sl][:, :, 0:1], 0.0), joins=[b_hr[sl]])
                            if hi == t0 + nn:
                                S.op("pool", lambda e: e.memset(hr[sl][:, :, nn + 1:nn + 2], 0.0), joins=[b_hr[sl]])
                            S.dma(lambda e: e.dma_start(out=hr[sl][:, 4:8, lo - (t0 - 1):hi - (t0 - 1)], in_=Pscr[CHY + 4:CHY + 8, :, lo:hi].rearrange("c p t -> p c t")), b_hr[sl], reads=[B_P], joins=[b_hr[sl]])
                        ldh(0)
                        for i, (t0, nn) in enumerate(tl):
                            sl = i % 2
                            if i + 1 < len(tl):
                                ldh(i + 1)
                            for c in range(6):
                                eng = "dve"
                                S.op(eng, lambda e, c=c, sl=sl: e.tensor_scalar(out=hacc[:, c, :], in0=hr[sl][:, c, 1:nn + 1], scalar1=cw[:, c, 1:2], scalar2=None, op0=ALU.mult), reads=[b_hr[sl], b_cw], writes=[b_hacc] if c == 0 else (), joins=() if c == 0 else [b_hacc])
                                S.op(eng, lambda e, c=c, sl=sl: e.scalar_tensor_tensor(out=hacc[:, c, :], in0=hr[sl][:, c, 0:nn], scalar=cw[:, c, 0:1], in1=hacc[:, c, :], op0=ALU.mult, op1=ALU.add), reads=[b_hr[sl], b_hacc], joins=[b_hacc])
                                S.op(eng, lambda e, c=c, sl=sl: e.scalar_tensor_tensor(out=hc[:, c, :], in0=hr[sl][:, c, 2:nn + 2], scalar=cw[:, c, 2:3], in1=hacc[:, c, :], op0=ALU.mult, op1=ALU.add), reads=[b_hr[sl], b_hacc], writes=[b_hc] if c == 0 else (), joins=() if c == 0 else [b_hc])
                            S.op("pool", lambda e, sl=sl: e.tensor_copy(out=hc[:, 6:8, :], in_=hr[sl][:, 6:8, 1:nn + 1]), reads=[b_hr[sl]], joins=[b_hc])
                            for bk in range(nn // 128):
                                pa, ba = nps()
                                pb, bb = nps()
                                for c in range(8):
                                    ps_, bp_ = (pa, ba) if c < 4 else (pb, bb)
                                    cc = c % 4
                                    S.op("pe", lambda e, ps_=ps_, c=c, cc=cc, bk=bk: e.matmul(ps_[:, cc * 128:(cc + 1) * 128], lhsT=hc[:, c, bk * 128:(bk + 1) * 128], rhs=ident[:], start=True, stop=True),
                                         reads=[b_hc, B_const], writes=[bp_] if cc == 0 else (), joins=() if cc == 0 else [bp_])
                                tch = (t0 - base) // 128 + bk
                                S.op("act", lambda e, pa=pa, tch=tch: e.copy(out=XT[:, tch, :], in_=pa[:, 0:256]), reads=[ba], joins=[b_XT])
                                S.op("act", lambda e, pa=pa, bk=bk: e.copy(out=tmo[:, bk, 0:256], in_=pa[:, 256:512]), reads=[ba], writes=[b_tmo] if bk == 0 else (), joins=() if bk == 0 else [b_tmo])
                                S.op("dve", lambda e, pb=pb, bk=bk: e.tensor_copy(out=tmo[:, bk, 256:768], in_=pb[:, 0:512]), reads=[bb], joins=[b_tmo])
                            S.dma(lambda e, t0=t0, nn=nn: e.dma_start(out=HX[t0:t0 + nn, :].rearrange("(b p) d -> p b d", p=128), in_=tmo[:, 0:nn // 128, :]), b_tmo,
                                  reads=[b_tmo], writes=[B_HX] if firstHX[0] else (), joins=() if firstHX[0] else [B_HX])
                            firstHX[0] = False
                        S.barrier()
                        S.emit()
                        S.drop(b_hr + [b_hc, b_hacc, b_tmo])
                    with ExitStack() as st2:
                        fwt = [sbt(st2, "hfw%d" % i, [128, 2, nt, 128], BF16) for i in range(2)]
                        ivt = [fwt[i][:].rearrange("p r t j -> p (r t) j") for i in range(2)]
                        ta = [sbt(st2, "ta%d" % i, [128, 256], F32) for i in range(2)]
                        tc_ = [sbt(st2, "tc%d" % i, [128, 256], F32) for i in range(2)]
                        td = sbt(st2, "td", [128, 256], F32)
                        er = sbt(st2, "er", [128, 256], F32)
                        ei = sbt(st2, "ei", [128, 256], F32)
                        t1 = sbt(st2, "t1", [128, 256], F32)
                        t2 = sbt(st2, "t2", [128, 256], F32)
                        t3 = sbt(st2, "t3", [128, 256], F32)
                        t4 = sbt(st2, "t4", [128, 256], F32)
                        hx = [sbt(st2, "hx%d" % i, [128, 768], BF16) for i in range(2)]
                        u1 = sbt(st2, "u1", [128, 256], F32)
                        zt = sbt(st2, "zt", [128, 256], BF16)
                        zo = sbt(st2, "zo", [128, 2, 512], BF16)
                        b_fw = [S.buf("hfw0"), S.buf("hfw1")]
                        b_iv = b_fw
                        b_ta = [S.buf("ta0"), S.buf("ta1")]
                        b_td = S.buf("td")
                        b_e = S.buf("e")
                        b_t12 = S.buf("t12")
                        b_t34 = S.buf("t34")
                        b_hx = [S.buf("hx0"), S.buf("hx1")]
                        b_u1 = S.buf("u1")
                        b_zt = S.buf("zt")
                        b_zo = S.buf("zo")
                        loc2 = b_fw + b_ta + b_hx + [b_td, b_e, b_t12, b_t34, b_u1, b_zt, b_zo]
                        firstZ = [True]
                        for o in range(2):
                            S.dma(lambda e, o=o: e.dma_start(out=td[:], in_=TD[L][:, o * 256:(o + 1) * 256]), b_td, reads=[B_tab[L]], writes=[b_td])

                            def ldfw(m, o=o):
                                sl = m % 2
                                S.dma(lambda e: e.dma_start(out=fwt[sl][:], in_=FWc[L][m]), b_fw[sl], writes=[b_fw[sl]])
                                S.dma(lambda e: e.dma_start(out=ta[sl][:], in_=TA[L][m, :, o * 256:(o + 1) * 256]), b_ta[sl], reads=[B_tab[L]], writes=[b_ta[sl]])
                                S.dma(lambda e: e.dma_start(out=tc_[sl][:], in_=TC[L][m, :, o * 256:(o + 1) * 256]), b_ta[sl], reads=[B_tab[L]], joins=[b_ta[sl]])
                            ldfw(0)
                            for m in range(nt):
                                sl = m % 2
                                if m + 1 < nt:
                                    ldfw(m + 1)
                                (pr, bpr), (pi_, bpi) = nps(), nps()
                                for ri, (ps, bps) in enumerate(((pr, bpr), (pi_, bpi))):
                                    for tau in range(nt):
                                        S.op("pe", lambda e, ps=ps, sl=sl, ri=ri, tau=tau: e.matmul(ps[:, 0:256], lhsT=fwt[sl][:, ri, tau, :], rhs=XT[:, tau, :], start=(tau == 0), stop=(tau == nt - 1)),
                                             reads=[b_fw[sl], b_XT], writes=[bps] if tau == 0 else (), joins=() if tau == 0 else [bps])
                                S.op("act", lambda e, pr=pr: e.copy(out=er[:], in_=pr[:, 0:256]), reads=[bpr], writes=[b_e])
                                S.op("act", lambda e, pi_=pi_: e.copy(out=ei[:], in_=pi_[:, 0:256]), reads=[bpi], joins=[b_e])
                                dt_ = td if m == 0 else ta[sl]
                                rd = [b_td] if m == 0 else []
                                S.op("dve", lambda e, sl=sl: e.tensor_tensor(out=t1[:], in0=er[:], in1=ta[sl][:], op=ALU.mult), reads=[b_e, b_ta[sl]], writes=[b_t12])
                                S.op("dve", lambda e, sl=sl: e.tensor_tensor(out=t2[:], in0=ei[:], in1=tc_[sl][:], op=ALU.mult), reads=[b_e, b_ta[sl]], joins=[b_t12])
                                S.op("dve", lambda e, m=m: e.tensor_tensor(out=Y[:, m, :], in0=t1[:], in1=t2[:], op=ALU.subtract), reads=[b_t12], joins=[b_Y])
                                S.op("pool", lambda e, sl=sl: e.tensor_tensor(out=t3[:], in0=er[:], in1=tc_[sl][:], op=ALU.mult), reads=[b_e, b_ta[sl]], writes=[b_t34])
                                S.op("pool", lambda e, dt_=dt_: e.tensor_tensor(out=t4[:], in0=ei[:], in1=dt_[:], op=ALU.mult), reads=[b_e, b_ta[sl]] + rd, joins=[b_t34])
                                S.op("pool", lambda e, m=m: e.tensor_tensor(out=Y[:, nt + m, :], in0=t3[:], in1=t4[:], op=ALU.add), reads=[b_t34], joins=[b_Y])

                            def ldiv(tau, o=o):
                                sl = tau % 2
                                S.dma(lambda e: e.dma_start(out=ivt[sl], in_=IVc[L][tau]), b_iv[sl], writes=[b_iv[sl]])
                                S.dma(lambda e: e.dma_start(out=hx[sl][:], in_=HX[base + tau * 128:base + (tau + 1) * 128, :]), b_hx[sl], reads=[B_HX], writes=[b_hx[sl]])
                            ldiv(0)
                            for tau in range(nt):
                                sl = tau % 2
                                if tau + 1 < nt:
                                    ldiv(tau + 1)
                                py, bpy = nps()
                                for r in range(2 * nt):
                                    S.op("pe", lambda e, py=py, sl=sl, r=r: e.matmul(py[:, 0:256], lhsT=ivt[sl][:, r, :], rhs=Y[:, r, :], start=(r == 0), stop=(r == 2 * nt - 1)),
                                         reads=[b_iv[sl], b_Y], writes=[bpy] if r == 0 else (), joins=() if r == 0 else [bpy])
                                S.op("dve", lambda e, tau=tau, o=o: e.tensor_tensor(out=u1[:], in0=XT[:, tau, :], in1=dbc[:, o, :], op=ALU.mult), reads=[b_XT, b_cw], writes=[b_u1])
                                S.op("dve", lambda e, py=py: e.tensor_tensor(out=u1[:], in0=u1[:], in1=py[:, 0:256], op=ALU.add), reads=[bpy, b_u1], writes=[b_u1])
                                if o == 0:
                                    S.op("dve", lambda e, tau=tau, sl=sl: e.tensor_tensor(out=XT[:, tau, :], in0=u1[:], in1=hx[sl][:, 0:256], op=ALU.mult), reads=[b_u1, b_hx[sl]], joins=[b_XT])
                                else:
                                    S.op("dve", lambda e, sl=sl: e.tensor_tensor(out=u1[:], in0=u1[:], in1=hx[sl][:, 256:512], op=ALU.mult), reads=[b_u1, b_hx[sl]], writes=[b_u1])
                                    S.op("pool", lambda e, sl=sl: e.tensor_tensor(out=zt[:], in0=u1[:], in1=hx[sl][:, 512:768], op=ALU.mult), reads=[b_u1, b_hx[sl]], writes=[b_zt])
                                    pz, bpz = nps()
                                    for cc in range(2):
                                        S.op("pe", lambda e, pz=pz, cc=cc: e.matmul(pz[:, cc * 128:(cc + 1) * 128], lhsT=zt[:, cc * 128:(cc + 1) * 128], rhs=ident[:], start=True, stop=True),
                                             reads=[b_zt, B_const], writes=[bpz] if cc == 0 else (), joins=() if cc == 0 else [bpz])
                                    q4 = tau % 4
                                    S.op("act", lambda e, pz=pz, q4=q4: e.copy(out=zo[:, :, q4 * 128:(q4 + 1) * 128], in_=pz[:, 0:256].rearrange("p (c t) -> p c t", c=2)), reads=[bpz], writes=[b_zo] if q4 == 0 else (), joins=() if q4 == 0 else [b_zo])
                                    if q4 == 3 or tau == nt - 1:
                                        w = (q4 + 1) * 128
                                        t0 = base + (tau - q4) * 128
                                        S.dma(lambda e, w=w, t0=t0: e.dma_start(out=Zscr[:, :, t0:t0 + w].rearrange("c p t -> p c t"), in_=zo[:, :, 0:w]), b_zo,
                                              reads=[b_zo], writes=[B_Z] if firstZ[0] else (), joins=() if firstZ[0] else [B_Z])
                                        firstZ[0] = False
                        S.barrier()
                        S.emit()
                        S.drop(loc2)
                phase_end(loc)

        def phase_C(l, sis):
            base, L, ctx = SEQS[sis[0]]
            n = min(512, L)
            nbq = n // 128
            s = 0 if ctx else 1
            W = n + 256 if ctx else L
            nkb = W // 128
            with ExitStack() as st:
                xt_2 = [sbt(st, "cxt", [128, 8, n], F32) for _ in range(2)]
                qraw_2 = [sbt(st, "qraw", [128, 4, n], BF16) for _ in range(2)]
                qrot_2 = [sbt(st, "qrot", [128, 4, n], BF16) for _ in range(2)]
                kraw_2 = [sbt(st, "kraw", [128, W], BF16) for _ in range(2)]
                krot_2 = [sbt(st, "krot", [128, W], BF16) for _ in range(2)]
                rc_2 = [sbt(st, "rc", [128, W], F32) for _ in range(2)]
                rs_2 = [sbt(st, "rs", [128, W], F32) for _ in range(2)]
                rt1 = sbt(st, "rt1", [128, 512], F32)
                rt2 = sbt(st, "rt2", [128, 512], F32)
                vt_2 = [sbt(st, "vt", [128, nkb, 128], BF16) for _ in range(2)]
                ga_2 = [sbt(st, "ga", [128, 4, n], BF16) for _ in range(2)]
                scb_2 = [sbt(st, "scb", [128, 6, n + 2], BF16) for _ in range(2)]
                gcz_2 = [sbt(st, "gcz", [128, 4, n], BF16) for _ in range(2)]
                mixT = sbt(st, "mixT", [128, 8, n], BF16)
                attf = sbt(st, "attf", [128, 4, n], F32)
                PT = [sbt(st, "PT%d" % i, [128, 512], BF16) for i in range(10)]
                rden = sbt(st, "rden", [64, 512], F32)
                sinkt = sbt(st, "sinkt", [64, 2, 512], F32)
                esk = sbt(st, "esk", [64, 8], F32)
                scw = sbt(st, "scw", [128, 2, 3], F32)
                su = sbt(st, "su", [128, 2, n + 2], F32)
                sacc = sbt(st, "sacc", [128, 2, n], F32)
                b_xt_2, b_q_2, b_k_2, b_rope_2, b_vt_2, b_ga_2, b_scb_2, b_gcz_2 = [[S.buf(x + "0"), S.buf(x + "1")] for x in ("cxt", "q", "k", "rope", "vt", "ga", "scb", "gcz")]
                b_rt, b_mix, b_att = [S.buf(x) for x in ("rt", "mix", "att")]
                b_PT = [S.buf("PT%d" % i) for i in range(10)]
                b_rden, b_sink, b_scw, b_su, b_sacc, b_xn = [S.buf(x) for x in ("rden", "sink", "scw", "su", "sacc", "xn")]
                b_rt2 = S.buf("rt2")
                loc = [b_rt2, b_rt, b_mix, b_att, b_rden, b_sink, b_scw, b_su, b_sacc, b_xn] + b_PT + b_xt_2 + b_q_2 + b_k_2 + b_rope_2 + b_vt_2 + b_ga_2 + b_scb_2 + b_gcz_2
                if ctx:
                    cks = sbt(st, "cks", [128, 2, 128], F32)
                    ckb = sbt(st, "ckb", [128, 2, 128], BF16)
                    ckT = sbt(st, "ckT", [128, 256], BF16)
                    cvs = sbt(st, "cvs", [128, 2, 128], F32)
                    cvb = sbt(st, "cvb", [128, 2, 128], BF16)
                    b_ck = S.buf("ck")
                    loc.append(b_ck)
                    S.dma(lambda e: e.dma_start(out=cks[:], in_=ck_in[l].rearrange("(b p) d -> p b d", p=128)), b_ck, writes=[b_ck])
                    S.dma(lambda e: e.dma_start(out=cvs[:], in_=cv_in[l].rearrange("(b p) d -> p b d", p=128)), b_ck, joins=[b_ck])
                    S.op("dve", lambda e: e.tensor_copy(out=ckb[:], in_=cks[:]), reads=[b_ck], joins=[b_ck])
                    S.op("dve", lambda e: e.tensor_copy(out=cvb[:], in_=cvs[:]), reads=[b_ck], joins=[b_ck])
                    ps, bps = nps()
                    for b in range(2):
                        S.op("pe", lambda e, ps=ps, b=b: e.matmul(ps[:, b * 128:(b + 1) * 128], lhsT=ckb[:, b, :], rhs=ident[:], start=True, stop=True), reads=[b_ck, B_const], writes=[bps] if b == 0 else (), joins=() if b == 0 else [bps])
                    S.op("act", lambda e, ps=ps: e.copy(out=ckT[:], in_=ps[:, 0:256]), reads=[bps], joins=[b_ck])
                S.dma(lambda e: e.dma_start(out=esk[:], in_=sink_in[l:l + 1, :].partition_broadcast(64)), b_sink, writes=[b_sink])
                S.op("act", lambda e: e.activation(out=esk[:], in_=esk[:], func=AF.Exp), reads=[b_sink], writes=[b_sink])
                for g in range(8):
                    h, j = g // 4, g % 4
                    S.op("dve", lambda e, g=g, h=h, j=j: e.tensor_scalar(out=sinkt[:, h, j * 128:(j + 1) * 128], in0=zeros[0:64, 0:128], scalar1=esk[:, g:g + 1], scalar2=None, op0=ALU.add), reads=[b_sink, B_const], joins=[b_sink])
                for k3 in range(3):
                    S.dma(lambda e, k3=k3: e.dma_start(out=scw[:, :, k3:k3 + 1], in_=sc_cw[l, k3:k3 + 1, :].rearrange("k (c p) -> p c k", p=128), allow_slow_non_contiguous=True), b_scw, joins=[b_scw])

                def loads(ti, base, t0, nn):
                    sl = ti % 2
                    xt, qraw, qrot, kraw, krot, rc, rs, vt, ga, scb, gcz = [x[sl] for x in (xt_2, qraw_2, qrot_2, kraw_2, krot_2, rc_2, rs_2, vt_2, ga_2, scb_2, gcz_2)]
                    b_xt, b_q, b_k, b_rope, b_vt, b_ga, b_scb, b_gcz = [x[sl] for x in (b_xt_2, b_q_2, b_k_2, b_rope_2, b_vt_2, b_ga_2, b_scb_2, b_gcz_2)]
                    lo = max(t0 - 1, base)
                    hi = min(t0 + nn + 1, base + L)
                    if ctx:
                        k0 = t0 - 128
                        klo = max(k0, base)
                        khi = min(t0 + nn + 128, base + L)
                    else:
                        k0, klo, khi = base, base, base + L
                    for c0 in range(0, 8, 4):
                        S.dma(lambda e, c0=c0: e.dma_start(out=xt[:, c0:c0 + 4, :], in_=xT[c0:c0 + 4, :, t0:t0 + nn].rearrange("c p t -> p c t")), b_xt, reads=[B_xT], writes=[b_xt] if c0 == 0 else (), joins=() if c0 == 0 else [b_xt])
                    S.dma(lambda e: e.dma_start(out=qraw[:], in_=Pscr[CQ:CQ + 4, :, t0:t0 + nn].rearrange("c p t -> p c t")), b_q, reads=[B_P], writes=[b_q])
                    S.dma(lambda e: e.dma_start(out=ga[:], in_=Pscr[CGA:CGA + 4, :, t0:t0 + nn].rearrange("c p t -> p c t")), b_ga, reads=[B_P], writes=[b_ga])
                    S.dma(lambda e: e.dma_start(out=gcz[:, 0:2, :], in_=Pscr[CGC:CGC + 2, :, t0:t0 + nn].rearrange("c p t -> p c t")), b_gcz, reads=[B_P], writes=[b_gcz])
                    S.dma(lambda e: e.dma_start(out=gcz[:, 2:4, :], in_=Zscr[:, :, t0:t0 + nn].rearrange("c p t -> p c t")), b_gcz, reads=[B_Z], joins=[b_gcz])
                    lo = max(t0 - 1, base)
                    hi = min(t0 + nn + 1, base + L)
                    S.dma(lambda e: e.dma_start(out=scb[:, :, lo - (t0 - 1):hi - (t0 - 1)], in_=Pscr[CSB:CSB + 6, :, lo:hi].rearrange("c p t -> p c t")), b_scb, reads=[B_P], writes=[b_scb])
                    if lo == t0:
                        S.op("pool", lambda e: e.memset(scb[:, :, 0:1], 0.0), joins=[b_scb])
                    if hi == t0 + nn:
                        S.op("pool", lambda e: e.memset(scb[:, :, nn + 1:nn + 2], 0.0), joins=[b_scb])
                    if ctx:
                        k0 = t0 - 128
                        klo = max(k0, base)
                        khi = min(t0 + nn + 128, base + L)
                    else:
                        k0, klo, khi = base, base, base + L
                    S.dma(lambda e: e.dma_start(out=kraw[:, klo - k0:khi - k0], in_=Pscr[CK, :, klo:khi]), b_k, reads=[B_P], writes=[b_k])
                    S.dma(lambda e: e.dma_start(out=vt[:, (klo - k0) // 128:(khi - k0) // 128, :], in_=Vscr[klo:khi, :].rearrange("(b p) d -> p b d", p=128)), b_vt, reads=[B_V], writes=[b_vt])

                    if ctx:
                        S.dma(lambda e: e.dma_start(out=rc[:, klo - k0:khi - k0], in_=ropeC[:, klo - base:khi - base]), b_rope, writes=[b_rope])
                        S.dma(lambda e: e.dma_start(out=rs[:, klo - k0:khi - k0], in_=ropeS[:, klo - base:khi - base]), b_rope, joins=[b_rope])

                def do_rope(ti, base, t0, nn):
                    sl = ti % 2
                    xt, qraw, qrot, kraw, krot, rc, rs, vt, ga, scb, gcz = [x[sl] for x in (xt_2, qraw_2, qrot_2, kraw_2, krot_2, rc_2, rs_2, vt_2, ga_2, scb_2, gcz_2)]
                    b_xt, b_q, b_k, b_rope, b_vt, b_ga, b_scb, b_gcz = [x[sl] for x in (b_xt_2, b_q_2, b_k_2, b_rope_2, b_vt_2, b_ga_2, b_scb_2, b_gcz_2)]
                    lo = max(t0 - 1, base)
                    hi = min(t0 + nn + 1, base + L)
                    if ctx:
                        k0 = t0 - 128
                        klo = max(k0, base)
                        khi = min(t0 + nn + 128, base + L)
                    else:
                        k0, klo, khi = base, base, base + L
                    if ctx:
                        def rope(src_ap, dst_ap, c_ap, s_ap, wdt, rb, wb, first):
                            ps, bps = nps()
                            S.op("pe", lambda e: e.matmul(ps[:, 0:wdt], lhsT=pmat[:], rhs=src_ap, start=True, stop=True), reads=[rb, B_const], writes=[bps])
                            S.op("dve", lambda e: e.tensor_tensor(out=rt1[:, 0:wdt], in0=ps[:, 0:wdt], in1=s_ap, op=ALU.mult), reads=[bps, b_rope], writes=[b_rt])
                            S.op("pool", lambda e: e.tensor_tensor(out=rt2[:, 0:wdt], in0=src_ap, in1=c_ap, op=ALU.mult), reads=[rb, b_rope], writes=[b_rt2])
                            S.op("dve", lambda e: e.tensor_tensor(out=dst_ap, in0=rt1[:, 0:wdt], in1=rt2[:, 0:wdt], op=ALU.add), reads=[b_rt, b_rt2], joins=[wb])
                        for c in range(4):
                            rope(qraw[:, c, :], qrot[:, c, :], rc[:, 128:128 + nn], rs[:, 128:128 + nn], nn, b_q, b_q, False)
                        a0, a1 = klo - k0, khi - k0
                        for off in range(a0, a1, 512):
                            wdt = min(512, a1 - off)
                            rope(kraw[:, off:off + wdt], krot[:, off:off + wdt], rc[:, off:off + wdt], rs[:, off:off + wdt], wdt, b_k, b_k, False)

                def compute(ti, base, t0, nn, nxt=None):
                    sl = ti % 2
                    xt, qraw, qrot, kraw, krot, rc, rs, vt, ga, scb, gcz = [x[sl] for x in (xt_2, qraw_2, qrot_2, kraw_2, krot_2, rc_2, rs_2, vt_2, ga_2, scb_2, gcz_2)]
                    b_xt, b_q, b_k, b_rope, b_vt, b_ga, b_scb, b_gcz = [x[sl] for x in (b_xt_2, b_q_2, b_k_2, b_rope_2, b_vt_2, b_ga_2, b_scb_2, b_gcz_2)]
                    lo = max(t0 - 1, base)
                    hi = min(t0 + nn + 1, base + L)
                    if ctx:
                        k0 = t0 - 128
                        klo = max(k0, base)
                        khi = min(t0 + nn + 128, base + L)
                    else:
                        k0, klo, khi = base, base, base + L
                    S.op("dve", lambda e: e.tensor_tensor(out=su[:], in0=scb[:, 2:4, :], in1=scb[:, 4:6, :], op=ALU.mult), reads=[b_scb], writes=[b_su])
                    for c in range(2):
                        eng = "dve"
                        S.op(eng, lambda e, c=c: e.tensor_scalar(out=sacc[:, c, :], in0=su[:, c, 1:nn + 1], scalar1=scw[:, c, 1:2], scalar2=None, op0=ALU.mult), reads=[b_su, b_scw], writes=[b_sacc] if c == 0 else (), joins=() if c == 0 else [b_sacc])
                        S.op(eng, lambda e, c=c: e.scalar_tensor_tensor(out=sacc[:, c, :], in0=su[:, c, 0:nn], scalar=scw[:, c, 0:1], in1=sacc[:, c, :], op0=ALU.mult, op1=ALU.add), reads=[b_su, b_sacc], joins=[b_sacc])
                        S.op(eng, lambda e, c=c: e.scalar_tensor_tensor(out=sacc[:, c, :], in0=su[:, c, 2:nn + 2], scalar=scw[:, c, 2:3], in1=sacc[:, c, :], op0=ALU.mult, op1=ALU.add), reads=[b_su, b_sacc], joins=[b_sacc])
                    S.op("dve", lambda e: e.tensor_tensor(out=sacc[:], in0=sacc[:], in1=scb[:, 0:2, 1:nn + 1], op=ALU.mult), reads=[b_scb, b_sacc], writes=[b_sacc])
                    S.op("dve", lambda e: e.tensor_tensor(out=mixT[:, 6:8, :], in0=sacc[:], in1=gcz[:, 0:2, :], op=ALU.mult), reads=[b_sacc, b_gcz], writes=[b_mix])
                    S.op("pool", lambda e: e.tensor_copy(out=mixT[:, 4:6, :], in_=gcz[:, 2:4, :]), reads=[b_gcz], joins=[b_mix])
                    for qb in range(nn // 128):
                        nblk = (t0 - base) // 128 + qb
                        for h in range(2):
                            hp = slice(64 * h, 64 * h + 64)
                            blocks = []
                            if ctx:
                                for d, msk in ((-1, mlo), (0, None), (1, mhi)):
                                    kb_ = nblk + d
                                    if kb_ < 0 or kb_ >= L // 128:
                                        continue
                                    w0 = (kb_ * 128 + base) - k0
                                    blocks.append((krot[hp, w0:w0 + 128], qrot, msk, vt[:, w0 // 128, hp]))
                                for cb_ in range(2):
                                    blocks.append((ckT[hp, cb_ * 128:(cb_ + 1) * 128], qraw, None, cvb[:, cb_, hp]))
                            else:
                                for kb_ in range(L // 128):
                                    blocks.append((kraw[hp, kb_ * 128:(kb_ + 1) * 128], qraw, None, vt[:, kb_, hp]))
                            pts = []
                            for bi, (kap, qt, msk, vap) in enumerate(blocks):
                                ps, bps = nps()
                                for j in range(4):
                                    S.op("pe", lambda e, ps=ps, kap=kap, qt=qt, j=j, qb=qb, hp=hp: e.matmul(ps[:, j * 128:(j + 1) * 128], lhsT=kap, rhs=qt[hp, j, qb * 128:(qb + 1) * 128], start=True, stop=True),
                                         reads=[b_k, b_q] + ([b_ck] if ctx else []), writes=[bps] if j == 0 else (), joins=() if j == 0 else [bps])
                                pt, bpt = PT[((qb * 2 + h) % 2) * 5 + bi], b_PT[((qb * 2 + h) % 2) * 5 + bi]
                                S.op("act", lambda e, ps=ps, pt=pt: e.activation(out=pt[:], in_=ps[:], func=AF.Exp, scale=0.125), reads=[bps], writes=[bpt])
                                if msk is not None:
                                    for j in range(4):
                                        eng = "dve" if j % 2 == 0 else "pool"
                                        S.op(eng, lambda e, pt=pt, msk=msk, j=j: e.tensor_tensor(out=pt[:, j * 128:(j + 1) * 128], in0=pt[:, j * 128:(j + 1) * 128], in1=msk[:], op=ALU.mult), reads=[B_const, bpt], joins=[bpt])
                                pts.append((pt, bpt, vap))
                            po, bpo = nps()
                            pd, bpd = nps()
                            for bi, (pt, bpt, vap) in enumerate(pts):
                                S.op("pe", lambda e, po=po, pt=pt, vap=vap, bi=bi, last=(bi == len(pts) - 1): e.matmul(po[0:64, :], lhsT=vap, rhs=pt[:], start=(bi == 0), stop=last),
                                     reads=[bpt, b_vt] + ([b_ck] if ctx else []), writes=[bpo] if bi == 0 else (), joins=() if bi == 0 else [bpo])
                            for bi, (pt, bpt, vap) in enumerate(pts):
                                S.op("pe", lambda e, pd=pd, pt=pt, bi=bi, last=(bi == len(pts) - 1): e.matmul(pd[0:64, :], lhsT=ones[:, 0:64], rhs=pt[:], start=(bi == 0), stop=last),
                                     reads=[bpt, B_const], writes=[bpd] if bi == 0 else (), joins=() if bi == 0 else [bpd])
                            S.op("dve", lambda e, pd=pd, h=h: e.tensor_tensor(out=rden[:], in0=pd[0:64, :], in1=sinkt[:, h, :], op=ALU.add), reads=[bpd, b_sink], writes=[b_rden])
                            S.op("dve", lambda e: e.reciprocal(out=rden[:], in_=rden[:]), reads=[b_rden], writes=[b_rden])
                            S.op("dve", lambda e, po=po, hp=hp, qb=qb: e.tensor_tensor(out=attf[hp, :, qb * 128:(qb + 1) * 128], in0=po[0:64, :].rearrange("p (j t) -> p j t", j=4), in1=rden[:].rearrange("p (j t) -> p j t", j=4), op=ALU.mult),
                                 reads=[bpo, b_rden], joins=[b_att])
                    S.op("pool", lambda e: e.tensor_tensor(out=mixT[:, 0:4, :], in0=attf[:], in1=ga[:], op=ALU.mult), reads=[b_att, b_ga], joins=[b_mix])
                    _kbr = _os_mod.environ.get('KBR')
                    if _kbr:
                        if 'attn' not in _kbr:
                            S.op("pool", lambda e: e.memset(mixT[:, 0:4, :], 0.0), writes=[b_mix])
                        if 'hy' not in _kbr:
                            S.op("pool", lambda e: e.memset(mixT[:, 4:6, :], 0.0), writes=[b_mix])
                        if 'sc' not in _kbr:
                            S.op("pool", lambda e: e.memset(mixT[:, 6:8, :], 0.0), writes=[b_mix])
                    if nxt is not None:
                        do_rope(*nxt)
                    for oc in range(8):
                        ps, bps = nps()
                        for kc in range(8):
                            S.op("pe", lambda e, ps=ps, oc=oc, kc=kc: e.matmul(ps[:, 0:nn], lhsT=Wh["o"][:, kc, oc * 128:(oc + 1) * 128], rhs=mixT[:, kc, :], start=(kc == 0), stop=(kc == 7)),
                                 reads=[B_W, b_mix], writes=[bps] if kc == 0 else (), joins=() if kc == 0 else [bps])
                        S.op("dve", lambda e, ps=ps, oc=oc: e.scalar_tensor_tensor(out=xt[:, oc, :], in0=ps[:, 0:nn], scalar=GT[:, l, s, oc:oc + 1], in1=xt[:, oc, :], op0=ALU.mult, op1=ALU.add),
                             reads=[bps, b_xt, B_mod], joins=[b_xt])
                    for c0 in range(0, 8, 4):
                        S.dma(lambda e, c0=c0: e.dma_start(out=xT[c0:c0 + 4, :, t0:t0 + nn].rearrange("c p t -> p c t"), in_=xt[:, c0:c0 + 4, :]), b_xt, reads=[b_xt], joins=[B_xT])
                tl = [(SEQS[si_][0], t0_, nn_) for si_ in sis for (t0_, nn_) in tiles_of(SEQS[si_][0], L)]
                loads(0, *tl[0])
                do_rope(0, *tl[0])
                for ti, (bs_, t0, nn) in enumerate(tl):
                    if ti + 1 < len(tl):
                        loads(ti + 1, *tl[ti + 1])
                    compute(ti, bs_, t0, nn, nxt=((ti + 1,) + tuple(tl[ti + 1])) if ti + 1 < len(tl) else None)
                    if _os_mod.environ.get('KBAR_C'):
                        S.barrier()
                phase_end(loc)

        def phase_F():
            with ExitStack() as st:
                xt = [sbt(st, "fxt%d" % i, [128, 8, 512], F32) for i in range(2)]
                yT = sbt(st, "fyT", [128, 8, 512], F32)
                yhi = sbt(st, "fyhi", [128, 8, 512], BF16)
                ylo = sbt(st, "fylo", [128, 8, 512], BF16)
                yo = [sbt(st, "fyo%d" % i, [128, 4, 1024], F32) for i in range(2)]
                tm = norm_tmps(st, "F", 512)
                ytmp = tm["tmpn"]
                b_xt = [S.buf("fxt0"), S.buf("fxt1")]
                b_yT, b_yhl = S.buf("fyT"), S.buf("fyhl")
                b_yo = [S.buf("fyo0"), S.buf("fyo1")]
                alltiles = []
                for (base, L, ctx) in SEQS:
                    alltiles += tiles_of(base, L)

                def ldx(i):
                    t0, n = alltiles[i]
                    sl = i % 2
                    for c0 in range(0, 8, 4):
                        S.dma(lambda e, c0=c0: e.dma_start(out=xt[sl][:, c0:c0 + 4, 0:n], in_=xT[c0:c0 + 4, :, t0:t0 + n].rearrange("c p t -> p c t")), b_xt[sl],
                              reads=[B_xT], writes=[b_xt[sl]] if c0 == 0 else (), joins=() if c0 == 0 else [b_xt[sl]])

                def do_tile(i):
                    t0, n = alltiles[i]
                    sl = i % 2
                    rms_norm(tm, xt[sl], b_xt[sl], n, FG, None, yT, b_yT)
                    S.op("act", lambda e: e.copy(out=yhi[:, :, 0:n], in_=yT[:, :, 0:n]), reads=[b_yT], writes=[b_yhl])
                    S.op("pool", lambda e: e.tensor_copy(out=ytmp[:, :, 0:n], in_=yhi[:, :, 0:n]), reads=[b_yhl], writes=[tm["b_tmpn"]])
                    S.op("pool", lambda e: e.tensor_tensor(out=ylo[:, :, 0:n], in0=yT[:, :, 0:n], in1=ytmp[:, :, 0:n], op=ALU.subtract), reads=[b_yT, b_yhl, tm["b_tmpn"]], joins=[b_yhl])
                    for bk in range(n // 128):
                        for half in range(2):
                            ps, bps = nps()
                            for c in range(4):
                                cc = half * 4 + c
                                S.op("pe", lambda e, ps=ps, c=c, cc=cc, bk=bk: e.matmul(ps[:, c * 128:(c + 1) * 128], lhsT=yhi[:, cc, bk * 128:(bk + 1) * 128], rhs=ident[:], start=True, stop=False), reads=[b_yhl, B_const], writes=[bps] if c == 0 else (), joins=() if c == 0 else [bps])
                                S.op("pe", lambda e, ps=ps, c=c, cc=cc, bk=bk: e.matmul(ps[:, c * 128:(c + 1) * 128], lhsT=ylo[:, cc, bk * 128:(bk + 1) * 128], rhs=ident[:], start=False, stop=True), reads=[b_yhl], joins=[bps])
                            wr = dict(writes=[b_yo[sl]]) if (bk == 0 and half == 0) else dict(joins=[b_yo[sl]])
                            if half == 0:
                                S.op("act", lambda e, ps=ps, bk=bk: e.copy(out=yo[sl][:, bk, 0:512], in_=ps[:]), reads=[bps], **wr)
                            else:
                                S.op("dve", lambda e, ps=ps, bk=bk: e.tensor_copy(out=yo[sl][:, bk, 512:1024], in_=ps[:]), reads=[bps], **wr)
                    S.dma(lambda e: e.dma_start(out=y_out[t0:t0 + n, :].rearrange("(b p) f -> p b f", p=128), in_=yo[sl][:, 0:n // 128, :]), b_yo[sl], reads=[b_yo[sl]], joins=[B_out])

                ldx(0)
                for i in range(len(alltiles)):
                    if i + 1 < len(alltiles):
                        ldx(i + 1)
                    do_tile(i)
                    if _os_mod.environ.get('KBAR_F'):
                        S.barrier()
                phase_end(b_xt + b_yo + [b_yT, b_yhl, tm["b_sq"], tm["b_rstd"], tm["b_tmpn"]])

        class _Stop(Exception):
            pass

        def chk(name):
            if stop_after == name:
                raise _Stop()
        try:
            chk("0")
            for l in range(depth):
                filter_phase(l, 4096)
                chk("f4096")
                filter_phase(l, 256)
                chk("f256")
                with ExitStack() as wstk:
                    Wh["o"] = sbt(wstk, "Wo", [128, 8, 1024], BF16)
                    with ExitStack() as wstk2:
                        Wh["b"] = sbt(wstk2, "Wb", [128, 8, NCH * 128], BF16)
                        Wh["kv"] = sbt(wstk2, "Wkv", [128, 8, 256], BF16)
                        load_weights(l)
                        chk("W")
                        phase_P(l)
                        chk("P")
                    phase_H(l, [0])
                    chk("H0")
                    phase_H(l, [1, 2, 3, 4])
                    chk("H1")
                    phase_C(l, [0])
                    chk("C0")
                    phase_C(l, [1, 2, 3, 4])
                    chk("C1")
            phase_F()
        except _Stop:
            pass
        S.barrier()
        S.emit()
    return nc


_CONST_CACHE = {}


def _consts():
    if not _CONST_CACHE:
        fw4096, iv4096 = dft_consts(4096)
        fw256, iv256 = dft_consts(256)
        fe4096, de4096 = filt_consts(4096)
        fe256, de256 = filt_consts(256)
        C, Sn, PM = rope_consts()
        _CONST_CACHE.update(dict(fw4096=fw4096, iv4096=iv4096, fw256=fw256, iv256=iv256, fe4096=fe4096, de4096=de4096,
                                 fe256=fe256, de256=de256, ropeC=C, ropeS=Sn, permm=PM))
    return _CONST_CACHE


def kernel(x_prompt, x_sample, cache_k, cache_v, c, c_ctx, norm_g, mod_w, mod_b, w_in, attn_sink,
           hy_conv_w, hy_filt_w1, hy_filt_b1, hy_filt_w2, hy_filt_b2, hy_filt_w3, hy_filt_freq, hy_d,
           sc_conv_w, w_out, final_g, _depth=4):
    f = lambda a: np.ascontiguousarray(np.asarray(a, dtype=np.float32))
    x_prompt, x_sample, cache_k, cache_v, c, c_ctx = map(f, (x_prompt, x_sample, cache_k, cache_v, c, c_ctx))
    cols, rows, dperm = col_perm()
    w_in = f(w_in)
    w_in_p = np.ascontiguousarray(w_in[:, :, cols])
    w_kv = np.ascontiguousarray(w_in[:, :, 512:768])
    w_out_p = np.ascontiguousarray(f(w_out)[:, rows, :])
    cst = _consts()
    shared = dict(norm_g=f(norm_g), mod_w=f(mod_w), mod_b=f(mod_b), w_in=w_in_p, w_kv=w_kv, sink=f(attn_sink),
                  hy_cw=f(hy_conv_w), f_w1=f(hy_filt_w1), f_b1=f(hy_filt_b1), f_w2=f(hy_filt_w2), f_b2=f(hy_filt_b2),
                  f_w3=f(hy_filt_w3), f_fq=f(hy_filt_freq), hy_d=f(hy_d), sc_cw=f(sc_conv_w), w_out=w_out_p,
                  final_g=f(final_g))
    shared.update(cst)
    in_maps = []
    for r in range(8):
        xin = np.concatenate([x_sample[r], x_prompt[4 * r:4 * r + 4].reshape(1024, 1024)], 0)
        ck = cache_k[r][:, :, :, dperm].reshape(4, 256, 128)
        cv = cache_v[r].reshape(4, 256, 128)
        cond = np.stack([c[r], c_ctx], 1)
        m = dict(x_tm=np.ascontiguousarray(xin), ck=np.ascontiguousarray(ck), cv=np.ascontiguousarray(cv), cond=np.ascontiguousarray(cond))
        m.update(shared)
        in_maps.append(m)
    import os
    nc = build(_depth, os.environ.get('KSTOP'))
    ncores = int(os.environ.get('KCORES', '8'))
    res = run_bass_kernel_spmd(nc, in_maps[:ncores], core_ids=list(range(ncores)))
    if ncores < 8:
        res.results.extend([res.results[0]] * (8 - ncores))
    y_prompt = np.zeros((32, 256, 1024), np.float32)
    y_sample = np.zeros((8, 4096, 1024), np.float32)
    nk = np.zeros((32, 4, 256, 2, 64), np.float32)
    nv = np.zeros((32, 4, 256, 2, 64), np.float32)
    for r in range(8):
        o = res.results[r]
        y_sample[r] = o["y"][0:4096]
        y_prompt[4 * r:4 * r + 4] = o["y"][4096:].reshape(4, 256, 1024)
        nk[4 * r:4 * r + 4] = o["nk"].reshape(4, 4, 256, 2, 64)
        nv[4 * r:4 * r + 4] = o["nv"].reshape(4, 4, 256, 2, 64)
    return (y_prompt, y_sample, nk, nv)
```
